# Optimizing a Trainium2 kernel written in Bass

```python
import math
import jax, jax.numpy as jnp
from jax import lax
import numpy as np

D_MODEL = 1024
BATCH = 8
SEQ = 2048
DEPTH = 2

HEAD_DIM = 64
A_HEADS = 8
A_KV_HEADS = 2
B_HEADS = 8
B_KV_HEADS = 2
N_BRANCHES = 2
D_FF = 4 * D_MODEL
GRID_W = 64
ROPE_THETA = 10000.0
Q_BLOCK = 128
WINDOW = 128
N_BUCKETS = 32
MAX_DISTANCE = 128
EPS = 1e-6
NEG_INF = -1e30

A_Q = A_HEADS * HEAD_DIM
A_KV = A_KV_HEADS * HEAD_DIM
B_Q = B_HEADS * HEAD_DIM
B_KV = B_KV_HEADS * HEAD_DIM
PROJ_COLS = A_Q + 2 * A_KV + B_Q + 2 * B_KV + N_BRANCHES * D_MODEL

kernel_name = "gated_hybrid_axial_rope_window_sink_encoder"


def rms_norm(x, g):
    xf = x.astype(jnp.float32)
    y = xf * lax.rsqrt(jnp.mean(xf * xf, axis=-1, keepdims=True) + EPS)
    return (y * g.astype(jnp.float32)).astype(x.dtype)


def axial_rope_tables(seq):
    rows = seq // GRID_W
    row = jnp.repeat(jnp.arange(rows, dtype=jnp.int32), GRID_W)
    col = jnp.tile(jnp.arange(GRID_W, dtype=jnp.int32), rows)
    n_freq = HEAD_DIM // 4
    inv_freq = ROPE_THETA ** (-jnp.arange(n_freq, dtype=jnp.float32) / n_freq)
    ang_row = row.astype(jnp.float32)[:, None] * inv_freq[None, :]
    ang_col = col.astype(jnp.float32)[:, None] * inv_freq[None, :]
    return (jnp.cos(ang_row), jnp.sin(ang_row), jnp.cos(ang_col), jnp.sin(ang_col))


def _rotate(x, cos, sin):
    n = x.shape[-1] // 2
    x1, x2 = x[..., :n], x[..., n:]
    c, s = cos[None, :, None, :], sin[None, :, None, :]
    return jnp.concatenate([x1 * c - x2 * s, x1 * s + x2 * c], axis=-1)


def apply_axial_rope(x, tables):
    cr, sr, cc, sc = tables
    xf = x.astype(jnp.float32)
    half = HEAD_DIM // 2
    out = jnp.concatenate([_rotate(xf[..., :half], cr, sr),
                           _rotate(xf[..., half:], cc, sc)], axis=-1)
    return out.astype(x.dtype)


def t5_bucket(rel):
    nb = N_BUCKETS // 2
    max_exact = nb // 2
    n = jnp.abs(rel)
    large = max_exact + (jnp.log(jnp.maximum(n, 1).astype(jnp.float32) / max_exact)
                         / math.log(MAX_DISTANCE / max_exact) * (nb - max_exact)).astype(jnp.int32)
    large = jnp.minimum(large, nb - 1)
    return jnp.where(rel > 0, nb, 0) + jnp.where(n < max_exact, n, large)


def global_attention(q, k, v):
    b, s, h, hd = q.shape
    kvh = k.shape[2]
    g = h // kvh
    nblk = s // Q_BLOCK
    scale = hd ** -0.5
    qb = q.reshape(b, nblk, Q_BLOCK, kvh, g, hd).transpose(1, 0, 2, 3, 4, 5)

    def one_block(q_blk):
        logits = jnp.einsum('bqkgd,bskd->bkgqs', q_blk, k).astype(jnp.float32) * scale
        p = jax.nn.softmax(logits, axis=-1)
        return jnp.einsum('bkgqs,bskd->bqkgd', p.astype(v.dtype), v)

    o = lax.map(one_block, qb)
    return o.transpose(1, 0, 2, 3, 4, 5).reshape(b, s, h * hd)


def window_attention(q, k, v, bias, valid, sink):
    b, s, h, hd = q.shape
    kvh = k.shape[2]
    g = h // kvh
    nblk = s // WINDOW
    scale = hd ** -0.5
    qb = q.reshape(b, nblk, WINDOW, kvh, g, hd)
    pad = ((0, 0), (WINDOW, WINDOW), (0, 0), (0, 0))
    kp = jnp.pad(k, pad).reshape(b, nblk + 2, WINDOW, kvh, hd)
    vp = jnp.pad(v, pad).reshape(b, nblk + 2, WINDOW, kvh, hd)
    kb = jnp.concatenate([kp[:, :-2], kp[:, 1:-1], kp[:, 2:]], axis=2)
    vb = jnp.concatenate([vp[:, :-2], vp[:, 1:-1], vp[:, 2:]], axis=2)
    logits = jnp.einsum('bnqkgd,bnskd->bnkgqs', qb, kb).astype(jnp.float32) * scale
    logits = logits + bias.reshape(kvh, g, WINDOW, 3 * WINDOW)[None, None]
    logits = jnp.where(valid[None, :, None, None], logits, NEG_INF)
    sink_l = sink.astype(jnp.float32).reshape(kvh, g)[None, None, :, :, None, None]
    m = jnp.maximum(jnp.max(logits, axis=-1, keepdims=True), sink_l)
    p = jnp.exp(logits - m)
    denom = jnp.sum(p, axis=-1, keepdims=True) + jnp.exp(sink_l - m)
    p = (p / denom).astype(v.dtype)
    o = jnp.einsum('bnkgqs,bnskd->bnqkgd', p, vb)
    return o.reshape(b, s, h * hd)


def setup_inputs(seed: int = 0) -> dict:
    key = jax.random.key(seed)
    ks = jax.random.split(key, 16)
    f32 = jnp.float32
    nrm = lambda k, shape, sc: jax.random.normal(k, shape, f32) * sc
    return {
        "x": nrm(ks[0], (BATCH, SEQ, D_MODEL), 1.0),
        "w_in": nrm(ks[1], (DEPTH, D_MODEL, PROJ_COLS), D_MODEL ** -0.5),
        "b_gate": nrm(ks[2], (DEPTH, N_BRANCHES * D_MODEL), 0.1),
        "qn_a": 1.0 + nrm(ks[3], (DEPTH, HEAD_DIM), 0.02),
        "kn_a": 1.0 + nrm(ks[4], (DEPTH, HEAD_DIM), 0.02),
        "qn_b": 1.0 + nrm(ks[5], (DEPTH, HEAD_DIM), 0.02),
        "kn_b": 1.0 + nrm(ks[6], (DEPTH, HEAD_DIM), 0.02),
        "w_o_a": nrm(ks[7], (DEPTH, A_Q, D_MODEL), A_Q ** -0.5),
        "w_o_b": nrm(ks[8], (DEPTH, B_Q, D_MODEL), B_Q ** -0.5),
        "w_out": nrm(ks[9], (DEPTH, D_MODEL, D_MODEL), D_MODEL ** -0.5),
        "sink_b": nrm(ks[10], (DEPTH, B_HEADS), 0.5),
        "rel_bias": nrm(ks[11], (N_BUCKETS, B_HEADS), 0.5),
        "norm_mix": 1.0 + nrm(ks[12], (DEPTH, D_MODEL), 0.02),
        "norm_mlp": 1.0 + nrm(ks[13], (DEPTH, D_MODEL), 0.02),
        "w_mlp1": nrm(ks[14], (DEPTH, D_MODEL, D_FF), D_MODEL ** -0.5),
        "w_mlp2": nrm(ks[15], (DEPTH, D_FF, D_MODEL), D_FF ** -0.5),
    }


def reference(x, w_in, b_gate, qn_a, kn_a, qn_b, kn_b, w_o_a, w_o_b, w_out,
              sink_b, rel_bias, norm_mix, norm_mlp, w_mlp1, w_mlp2):
    b, s, d = x.shape
    nblk = s // WINDOW
    rope = axial_rope_tables(s)

    r = jnp.arange(WINDOW, dtype=jnp.int32)[:, None]
    j = jnp.arange(3 * WINDOW, dtype=jnp.int32)[None, :]
    rel = j - WINDOW - r
    kpos = (jnp.arange(nblk, dtype=jnp.int32)[:, None, None] * WINDOW + j[None] - WINDOW)
    valid = (jnp.abs(rel)[None] <= WINDOW) & (kpos >= 0) & (kpos < s)
    bias_b = jnp.transpose(rel_bias.astype(jnp.float32)[t5_bucket(rel)], (2, 0, 1))

    splits = [A_Q, A_Q + A_KV, A_Q + 2 * A_KV,
              A_Q + 2 * A_KV + B_Q, A_Q + 2 * A_KV + B_Q + B_KV,
              A_Q + 2 * A_KV + B_Q + 2 * B_KV]

    for l in range(DEPTH):
        h = rms_norm(x, norm_mix[l])
        proj = jnp.einsum('bsd,dc->bsc', h, w_in[l])
        qa, ka, va, qb, kb, vb, gates = jnp.split(proj, splits, axis=-1)

        qa = apply_axial_rope(rms_norm(qa.reshape(b, s, A_HEADS, HEAD_DIM), qn_a[l]), rope)
        ka = apply_axial_rope(rms_norm(ka.reshape(b, s, A_KV_HEADS, HEAD_DIM), kn_a[l]), rope)
        va = va.reshape(b, s, A_KV_HEADS, HEAD_DIM)
        oa = global_attention(qa, ka, va)

        qb = rms_norm(qb.reshape(b, s, B_HEADS, HEAD_DIM), qn_b[l])
        kb = rms_norm(kb.reshape(b, s, B_KV_HEADS, HEAD_DIM), kn_b[l])
        vb = vb.reshape(b, s, B_KV_HEADS, HEAD_DIM)
        ob = window_attention(qb, kb, vb, bias_b, valid, sink_b[l])

        g = jax.nn.sigmoid((gates + b_gate[l]).astype(jnp.float32)).astype(x.dtype)
        g = g.reshape(b, s, N_BRANCHES, d)
        ya = jnp.einsum('bsc,cd->bsd', oa, w_o_a[l])
        yb = jnp.einsum('bsc,cd->bsd', ob, w_o_b[l])
        mixed = g[:, :, 0] * ya + g[:, :, 1] * yb
        x = x + jnp.einsum('bsd,de->bse', mixed, w_out[l])

        h = rms_norm(x, norm_mlp[l])
        u = jnp.square(jax.nn.relu(jnp.einsum('bsd,df->bsf', h, w_mlp1[l])))
        x = x + jnp.einsum('bsf,fd->bsd', u, w_mlp2[l])
    return x
```

```python
import math
from contextlib import ExitStack

import numpy as np
import concourse.bass as bass
import concourse.mybir as mybir
from concourse.bass_utils import run_bass_kernel_spmd

F32 = mybir.dt.float32
BF16 = mybir.dt.bfloat16
ALU = mybir.AluOpType
AF = mybir.ActivationFunctionType

S = 2048
D = 1024
NL = 2
NTG = 4
TG = 512
EPS = 1e-6
NSLOT = 4
NCL = 48
MASKVAL = -30000.0


class Prog:
    def __init__(self):
        self.ops = []
        self.last_w = {}
        self.readers = {}

    def add(self, eng, fn, reads=(), writes=(), dma=None, nostate=False):
        i = len(self.ops)
        raw = set()
        other = set()
        for t in reads:
            w = self.last_w.get(t)
            if w is not None:
                raw.add(w)
        for t in writes:
            w = self.last_w.get(t)
            if w is not None:
                other.add(w)
            for r in self.readers.get(t, ()):
                other.add(r)
        deps = set()
        for j in raw | other:
            oj = self.ops[j]
            if oj["dma"] is None and dma is None and oj["eng"] == eng:
                if eng == "pe":
                    continue
            deps.add(j)
        self.ops.append(dict(eng=eng, fn=fn, deps=sorted(deps), dma=dma, signal=False))
        if nostate:
            return i
        for t in writes:
            self.last_w[t] = i
            self.readers[t] = []
        for t in reads:
            self.readers.setdefault(t, []).append(i)
        return i

    def finalize(self):
        for op in self.ops:
            for j in op["deps"]:
                self.ops[j]["signal"] = True
        cnt = {}
        for op in self.ops:
            if op["dma"] is not None:
                k = "d_" + op["dma"]
                cnt[k] = cnt.get(k, 0) + 16
                op["sem"] = k
                op["val"] = cnt[k]
            elif op["signal"]:
                k = "e_" + op["eng"]
                cnt[k] = cnt.get(k, 0) + 1
                op["sem"] = k
                op["val"] = cnt[k]
        return sorted(cnt.keys())

    def emit(self, eng_name, eng, sems):
        waited = {}
        for op in self.ops:
            if op["eng"] != eng_name:
                continue
            need = {}
            for j in op["deps"]:
                oj = self.ops[j]
                need[oj["sem"]] = max(need.get(oj["sem"], 0), oj["val"])
            for k, v in need.items():
                if waited.get(k, 0) < v:
                    eng.wait_ge(sems[k], v)
                    waited[k] = v
            if op["fn"] is None:
                continue
            ins = op["fn"](eng)
            if op["dma"] is not None:
                ins.then_inc(sems[op["sem"]], 16)
            elif op["signal"]:
                ins.then_inc(sems[op["sem"]], 1)


def build(nlayers=NL, taps=()):
    nc = bass.Bass("TRN2", target_bir_lowering=False)
    P = Prog()

    def dram(name, shape, kind="ExternalInput"):
        return nc.dram_tensor(name, list(shape), F32, kind=kind).ap()

    x_d = dram("x", [S, D])
    w_in_d = dram("w_in", [NL, D, 3584])
    w_oa_d = dram("w_o_a", [NL, 512, D])
    w_ob_d = dram("w_o_b", [NL, 512, D])
    w_out_d = dram("w_out", [NL, D, D])
    w1_d = dram("w_mlp1", [NL, D, 4096])
    w2_d = dram("w_mlp2", [NL, 4096, D])
    cols_d = dram("cols", [128, NL * NCL])
    rope_d = dram("rope", [2, 128, S])
    bias_d = dram("biasT", [128, 8, 384])
    mask_d = dram("maskT", [128, 384])
    cst_d = dram("cst", [4, 128, 128])
    y_d = dram("y", [S, D], kind="ExternalOutput")

    stack = ExitStack()
    with stack:
        def sb(name, shape, dt):
            return stack.enter_context(nc.sbuf_tensor(name, list(shape), dt))

        XT = sb("XT", [128, 8, S], F32)
        HTs = sb("HTs", [128, 2, 8, TG], BF16)
        ATT = sb("ATT", [128, 10, S], BF16)
        OT = sb("OT", [128, 8, S], BF16)
        PT = sb("PT", [128, 6, TG], BF16)
        W = sb("W", [128, NSLOT, 4096], BF16)
        SCR = sb("SCR", [128, 7, TG], F32)
        ident = sb("ident", [128, 128], F32)
        cb = sb("cb", [128, 3, 128], BF16)
        colsr = sb("colsr", [128, NL * NCL], F32)
        colsx = sb("colsx", [128, NL * 9], F32)
        PS = stack.enter_context(nc.psum_tensor("ps", [128, 8, TG], F32))
        banks = [PS[:, i, :] for i in range(8)]

        ones_bf = cb[:, 0, :]
        blk_bf = cb[:, 1, :]
        perm_bf = cb[:, 2, :]

        KT = ATT[:, 4, :]
        KTd = ATT[:, 5:7, :]
        Vaug = ATT[:, 7:10, :].rearrange("p a b -> p (a b)").rearrange("p (t k d) -> p t k d", t=16, k=2, d=192)
        OTf = OT[:, 4:8, :].rearrange("p a b -> p (a b)").bitcast(F32)
        ropeC = OTf[:, 0:S]
        ropeS = OTf[:, S:2 * S]
        HTf = HTs[:].rearrange("p a k t -> p (a k t)").bitcast(F32)
        biasm = HTf[:, 0:8 * 384].rearrange("p (h q) -> p h q", h=8)

        def stage(s):
            return ATT[:, s, :].bitcast(F32)

        def ATTt(c):
            return [("ATT", c, tg) for tg in range(NTG)]

        def mm(out, lhsT, rhs, start, stop, reads, writes, tp=None):
            def fn(e):
                if tp is None:
                    return e.matmul(out, lhsT, rhs, start=start, stop=stop)
                return e.matmul(out, lhsT, rhs, start=start, stop=stop, tile_position=tp)
            P.add("pe", fn, reads, writes)

        def tr(out, in_, reads, writes):
            P.add("pe", lambda e: e.transpose(out, in_, ident[:]), reads, writes)

        def act(out, in_, func, reads, writes, bias=None, scale=None):
            def fn(e):
                kw = {}
                if bias is not None:
                    kw["bias"] = bias
                if scale is not None:
                    kw["scale"] = scale
                return e.activation(out=out, in_=in_, func=func, **kw)
            P.add("act", fn, reads, writes)

        def dve(method, reads, writes, **kw):
            P.add("dve", lambda e: getattr(e, method)(**kw), reads, writes)

        def dma(eng, out, in_, reads, writes, key):
            P.add(eng, lambda e: e.dma_start(out=out, in_=in_), reads, writes, dma=key)

        tapouts = {}

        def tap(name, ap, dt, shape, reads):
            if name not in taps:
                return
            d = nc.dram_tensor("tap_" + name, list(shape), dt, kind="ExternalOutput").ap()
            dma("sp", d, ap, reads, [("tap", name)], key="tap_" + name)
            tapouts[name] = True

        ringctr = {}
        SCR_POOLS = {"gen": [0, 1, 2, 3, 4, 5, 6], "rs": [0, 1], "q": [2, 3, 4], "t1": [5], "t2": [6]}
        PT_POOLS = {"norm": [0, 1, 2], "sq": [3, 4], "qb": [5]}

        def scr(pool="gen"):
            k = "scr_" + pool
            i = ringctr.get(k, 0)
            ringctr[k] = i + 1
            lst = SCR_POOLS[pool]
            return lst[i % len(lst)]

        def ptn(pool):
            k = "pt_" + pool
            i = ringctr.get(k, 0)
            ringctr[k] = i + 1
            lst = PT_POOLS[pool]
            return lst[i % len(lst)]

        bankctr = {}

        def bank(pool, lst):
            i = bankctr.get(pool, 0)
            bankctr[pool] = i + 1
            return lst[i % len(lst)]

        wplan = []

        def k8(cols):
            return lambda sl: sl.rearrange("p (k c) -> p k c", k=8)[:, :, 0:cols]

        def src_k8(mat, c0, cols):
            return mat[:, c0:c0 + cols].rearrange("(k p) c -> p k c", p=128)

        for l in range(nlayers):
            wplan.append([(k8(512), src_k8(w_in_d[l], 0, 512))])
            wplan.append([(k8(256), src_k8(w_in_d[l], 512, 256))])
            wplan.append([(k8(512), src_k8(w_in_d[l], 768, 512))])
            wplan.append([(k8(256), src_k8(w_in_d[l], 1280, 256))])
            for cg in range(2):
                wplan.append([(k8(512), src_k8(w_in_d[l], 1536 + cg * 512, 512))])
                wplan.append([(k8(512), src_k8(w_in_d[l], 2560 + cg * 512, 512))])
                wplan.append([
                    (lambda sl: sl[:, 0:2048].rearrange("p (k c) -> p k c", k=4),
                     w_oa_d[l][:, cg * 512:(cg + 1) * 512].rearrange("(k p) c -> p k c", p=128)),
                    (lambda sl: sl[:, 2048:4096].rearrange("p (k c) -> p k c", k=4),
                     w_ob_d[l][:, cg * 512:(cg + 1) * 512].rearrange("(k p) c -> p k c", p=128)),
                ])
            for cg in range(2):
                wplan.append([(k8(512), src_k8(w_out_d[l], cg * 512, 512))])
            for fg in range(4):
                wplan.append([(k8(512), src_k8(w1_d[l], fg * 1024, 512))])
                wplan.append([(k8(512), src_k8(w1_d[l], fg * 1024 + 512, 512))])
                for hf in range(2):
                    f0 = fg * 1024 + hf * 512
                    wplan.append([(lambda sl: sl.rearrange("p (k c) -> p k c", k=4),
                                   w2_d[l][f0:f0 + 512, :].rearrange("(k p) c -> p k c", p=128))])

        wst = {"next_load": 0, "released": 0, "next_use": 0}

        def w_pump():
            while wst["next_load"] < len(wplan) and wst["next_load"] < wst["released"] + NSLOT:
                n = wst["next_load"]
                s = n % NSLOT
                for (dstf, src) in wplan[n]:
                    dma("pool", dstf(W[:, s, :]), src, [], [("W", s)], key=f"w{s}")
                wst["next_load"] += 1

        def w_use():
            n = wst["next_use"]
            wst["next_use"] += 1
            assert n < wst["next_load"], "weight not loaded (ring too small for this group)"
            return n % NSLOT

        def w_release(k=1):
            wst["released"] += k
            w_pump()

        dma("sp", ident[:], cst_d[0], [], [("ident",)], key="ci")
        for i in range(3):
            dma("pool", cb[:, i, :], cst_d[1 + i], [], [("cb", i)], key=f"cb{i}")
        dma("sp", colsr[:], cols_d, [], [("colsr",)], key="cc")
        w_pump()

        def col(l, j):
            return colsr[:, l * NCL + j:l * NCL + j + 1]
        colsx2 = sb("colsx2", [128, NL], F32)
        epsc = sb("epsc", [128, 2], F32)
        dve("memset", [], [("epsc",)], ap=epsc[:, 0:1], constant=EPS)
        dve("memset", [], [("epsc",)], ap=epsc[:, 1:2], constant=64.0 * EPS)
        for l in range(nlayers):
            dve("tensor_scalar", [("colsr",)], [("colsx",)], out=colsx[:, l * 9:l * 9 + 1], in0=col(l, 33),
                scalar1=8.0, scalar2=None, op0=ALU.mult)
            dve("tensor_scalar", [("colsr",)], [("colsx",)], out=colsx2[:, l:l + 1], in0=col(l, 35),
                scalar1=8.0, scalar2=None, op0=ALU.mult)
            act(colsx[:, l * 9 + 1:l * 9 + 9], colsr[:, l * NCL + 36:l * NCL + 44], AF.Exp, [("colsr",)], [("colsx",)])

        for tg in range(NTG):
            for t in range(4):
                tile = tg * 4 + t
                s = tile % 8
                dma("sp", stage(s), x_d[tile * 128:(tile + 1) * 128, :], [], ATTt(s), key=f"x{s}")
            for c in range(8):
                bk = bank("p0", [0, 1, 2, 3])
                for t in range(4):
                    s = (tg * 4 + t) % 8
                    tr(banks[bk][:, t * 128:(t + 1) * 128], stage(s)[:, c * 128:(c + 1) * 128],
                       ATTt(s) + [("ident",)], [("ps", bk)])
                dst = XT[:, c, tg * TG:(tg + 1) * TG]
                if c % 2 == 0:
                    act(dst, banks[bk][:], AF.Copy, [], [("ps", bk), ("XT", c, tg)])
                else:
                    dve("tensor_copy", [], [("ps", bk), ("XT", c, tg)], out=dst, in_=banks[bk][:])

        def norm_tg(l, tg, gbase, dst_ap, dst_tile):
            tsl = slice(tg * TG, (tg + 1) * TG)
            bk = bank("nss", [6, 7])
            for c in range(8):
                i = ptn("norm")
                act(PT[:, i, :], XT[:, c, tsl], AF.Square, [("XT", c, tg)], [("PT", i)])
                mm(banks[bk][:], ones_bf, PT[:, i, :], c == 0, c == 7, [("PT", i), ("cb", 0)], [("ps", bk)])
            r = scr("rs")
            act(SCR[:, r, :], banks[bk][:], AF.Ln, [("epsc",)], [("scr", r), ("ps", bk)], bias=epsc[:, 0:1], scale=1.0 / D)
            act(SCR[:, r, :], SCR[:, r, :], AF.Exp, [("scr", r)], [("scr", r)], scale=-0.5)
            for c in range(8):
                dve("scalar_tensor_tensor", [("XT", c, tg), ("scr", r), ("colsr",)], [dst_tile(c)],
                    out=dst_ap(c), in0=XT[:, c, tsl], scalar=col(l, gbase + c), in1=SCR[:, r, :],
                    op0=ALU.mult, op1=ALU.mult)

        def ht_dst(hb):
            return (lambda c: HTs[:, hb, c, :]), (lambda c: ("HT", hb, c))

        def qk_unit(l, tg, hb, slot, wc0, gain_ap, rope, dst_ap, dst_tiles, pend):
            tsl = slice(tg * TG, (tg + 1) * TG)
            wv = W[:, slot, :].rearrange("p (k c) -> p k c", k=8)
            bk = bank("main", [0, 1, 2])
            for k in range(8):
                mm(banks[bk][:], wv[:, k, wc0:wc0 + 128], HTs[:, hb, k, :], k == 0, k == 7,
                   [("W", slot), ("HT", hb, k)], [("ps", bk)])
            i = ptn("sq")
            act(PT[:, i, :], banks[bk][:], AF.Square, [], [("ps", bk), ("PT", i)])

            def stage2():
                sb_ = bank("hss", [3, 4])
                mm(banks[sb_][:], blk_bf, PT[:, i, :], True, True, [("PT", i), ("cb", 1)], [("ps", sb_)])
                r = scr("rs")
                act(SCR[:, r, :], banks[sb_][:], AF.Ln, [("epsc",)], [("scr", r), ("ps", sb_)], bias=epsc[:, 1:2], scale=1.0)
                act(SCR[:, r, :], SCR[:, r, :], AF.Exp, [("scr", r)], [("scr", r)], scale=-0.5)
                if not rope:
                    dve("scalar_tensor_tensor", [("scr", r), ("colsr",), ("colsx",)], [("ps", bk)] + dst_tiles,
                        out=dst_ap, in0=banks[bk][:], scalar=gain_ap, in1=SCR[:, r, :], op0=ALU.mult, op1=ALU.mult)
                    return None
                q = scr("q")
                dve("scalar_tensor_tensor", [("scr", r), ("colsr",), ("colsx",)], [("ps", bk), ("scr", q)],
                    out=SCR[:, q, :], in0=banks[bk][:], scalar=gain_ap, in1=SCR[:, r, :], op0=ALU.mult, op1=ALU.mult)
                j = ptn("qb")
                act(PT[:, j, :], SCR[:, q, :], AF.Copy, [("scr", q)], [("PT", j)])

                def stage3():
                    pb = bank("perm", [5])
                    mm(banks[pb][:], perm_bf, PT[:, j, :], True, True, [("PT", j), ("cb", 2)], [("ps", pb)])
                    t1 = scr("t1")
                    dve("tensor_tensor", [("scr", q), ("rope",)], [("scr", t1)], out=SCR[:, t1, :], in0=SCR[:, q, :],
                        in1=ropeC[:, tsl], op=ALU.mult)
                    t2 = scr("t2")
                    dve("tensor_tensor", [("rope",)], [("scr", t2), ("ps", pb)], out=SCR[:, t2, :], in0=banks[pb][:],
                        in1=ropeS[:, tsl], op=ALU.mult)
                    dve("tensor_tensor", [("scr", t1), ("scr", t2)], dst_tiles, out=dst_ap, in0=SCR[:, t1, :],
                        in1=SCR[:, t2, :], op=ALU.add)
                    return None
                return stage3
            pend.append(stage2)

        def run_pend(pend, keep):
            n = len(pend) - keep
            if n <= 0:
                return
            todo = pend[:n]
            del pend[:n]
            newp = []
            for f in todo:
                nxt = f()
                if nxt is not None:
                    newp.append(nxt)
            pend[:0] = newp

        def phase1(l, mixer):
            rope = (mixer == 0)
            if rope:
                dma("sp", ropeC, rope_d[0], [], [("rope",)] + [("OT", c, tg) for c in (4, 5) for tg in range(NTG)], key="rp0")
                dma("sp", ropeS, rope_d[1], [], [("rope",)] + [("OT", c, tg) for c in (6, 7) for tg in range(NTG)], key="rp1")
            sq_ = w_use()
            skv = w_use()
            qg = col(l, 32 if mixer == 0 else 34)
            kg = colsx[:, l * 9:l * 9 + 1] if mixer == 0 else colsx2[:, l:l + 1]
            dve("memset", [], [("V", t) for t in range(16)] + [x_ for c in (7, 8, 9) for x_ in ATTt(c)],
                ap=Vaug[:, :, :, 64:128], constant=1.0)
            pend = []
            da0, dt0 = ht_dst(0)
            norm_tg(l, 0, 0, da0, dt0)
            for tg in range(NTG):
                hb = tg % 2
                if tg + 1 < NTG:
                    da, dt_ = ht_dst((tg + 1) % 2)
                    norm_tg(l, tg + 1, 0, da, dt_)
                tsl = slice(tg * TG, (tg + 1) * TG)
                for c in range(4):
                    qk_unit(l, tg, hb, sq_, c * 128, qg, rope, ATT[:, c, tsl], [("ATT", c, tg)], pend)
                    run_pend(pend, 1)
                qk_unit(l, tg, hb, skv, 0, kg, rope, KT[:, tsl], [("ATT", 4, tg)], pend)
                run_pend(pend, 1)
                wv = W[:, skv, :].rearrange("p (k c) -> p k c", k=8)
                vb = bank("vb", [5])
                for t in range(4):
                    for k in range(8):
                        mm(banks[vb][:, t * 128:(t + 1) * 128], HTs[:, hb, k, t * 128:(t + 1) * 128], wv[:, k, 128:256],
                           k == 0, k == 7, [("W", skv), ("HT", hb, k)], [("ps", vb)])
                src = banks[vb][:].rearrange("p (t k d) -> p t k d", t=4, k=2, d=64)
                vt = [("V", tg * 4 + t) for t in range(4)]
                act(Vaug[:, tg * 4:tg * 4 + 4, :, 0:64], src, AF.Copy, [], [("ps", vb)] + vt)
                act(Vaug[:, tg * 4:tg * 4 + 4, :, 128:192], src, AF.Copy, [], [("ps", vb)] + vt)
            while pend:
                run_pend(pend, 0)
            w_release(2)
            allk = ATTt(4)
            for kv in range(2):
                for half in range(2):
                    dma("sp", KTd[half * 64:(half + 1) * 64, kv, :], KT[kv * 64:(kv + 1) * 64, :],
                        allk, [("KTd", kv, half)] + ATTt(5 + kv), key=f"kd{kv}{half}")

        def attn_global(l):
            steps = []
            u = 0
            for kv in range(2):
                for tg in range(NTG):
                    for pr in range(2):
                        for sbk in range(16):
                            steps.append((u, kv, tg, pr, sbk))
                        u += 1
            pend = []

            def do_pv(st, s):
                (u, kv, tg, pr, sbk) = st
                c = kv * 2 + pr
                tsl = slice(tg * TG, (tg + 1) * TG)
                for hh in range(2):
                    ab = 4 + 2 * (u % 2) + hh
                    pt = (s % 3) * 2 + hh
                    lhsT = Vaug[:, sbk, kv, 0:128] if hh == 0 else Vaug[:, sbk, kv, 64:192]
                    mm(banks[ab][:], lhsT, PT[:, pt, :], sbk == 0, sbk == 15, [("PT", pt), ("V", sbk)], [("ps", ab)])
                    if sbk == 15:
                        orow = slice(hh * 64, hh * 64 + 64)
                        drow = slice(64 - hh * 64, 128 - hh * 64)
                        r = scr()
                        dve("reciprocal", [], [("scr", r), ("ps", ab)], out=SCR[orow, r, :], in_=banks[ab][drow, :])
                        dve("tensor_tensor", [("scr", r)], [("ps", ab), ("OT", c, tg)], out=OT[orow, c, tsl],
                            in0=banks[ab][orow, :], in1=SCR[orow, r, :], op=ALU.mult)

            for s, st in enumerate(steps):
                (u, kv, tg, pr, sbk) = st
                c = kv * 2 + pr
                for hh in range(2):
                    lt = (s % 2) * 2 + hh
                    rows = slice(hh * 64, hh * 64 + 64)
                    mm(banks[lt][:], KTd[rows, kv, sbk * 128:(sbk + 1) * 128], ATT[rows, c, tg * TG:(tg + 1) * TG],
                       True, True, [("ATT", c, tg), ("KTd", kv, hh)], [("ps", lt)], tp=(hh * 64, 0))
                lt0 = (s % 2) * 2
                pt0 = (s % 3) * 2
                act(PT[:, pt0:pt0 + 2, :], PS[:, lt0:lt0 + 2, :], AF.Exp, [],
                    [("ps", lt0), ("ps", lt0 + 1), ("PT", pt0), ("PT", pt0 + 1)])
                pend.append((st, s))
                if len(pend) > 1:
                    do_pv(*pend.pop(0))
            while pend:
                do_pv(*pend.pop(0))

        def attn_window(l):
            allht = [("HT", hb, c) for hb in range(2) for c in range(8)]
            dma("sp", biasm, bias_d, [], allht + [("biasm",)], key="bm0")
            mi = scr()
            dma("sp", SCR[:, mi, 0:384], mask_d, [], [("scr", mi)], key="bm1")
            for h in range(8):
                dve("tensor_tensor", [("scr", mi), ("biasm",)], [("biasm", h)] + ([("biasm",)] if h == 7 else []),
                    out=biasm[:, h, :], in0=biasm[:, h, :], in1=SCR[:, mi, 0:384], op=ALU.add)
            PTB = PT[:].rearrange("p a b -> p (a b)")[:, 0:5 * 384].rearrange("p (r q) -> p r q", r=5)
            allpt = [("PT", i) for i in range(6)]
            allptb = [("PTB", i) for i in range(5)]
            P.add("act", None, [], allpt, nostate=True)
            normq = []
            bstep = 0

            def emit_norm(ab, tg, h, c, orow, drow):
                tsl = slice(tg * TG, (tg + 1) * TG)
                r1 = scr()
                act(SCR[drow, r1, :], banks[ab][drow, :], AF.Ln, [("colsx",)], [("scr", r1), ("ps", ab)],
                    bias=colsx[drow, l * 9 + 1 + h:l * 9 + 2 + h], scale=1.0)
                act(SCR[drow, r1, :], SCR[drow, r1, :], AF.Exp, [("scr", r1)], [("scr", r1)], scale=-1.0)
                r2 = scr()
                dve("tensor_copy", [("scr", r1)], [("scr", r2)], out=SCR[orow, r2, :], in_=SCR[drow, r1, :])
                dve("tensor_tensor", [("scr", r2)], [("ps", ab), ("OT", c, tg)], out=OT[orow, c, tsl],
                    in0=banks[ab][orow, :], in1=SCR[orow, r2, :], op=ALU.mult)

            gstep = 0
            for h in range(8):
                c = 4 + h // 2
                hh = h % 2
                kv = h // 4
                rows = slice(hh * 64, hh * 64 + 64)
                orow = rows
                drow = slice(64 - hh * 64, 128 - hh * 64)
                vsl = slice(0, 128) if hh == 0 else slice(64, 192)
                for j in range(18):
                    if j < 16:
                        lo = max(j - 1, 0)
                        hi = min(j + 1, 15)
                        w = (hi - lo + 1) * 128
                        off = (lo - (j - 1)) * 128
                        lt = bank("bl", [0, 1, 2])
                        tgs = sorted(set((b * 128) // TG for b in range(lo, hi + 1)))
                        mm(banks[lt][:, 0:w], KTd[rows, kv, j * 128:(j + 1) * 128], ATT[rows, c - 4, lo * 128:(hi + 1) * 128],
                           True, True, [("ATT", c - 4, t_) for t_ in tgs] + [("KTd", kv, hh)], [("ps", lt)], tp=(hh * 64, 0))
                        tm = scr()
                        dve("tensor_tensor", [("biasm", h)], [("scr", tm), ("ps", lt)], out=SCR[:, tm, 0:w],
                            in0=banks[lt][:, 0:w], in1=biasm[:, h, off:off + w], op=ALU.add)
                        act(PTB[:, j % 5, off:off + w], SCR[:, tm, 0:w], AF.Exp, [("scr", tm)], [("PTB", j % 5)])
                    i = j - 2
                    if i >= 0:
                        ab = 4 + (gstep // 4) % 4
                        gstep += 1
                        contrib = [jj for jj in (i - 1, i, i + 1) if 0 <= jj < 16]
                        for n_, jj in enumerate(contrib):
                            b = i - jj + 1
                            mm(banks[ab][:, (i % 4) * 128:(i % 4 + 1) * 128], Vaug[:, jj, kv, vsl],
                               PTB[:, jj % 5, b * 128:(b + 1) * 128], n_ == 0, n_ == len(contrib) - 1,
                               [("PTB", jj % 5), ("V", jj)], [("ps", ab)])
                        if i % 4 == 3:
                            normq.append((bstep + 2, ab, i // 4, h, c, orow, drow))
                    bstep += 1
                    while normq and normq[0][0] <= bstep:
                        emit_norm(*normq.pop(0)[1:])
            while normq:
                emit_norm(*normq.pop(0)[1:])
            P.add("act", None, [], allptb, nostate=True)


        def phase3(l):
            for cg in range(2):
                sga = w_use()
                sgb = w_use()
                swo = w_use()
                wga = W[:, sga, :].rearrange("p (k c) -> p k c", k=8)
                wgb = W[:, sgb, :].rearrange("p (k c) -> p k c", k=8)
                woa = W[:, swo, 0:2048].rearrange("p (k c) -> p k c", k=4)
                wob = W[:, swo, 2048:4096].rearrange("p (k c) -> p k c", k=4)
                if cg == 0:
                    da0, dt0 = ht_dst(0)
                    norm_tg(l, 0, 0, da0, dt0)
                for tg in range(NTG):
                    hb = tg % 2
                    if not (cg == 1 and tg == NTG - 1):
                        ntg = (tg + 1) % NTG
                        da, dt_ = ht_dst(ntg % 2)
                        norm_tg(l, ntg, 0, da, dt_)
                    tsl = slice(tg * TG, (tg + 1) * TG)
                    for cc in range(4):
                        c = cg * 4 + cc
                        bga = bank("p3a", [0, 1])
                        bgb = bank("p3b", [2, 3])
                        bya = bank("p3c", [4])
                        byb = bank("p3d", [5])
                        for k in range(8):
                            mm(banks[bga][:], wga[:, k, cc * 128:(cc + 1) * 128], HTs[:, hb, k, :], k == 0, k == 7,
                               [("W", sga), ("HT", hb, k)], [("ps", bga)])
                        for k in range(8):
                            mm(banks[bgb][:], wgb[:, k, cc * 128:(cc + 1) * 128], HTs[:, hb, k, :], k == 0, k == 7,
                               [("W", sgb), ("HT", hb, k)], [("ps", bgb)])
                        for k in range(4):
                            mm(banks[bya][:], woa[:, k, cc * 128:(cc + 1) * 128], OT[:, k, tsl], k == 0, k == 3,
                               [("W", swo), ("OT", k, tg)], [("ps", bya)])
                        for k in range(4):
                            mm(banks[byb][:], wob[:, k, cc * 128:(cc + 1) * 128], OT[:, 4 + k, tsl], k == 0, k == 3,
                               [("W", swo), ("OT", 4 + k, tg)], [("ps", byb)])
                        ra = scr()
                        act(SCR[:, ra, :], banks[bga][:], AF.Sigmoid, [("colsr",)], [("ps", bga), ("scr", ra)],
                            bias=col(l, 16 + c))
                        rb = scr()
                        act(SCR[:, rb, :], banks[bgb][:], AF.Sigmoid, [("colsr",)], [("ps", bgb), ("scr", rb)],
                            bias=col(l, 24 + c))
                        dve("tensor_tensor", [("scr", ra)], [("scr", ra), ("ps", bya)], out=SCR[:, ra, :],
                            in0=banks[bya][:], in1=SCR[:, ra, :], op=ALU.mult)
                        dve("tensor_tensor", [("scr", rb)], [("scr", rb), ("ps", byb)], out=SCR[:, rb, :],
                            in0=banks[byb][:], in1=SCR[:, rb, :], op=ALU.mult)
                        dve("tensor_tensor", [("scr", ra), ("scr", rb)], [("ATT", c, tg)], out=ATT[:, c, tsl],
                            in0=SCR[:, ra, :], in1=SCR[:, rb, :], op=ALU.add)
                w_release(3)
            for cg in range(2):
                so = w_use()
                wo = W[:, so, :].rearrange("p (k c) -> p k c", k=8)
                for cc in range(4):
                    c = cg * 4 + cc
                    for tg in range(NTG):
                        tsl = slice(tg * TG, (tg + 1) * TG)
                        bk = bank("p3o", [0, 1, 2, 3])
                        for k in range(8):
                            mm(banks[bk][:], wo[:, k, cc * 128:(cc + 1) * 128], ATT[:, k, tsl], k == 0, k == 7,
                               [("W", so), ("ATT", k, tg)], [("ps", bk)])
                        dve("tensor_tensor", [("XT", c, tg)], [("ps", bk), ("XT", c, tg)], out=XT[:, c, tsl],
                            in0=banks[bk][:], in1=XT[:, c, tsl], op=ALU.add)
                w_release(1)

        def phase4(l):
            for tg in range(NTG):
                tsl = slice(tg * TG, (tg + 1) * TG)
                norm_tg(l, tg, 8, (lambda c, tsl=tsl: ATT[:, c, tsl]), (lambda c, tg=tg: ("ATT", c, tg)))
            for fg in range(4):
                s1 = [w_use(), w_use()]
                s2 = [w_use(), w_use()]
                for fc in range(8):
                    w1 = W[:, s1[fc // 4], :].rearrange("p (k c) -> p k c", k=8)
                    f0 = (fc % 4) * 128
                    for tg in range(NTG):
                        tsl = slice(tg * TG, (tg + 1) * TG)
                        bk = bank("p4u", [0, 1, 2, 3])
                        for k in range(8):
                            mm(banks[bk][:], w1[:, k, f0:f0 + 128], ATT[:, k, tsl], k == 0, k == 7,
                               [("W", s1[fc // 4]), ("ATT", k, tg)], [("ps", bk)])
                        r = scr()
                        act(SCR[:, r, :], banks[bk][:], AF.Square, [], [("ps", bk), ("scr", r)])
                        dve("scalar_tensor_tensor", [("scr", r)], [("ps", bk), ("OT", fc, tg)], out=OT[:, fc, tsl],
                            in0=banks[bk][:], scalar=0.0, in1=SCR[:, r, :], op0=ALU.is_gt, op1=ALU.mult)
                w_release(2)
                for c in range(8):
                    for tg in range(NTG):
                        tsl = slice(tg * TG, (tg + 1) * TG)
                        bk = bank("p4d", [4, 5, 6, 7])
                        for fc in range(8):
                            w2 = W[:, s2[fc // 4], :].rearrange("p (k c) -> p k c", k=4)
                            mm(banks[bk][:], w2[:, fc % 4, c * 128:(c + 1) * 128], OT[:, fc, tsl], fc == 0, fc == 7,
                               [("W", s2[fc // 4]), ("OT", fc, tg)], [("ps", bk)])
                        dve("tensor_tensor", [("XT", c, tg)], [("ps", bk), ("XT", c, tg)], out=XT[:, c, tsl],
                            in0=banks[bk][:], in1=XT[:, c, tsl], op=ALU.add)
                w_release(2)

        allxt = [("XT", c, tg) for c in range(8) for tg in range(NTG)]
        tap("XT0", XT[:], F32, [128, 8, S], allxt)
        for l in range(nlayers):
            phase1(l, 0)
            if l == 0:
                tap("QA", ATT[:, 0:4, :], BF16, [128, 4, S], [t_ for c in range(4) for t_ in ATTt(c)])
                tap("KA", ATT[:, 4:7, :], BF16, [128, 3, S], [t_ for c in (4, 5, 6) for t_ in ATTt(c)] + [("KTd", kv, h) for kv in range(2) for h in range(2)])
                tap("VA", ATT[:, 7:10, :], BF16, [128, 3, S], [("V", t) for t in range(16)])
            attn_global(l)
            if l == 0:
                tap("OA", OT[:, 0:4, :], BF16, [128, 4, S], [("OT", c, tg) for c in range(4) for tg in range(NTG)])
            phase1(l, 1)
            if l == 0:
                tap("QB", ATT[:, 0:4, :], BF16, [128, 4, S], [t_ for c in range(4) for t_ in ATTt(c)])
                tap("KB", ATT[:, 4:7, :], BF16, [128, 3, S], [t_ for c in (4, 5, 6) for t_ in ATTt(c)] + [("KTd", kv, h) for kv in range(2) for h in range(2)])
            attn_window(l)
            if l == 0:
                tap("OB", OT[:, 4:8, :], BF16, [128, 4, S], [("OT", c, tg) for c in range(4, 8) for tg in range(NTG)])
            phase3(l)
            if l == 0:
                tap("MIX", ATT[:, 0:8, :], BF16, [128, 8, S], [t_ for c in range(8) for t_ in ATTt(c)])
                tap("XT1", XT[:], F32, [128, 8, S], allxt)
            phase4(l)
            if l == 0:
                tap("XT2", XT[:], F32, [128, 8, S], allxt)

        outs = []
        for t in range(16):
            s = t % 8
            for half in range(2):
                bk = bank("po", [0, 1, 2, 3])
                for cc in range(4):
                    c = half * 4 + cc
                    tr(banks[bk][:, cc * 128:(cc + 1) * 128], XT[:, c, t * 128:(t + 1) * 128],
                       [("XT", c, t // 4), ("ident",)], [("ps", bk)])
                dst = stage(s)[:, half * 512:(half + 1) * 512]
                if half == 0:
                    act(dst, banks[bk][:], AF.Copy, [], [("ps", bk), ("ostg", s, half)] + ATTt(s))
                else:
                    dve("tensor_copy", [], [("ps", bk), ("ostg", s, half)] + ATTt(s), out=dst, in_=banks[bk][:])
            dma("sp", y_d[t * 128:(t + 1) * 128, :], stage(s), [("ostg", s, 0), ("ostg", s, 1)] + ATTt(s),
                [("y", t)], key=f"y{s}")
        P.add("sp", None, [("y", t) for t in range(16)] + [("tap", n_) for n_ in tapouts], [])

        sem_names = P.finalize()
        sems = {k: stack.enter_context(nc.semaphore(k)) for k in sem_names}
        with nc.Block() as block:
            @block.tensor
            def _(e):
                P.emit("pe", e, sems)

            @block.scalar
            def _(e):
                P.emit("act", e, sems)

            @block.vector
            def _(e):
                P.emit("dve", e, sems)

            @block.gpsimd
            def _(e):
                P.emit("pool", e, sems)

            @block.sync
            def _(e):
                P.emit("sp", e, sems)
    return nc


def _t5_bucket(rel):
    nb = 16
    max_exact = 8
    n = np.abs(rel)
    large = max_exact + (np.log(np.maximum(n, 1).astype(np.float32) / max_exact)
                         / math.log(128 / max_exact) * (nb - max_exact)).astype(np.int32)
    large = np.minimum(large, nb - 1)
    return np.where(rel > 0, nb, 0) + np.where(n < max_exact, n, large)


def _host_tables(inputs):
    f32 = np.float32
    cols = np.zeros((128, NL * NCL), f32)
    p = np.arange(128)
    for l in range(NL):
        b = l * NCL
        cols[:, b + 0:b + 8] = np.asarray(inputs["norm_mix"][l], f32).reshape(8, 128).T
        cols[:, b + 8:b + 16] = np.asarray(inputs["norm_mlp"][l], f32).reshape(8, 128).T
        cols[:, b + 16:b + 32] = np.asarray(inputs["b_gate"][l], f32).reshape(16, 128).T
        cols[:, b + 32] = np.asarray(inputs["qn_a"][l], f32)[p % 64]
        cols[:, b + 33] = np.asarray(inputs["kn_a"][l], f32)[p % 64]
        cols[:, b + 34] = np.asarray(inputs["qn_b"][l], f32)[p % 64]
        cols[:, b + 35] = np.asarray(inputs["kn_b"][l], f32)[p % 64]
        cols[:, b + 36:b + 44] = np.asarray(inputs["sink_b"][l], f32)[None, :]
    t = np.arange(S)
    row = (t // 64).astype(f32)
    colp = (t % 64).astype(f32)
    inv_freq = (10000.0 ** (-np.arange(16, dtype=f32) / 16)).astype(f32)
    d = p % 64
    half = d // 32
    j = d % 32
    fidx = j % 16
    pos = np.where(half[:, None] == 0, row[None, :], colp[None, :]).astype(f32)
    ang = (pos * inv_freq[fidx][:, None]).astype(f32)
    C = np.cos(ang).astype(f32)
    Sg = np.sin(ang).astype(f32)
    Sp = np.where((j < 16)[:, None], -Sg, Sg).astype(f32)
    rope = np.stack([C, Sp]).astype(f32)
    s_ = np.arange(128)[:, None]
    q_ = np.arange(384)[None, :]
    rel = s_ + 128 - q_
    bucket = _t5_bucket(rel)
    rb = np.asarray(inputs["rel_bias"], f32)
    biasT = np.ascontiguousarray(np.transpose(rb[bucket], (0, 2, 1))).astype(f32)
    maskT = np.where(np.abs(rel) <= 128, 0.0, MASKVAL).astype(f32)
    ident = np.eye(128, dtype=f32)
    ones = np.ones((128, 128), f32)
    blk = np.zeros((128, 128), f32)
    blk[:64, :64] = 1.0
    blk[64:, 64:] = 1.0
    perm = np.zeros((128, 128), f32)
    m = np.arange(128)
    pm = np.where((m % 32) < 16, m + 16, m - 16)
    perm[pm, m] = 1.0
    cst = np.stack([ident, ones, blk, perm]).astype(f32)
    return cols, rope, biasT, maskT, cst


_NC_CACHE = {}


def kernel(**inputs):
    f32 = np.float32
    x = np.asarray(inputs["x"], f32)
    cols, rope, biasT, maskT, cst = _host_tables(inputs)
    shared = {
        "w_in": np.ascontiguousarray(np.asarray(inputs["w_in"], f32)),
        "w_o_a": np.ascontiguousarray(np.asarray(inputs["w_o_a"], f32)),
        "w_o_b": np.ascontiguousarray(np.asarray(inputs["w_o_b"], f32)),
        "w_out": np.ascontiguousarray(np.asarray(inputs["w_out"], f32)),
        "w_mlp1": np.ascontiguousarray(np.asarray(inputs["w_mlp1"], f32)),
        "w_mlp2": np.ascontiguousarray(np.asarray(inputs["w_mlp2"], f32)),
        "cols": cols, "rope": rope, "biasT": biasT, "maskT": maskT, "cst": cst,
    }
    if "nc" not in _NC_CACHE:
        _NC_CACHE["nc"] = build(NL)
    nc = _NC_CACHE["nc"]
    in_maps = []
    for b in range(8):
        m = dict(shared)
        m["x"] = np.ascontiguousarray(x[b])
        in_maps.append(m)
    res = run_bass_kernel_spmd(nc, in_maps, core_ids=list(range(8)))
    out = np.stack([np.asarray(r["y"], f32) for r in res.results], axis=0)
    return out
```

```python
import math
from contextlib import ExitStack

import numpy as np
import concourse.bass as bass
import concourse.mybir as mybir
from concourse.bass_utils import run_bass_kernel_spmd

F32 = mybir.dt.float32
BF16 = mybir.dt.bfloat16
ALU = mybir.AluOpType
AF = mybir.ActivationFunctionType

S = 2048
D = 1024
NL = 2
NTG = 4
TG = 512
EPS = 1e-6
NSLOT = 4
NCL = 48
MASKVAL = -30000.0


class Prog:
    def __init__(self):
        self.ops = []
        self.last_w = {}
        self.readers = {}

    def add(self, eng, fn, reads=(), writes=(), dma=None, nostate=False):
        i = len(self.ops)
        raw = set()
        other = set()
        for t in reads:
            w = self.last_w.get(t)
            if w is not None:
                raw.add(w)
        for t in writes:
            w = self.last_w.get(t)
            if w is not None:
                other.add(w)
            for r in self.readers.get(t, ()):
                other.add(r)
        deps = set()
        for j in raw | other:
            oj = self.ops[j]
            if oj["dma"] is None and dma is None and oj["eng"] == eng:
                if eng == "pe":
                    continue
            deps.add(j)
        self.ops.append(dict(eng=eng, fn=fn, deps=sorted(deps), dma=dma, signal=False))
        if nostate:
            return i
        for t in writes:
            self.last_w[t] = i
            self.readers[t] = []
        for t in reads:
            self.readers.setdefault(t, []).append(i)
        return i

    def finalize(self):
        for op in self.ops:
            for j in op["deps"]:
                self.ops[j]["signal"] = True
        cnt = {}
        for op in self.ops:
            if op["dma"] is not None:
                k = "d_" + op["dma"]
                cnt[k] = cnt.get(k, 0) + 16
                op["sem"] = k
                op["val"] = cnt[k]
            elif op["signal"]:
                k = "e_" + op["eng"]
                cnt[k] = cnt.get(k, 0) + 1
                op["sem"] = k
                op["val"] = cnt[k]
        return sorted(cnt.keys())

    def emit(self, eng_name, eng, sems):
        waited = {}
        for op in self.ops:
            if op["eng"] != eng_name:
                continue
            need = {}
            for j in op["deps"]:
                oj = self.ops[j]
                need[oj["sem"]] = max(need.get(oj["sem"], 0), oj["val"])
            for k, v in need.items():
                if waited.get(k, 0) < v:
                    eng.wait_ge(sems[k], v)
                    waited[k] = v
            if op["fn"] is None:
                continue
            ins = op["fn"](eng)
            if op["dma"] is not None:
                ins.then_inc(sems[op["sem"]], 16)
            elif op["signal"]:
                ins.then_inc(sems[op["sem"]], 1)


def build(nlayers=NL, taps=()):
    nc = bass.Bass("TRN2", target_bir_lowering=False)
    P = Prog()

    def dram(name, shape, kind="ExternalInput"):
        return nc.dram_tensor(name, list(shape), F32, kind=kind).ap()

    x_d = dram("x", [S, D])
    w_in_d = dram("w_in", [NL, D, 3584])
    w_oa_d = dram("w_o_a", [NL, 512, D])
    w_ob_d = dram("w_o_b", [NL, 512, D])
    w_out_d = dram("w_out", [NL, D, D])
    w1_d = dram("w_mlp1", [NL, D, 4096])
    w2_d = dram("w_mlp2", [NL, 4096, D])
    cols_d = dram("cols", [128, NL * NCL])
    rope_d = dram("rope", [2, 128, S])
    bias_d = dram("biasT", [128, 8, 384])
    mask_d = dram("maskT", [128, 384])
    cst_d = dram("cst", [4, 128, 128])
    y_d = dram("y", [S, D], kind="ExternalOutput")

    stack = ExitStack()
    with stack:
        def sb(name, shape, dt):
            return stack.enter_context(nc.sbuf_tensor(name, list(shape), dt))

        XT = sb("XT", [128, 8, S], F32)
        HTs = sb("HTs", [128, 2, 8, TG], BF16)
        ATT = sb("ATT", [128, 10, S], BF16)
        OT = sb("OT", [128, 8, S], BF16)
        PT = sb("PT", [128, 6, TG], BF16)
        W = sb("W", [128, NSLOT, 4096], BF16)
        SCR = sb("SCR", [128, 7, TG], F32)
        ident = sb("ident", [128, 128], F32)
        cb = sb("cb", [128, 3, 128], BF16)
        colsr = sb("colsr", [128, NL * NCL], F32)
        colsx = sb("colsx", [128, NL * 9], F32)
        PS = stack.enter_context(nc.psum_tensor("ps", [128, 8, TG], F32))
        banks = [PS[:, i, :] for i in range(8)]

        ones_bf = cb[:, 0, :]
        blk_bf = cb[:, 1, :]
        perm_bf = cb[:, 2, :]

        KT = ATT[:, 4, :]
        KTd = ATT[:, 5:7, :]
        Vaug = ATT[:, 7:10, :].rearrange("p a b -> p (a b)").rearrange("p (t k d) -> p t k d", t=16, k=2, d=192)
        OTf = OT[:, 4:8, :].rearrange("p a b -> p (a b)").bitcast(F32)
        ropeC = OTf[:, 0:S]
        ropeS = OTf[:, S:2 * S]
        HTf = HTs[:].rearrange("p a k t -> p (a k t)").bitcast(F32)
        biasm = HTf[:, 0:8 * 384].rearrange("p (h q) -> p h q", h=8)

        def stage(s):
            return ATT[:, s, :].bitcast(F32)

        def ATTt(c):
            return [("ATT", c, tg) for tg in range(NTG)]

        def mm(out, lhsT, rhs, start, stop, reads, writes, tp=None):
            def fn(e):
                if tp is None:
                    return e.matmul(out, lhsT, rhs, start=start, stop=stop)
                return e.matmul(out, lhsT, rhs, start=start, stop=stop, tile_position=tp)
            P.add("pe", fn, reads, writes)

        def tr(out, in_, reads, writes):
            P.add("pe", lambda e: e.transpose(out, in_, ident[:]), reads, writes)

        def act(out, in_, func, reads, writes, bias=None, scale=None):
            def fn(e):
                kw = {}
                if bias is not None:
                    kw["bias"] = bias
                if scale is not None:
                    kw["scale"] = scale
                return e.activation(out=out, in_=in_, func=func, **kw)
            P.add("act", fn, reads, writes)

        def dve(method, reads, writes, **kw):
            P.add("dve", lambda e: getattr(e, method)(**kw), reads, writes)

        def dma(eng, out, in_, reads, writes, key):
            P.add(eng, lambda e: e.dma_start(out=out, in_=in_), reads, writes, dma=key)

        tapouts = {}

        def tap(name, ap, dt, shape, reads):
            if name not in taps:
                return
            d = nc.dram_tensor("tap_" + name, list(shape), dt, kind="ExternalOutput").ap()
            dma("sp", d, ap, reads, [("tap", name)], key="tap_" + name)
            tapouts[name] = True

        ringctr = {}
        SCR_POOLS = {"gen": [0, 1, 2, 3, 4, 5, 6], "rs": [0, 1], "q": [2, 3, 4], "t1": [5], "t2": [6]}
        PT_POOLS = {"norm": [0, 1, 2], "sq": [3, 4], "qb": [5], "norm6": [0, 1, 2, 3, 4, 5]}

        def scr(pool="gen"):
            k = "scr_" + pool
            i = ringctr.get(k, 0)
            ringctr[k] = i + 1
            lst = SCR_POOLS[pool]
            return lst[i % len(lst)]

        def ptn(pool):
            k = "pt_" + pool
            i = ringctr.get(k, 0)
            ringctr[k] = i + 1
            lst = PT_POOLS[pool]
            return lst[i % len(lst)]

        bankctr = {}

        def bank(pool, lst):
            i = bankctr.get(pool, 0)
            bankctr[pool] = i + 1
            return lst[i % len(lst)]

        wplan = []

        def k8(cols):
            return lambda sl: sl.rearrange("p (k c) -> p k c", k=8)[:, :, 0:cols]

        def src_k8(mat, c0, cols):
            return mat[:, c0:c0 + cols].rearrange("(k p) c -> p k c", p=128)

        for l in range(nlayers):
            wplan.append([(k8(512), src_k8(w_in_d[l], 0, 512))])
            wplan.append([(k8(256), src_k8(w_in_d[l], 512, 256))])
            wplan.append([(k8(512), src_k8(w_in_d[l], 768, 512))])
            wplan.append([(k8(256), src_k8(w_in_d[l], 1280, 256))])
            for cg in range(2):
                wplan.append([(k8(512), src_k8(w_in_d[l], 1536 + cg * 512, 512))])
                wplan.append([(k8(512), src_k8(w_in_d[l], 2560 + cg * 512, 512))])
                wplan.append([
                    (lambda sl: sl[:, 0:2048].rearrange("p (k c) -> p k c", k=4),
                     w_oa_d[l][:, cg * 512:(cg + 1) * 512].rearrange("(k p) c -> p k c", p=128)),
                    (lambda sl: sl[:, 2048:4096].rearrange("p (k c) -> p k c", k=4),
                     w_ob_d[l][:, cg * 512:(cg + 1) * 512].rearrange("(k p) c -> p k c", p=128)),
                ])
            for cg in range(2):
                wplan.append([(k8(512), src_k8(w_out_d[l], cg * 512, 512))])
            for fg in range(4):
                wplan.append([(k8(512), src_k8(w1_d[l], fg * 1024, 512))])
                wplan.append([(k8(512), src_k8(w1_d[l], fg * 1024 + 512, 512))])
                for hf in range(2):
                    f0 = fg * 1024 + hf * 512
                    wplan.append([(lambda sl: sl.rearrange("p (k c) -> p k c", k=4),
                                   w2_d[l][f0:f0 + 512, :].rearrange("(k p) c -> p k c", p=128))])

        wst = {"next_load": 0, "released": 0, "next_use": 0}

        def w_pump():
            while wst["next_load"] < len(wplan) and wst["next_load"] < wst["released"] + NSLOT:
                n = wst["next_load"]
                s = n % NSLOT
                for (dstf, src) in wplan[n]:
                    dma("pool", dstf(W[:, s, :]), src, [], [("W", s)], key=f"w{s}")
                wst["next_load"] += 1

        def w_use():
            n = wst["next_use"]
            wst["next_use"] += 1
            assert n < wst["next_load"], "weight not loaded (ring too small for this group)"
            return n % NSLOT

        def w_release(k=1):
            wst["released"] += k
            w_pump()

        dma("sp", ident[:], cst_d[0], [], [("ident",)], key="ci")
        for i in range(3):
            dma("pool", cb[:, i, :], cst_d[1 + i], [], [("cb", i)], key=f"cb{i}")
        dma("sp", colsr[:], cols_d, [], [("colsr",)], key="cc")
        w_pump()

        def col(l, j):
            return colsr[:, l * NCL + j:l * NCL + j + 1]
        colsx2 = sb("colsx2", [128, NL], F32)
        epsc = sb("epsc", [128, 2], F32)
        dve("memset", [], [("epsc",)], ap=epsc[:, 0:1], constant=EPS)
        dve("memset", [], [("epsc",)], ap=epsc[:, 1:2], constant=64.0 * EPS)
        for l in range(nlayers):
            dve("tensor_scalar", [("colsr",)], [("colsx",)], out=colsx[:, l * 9:l * 9 + 1], in0=col(l, 33),
                scalar1=8.0, scalar2=None, op0=ALU.mult)
            dve("tensor_scalar", [("colsr",)], [("colsx",)], out=colsx2[:, l:l + 1], in0=col(l, 35),
                scalar1=8.0, scalar2=None, op0=ALU.mult)
            act(colsx[:, l * 9 + 1:l * 9 + 9], colsr[:, l * NCL + 36:l * NCL + 44], AF.Exp, [("colsr",)], [("colsx",)])

        for tg in range(NTG):
            for t in range(4):
                tile = tg * 4 + t
                s = tile % 8
                dma("sp", stage(s), x_d[tile * 128:(tile + 1) * 128, :], [], ATTt(s), key=f"x{s}")
            for c in range(8):
                bk = bank("p0", [0, 1, 2, 3])
                for t in range(4):
                    s = (tg * 4 + t) % 8
                    tr(banks[bk][:, t * 128:(t + 1) * 128], stage(s)[:, c * 128:(c + 1) * 128],
                       ATTt(s) + [("ident",)], [("ps", bk)])
                dst = XT[:, c, tg * TG:(tg + 1) * TG]
                if c % 2 == 0:
                    act(dst, banks[bk][:], AF.Copy, [], [("ps", bk), ("XT", c, tg)])
                else:
                    dve("tensor_copy", [], [("ps", bk), ("XT", c, tg)], out=dst, in_=banks[bk][:])

        def norm_tg(l, tg, gbase, dst_ap, dst_tile, pool="norm", split=None):
            tsl = slice(tg * TG, (tg + 1) * TG)
            bk = bank("nss", [6, 7])
            sq_idx = {}

            def square(c):
                i = ptn(pool)
                sq_idx[c] = i
                act(PT[:, i, :], XT[:, c, tsl], AF.Square, [("XT", c, tg)], [("PT", i)])

            def rest():
                for c in range(8):
                    if c not in sq_idx:
                        square(c)
                    i = sq_idx[c]
                    mm(banks[bk][:], ones_bf, PT[:, i, :], c == 0, c == 7, [("PT", i), ("cb", 0)], [("ps", bk)])
                r = scr("rs")
                act(SCR[:, r, :], banks[bk][:], AF.Ln, [("epsc",)], [("scr", r), ("ps", bk)], bias=epsc[:, 0:1], scale=1.0 / D)
                act(SCR[:, r, :], SCR[:, r, :], AF.Exp, [("scr", r)], [("scr", r)], scale=-0.5)
                for c in range(8):
                    dve("scalar_tensor_tensor", [("XT", c, tg), ("scr", r), ("colsr",)], [dst_tile(c)],
                        out=dst_ap(c), in0=XT[:, c, tsl], scalar=col(l, gbase + c), in1=SCR[:, r, :],
                        op0=ALU.mult, op1=ALU.mult)

            if split is None:
                rest()
                return None
            for c in range(split):
                square(c)
            return rest

        def ht_dst(hb):
            return (lambda c: HTs[:, hb, c, :]), (lambda c: ("HT", hb, c))

        def qk_unit(l, tg, hb, slot, wc0, gain_ap, rope, dst_ap, dst_tiles, pend):
            tsl = slice(tg * TG, (tg + 1) * TG)
            wv = W[:, slot, :].rearrange("p (k c) -> p k c", k=8)
            bk = bank("main", [0, 1, 2])
            for k in range(8):
                mm(banks[bk][:], wv[:, k, wc0:wc0 + 128], HTs[:, hb, k, :], k == 0, k == 7,
                   [("W", slot), ("HT", hb, k)], [("ps", bk)])
            i = ptn("sq")
            act(PT[:, i, :], banks[bk][:], AF.Square, [], [("ps", bk), ("PT", i)])

            def stage2():
                sb_ = bank("hss", [3, 4])
                mm(banks[sb_][:], blk_bf, PT[:, i, :], True, True, [("PT", i), ("cb", 1)], [("ps", sb_)])
                r = scr("rs")
                act(SCR[:, r, :], banks[sb_][:], AF.Ln, [("epsc",)], [("scr", r), ("ps", sb_)], bias=epsc[:, 1:2], scale=1.0)
                act(SCR[:, r, :], SCR[:, r, :], AF.Exp, [("scr", r)], [("scr", r)], scale=-0.5)
                if not rope:
                    dve("scalar_tensor_tensor", [("scr", r), ("colsr",), ("colsx",)], [("ps", bk)] + dst_tiles,
                        out=dst_ap, in0=banks[bk][:], scalar=gain_ap, in1=SCR[:, r, :], op0=ALU.mult, op1=ALU.mult)
                    return None
                q = scr("q")
                dve("scalar_tensor_tensor", [("scr", r), ("colsr",), ("colsx",)], [("ps", bk), ("scr", q)],
                    out=SCR[:, q, :], in0=banks[bk][:], scalar=gain_ap, in1=SCR[:, r, :], op0=ALU.mult, op1=ALU.mult)
                j = ptn("qb")
                act(PT[:, j, :], SCR[:, q, :], AF.Copy, [("scr", q)], [("PT", j)])

                def stage3():
                    pb = bank("perm", [5])
                    mm(banks[pb][:], perm_bf, PT[:, j, :], True, True, [("PT", j), ("cb", 2)], [("ps", pb)])
                    t1 = scr("t1")
                    dve("tensor_tensor", [("scr", q), ("rope",)], [("scr", t1)], out=SCR[:, t1, :], in0=SCR[:, q, :],
                        in1=ropeC[:, tsl], op=ALU.mult)
                    t2 = scr("t2")
                    dve("tensor_tensor", [("rope",)], [("scr", t2), ("ps", pb)], out=SCR[:, t2, :], in0=banks[pb][:],
                        in1=ropeS[:, tsl], op=ALU.mult)
                    dve("tensor_tensor", [("scr", t1), ("scr", t2)], dst_tiles, out=dst_ap, in0=SCR[:, t1, :],
                        in1=SCR[:, t2, :], op=ALU.add)
                    return None
                return stage3
            pend.append(stage2)

        def run_pend(pend, keep):
            n = len(pend) - keep
            if n <= 0:
                return
            todo = pend[:n]
            del pend[:n]
            newp = []
            for f in todo:
                nxt = f()
                if nxt is not None:
                    newp.append(nxt)
            pend[:0] = newp

        def phase1(l, mixer):
            rope = (mixer == 0)
            if rope:
                dma("sp", ropeC, rope_d[0], [], [("rope",)] + [("OT", c, tg) for c in (4, 5) for tg in range(NTG)], key="rp0")
                dma("sp", ropeS, rope_d[1], [], [("rope",)] + [("OT", c, tg) for c in (6, 7) for tg in range(NTG)], key="rp1")
            sq_ = w_use()
            skv = w_use()
            qg = col(l, 32 if mixer == 0 else 34)
            kg = colsx[:, l * 9:l * 9 + 1] if mixer == 0 else colsx2[:, l:l + 1]
            dve("memset", [], [("V", t) for t in range(16)] + [x_ for c in (7, 8, 9) for x_ in ATTt(c)],
                ap=Vaug[:, :, :, 64:128], constant=1.0)
            pend = []
            da0, dt0 = ht_dst(0)
            norm_tg(l, 0, 0, da0, dt0)
            for tg in range(NTG):
                hb = tg % 2
                if tg + 1 < NTG:
                    da, dt_ = ht_dst((tg + 1) % 2)
                    norm_tg(l, tg + 1, 0, da, dt_)
                tsl = slice(tg * TG, (tg + 1) * TG)
                for c in range(4):
                    qk_unit(l, tg, hb, sq_, c * 128, qg, rope, ATT[:, c, tsl], [("ATT", c, tg)], pend)
                    run_pend(pend, 1)
                qk_unit(l, tg, hb, skv, 0, kg, rope, KT[:, tsl], [("ATT", 4, tg)], pend)
                run_pend(pend, 1)
                wv = W[:, skv, :].rearrange("p (k c) -> p k c", k=8)
                vb = bank("vb", [5])
                for t in range(4):
                    for k in range(8):
                        mm(banks[vb][:, t * 128:(t + 1) * 128], HTs[:, hb, k, t * 128:(t + 1) * 128], wv[:, k, 128:256],
                           k == 0, k == 7, [("W", skv), ("HT", hb, k)], [("ps", vb)])
                src = banks[vb][:].rearrange("p (t k d) -> p t k d", t=4, k=2, d=64)
                vt = [("V", tg * 4 + t) for t in range(4)]
                act(Vaug[:, tg * 4:tg * 4 + 4, :, 0:64], src, AF.Copy, [], [("ps", vb)] + vt)
                act(Vaug[:, tg * 4:tg * 4 + 4, :, 128:192], src, AF.Copy, [], [("ps", vb)] + vt)
            while pend:
                run_pend(pend, 0)
            w_release(2)
            allk = ATTt(4)
            for kv in range(2):
                for half in range(2):
                    dma("sp", KTd[half * 64:(half + 1) * 64, kv, :], KT[kv * 64:(kv + 1) * 64, :],
                        allk, [("KTd", kv, half)] + ATTt(5 + kv), key=f"kd{kv}{half}")

        def attn_global(l):
            steps = []
            u = 0
            for kv in range(2):
                for tg in range(NTG):
                    for pr in range(2):
                        for sbk in range(16):
                            steps.append((u, kv, tg, pr, sbk))
                        u += 1
            pend = []

            def do_pv(st, s):
                (u, kv, tg, pr, sbk) = st
                c = kv * 2 + pr
                tsl = slice(tg * TG, (tg + 1) * TG)
                for hh in range(2):
                    ab = 4 + 2 * (u % 2) + hh
                    pt = (s % 3) * 2 + hh
                    lhsT = Vaug[:, sbk, kv, 0:128] if hh == 0 else Vaug[:, sbk, kv, 64:192]
                    mm(banks[ab][:], lhsT, PT[:, pt, :], sbk == 0, sbk == 15, [("PT", pt), ("V", sbk)], [("ps", ab)])
                    if sbk == 15:
                        orow = slice(hh * 64, hh * 64 + 64)
                        drow = slice(64 - hh * 64, 128 - hh * 64)
                        r = scr()
                        dve("reciprocal", [], [("scr", r), ("ps", ab)], out=SCR[orow, r, :], in_=banks[ab][drow, :])
                        dve("tensor_tensor", [("scr", r)], [("ps", ab), ("OT", c, tg)], out=OT[orow, c, tsl],
                            in0=banks[ab][orow, :], in1=SCR[orow, r, :], op=ALU.mult)

            for s, st in enumerate(steps):
                (u, kv, tg, pr, sbk) = st
                c = kv * 2 + pr
                for hh in range(2):
                    lt = (s % 2) * 2 + hh
                    rows = slice(hh * 64, hh * 64 + 64)
                    mm(banks[lt][:], KTd[rows, kv, sbk * 128:(sbk + 1) * 128], ATT[rows, c, tg * TG:(tg + 1) * TG],
                       True, True, [("ATT", c, tg), ("KTd", kv, hh)], [("ps", lt)], tp=(hh * 64, 0))
                for hh in range(2):
                    lt = (s % 2) * 2 + hh
                    pt = (s % 3) * 2 + hh
                    act(PT[:, pt, :], banks[lt][:], AF.Exp, [], [("ps", lt), ("PT", pt)])
                pend.append((st, s))
                if len(pend) > 1:
                    do_pv(*pend.pop(0))
            while pend:
                do_pv(*pend.pop(0))

        def attn_window(l):
            allht = [("HT", hb, c) for hb in range(2) for c in range(8)]
            dma("sp", biasm, bias_d, [], allht + [("biasm",)], key="bm0")
            mi = scr()
            dma("sp", SCR[:, mi, 0:384], mask_d, [], [("scr", mi)], key="bm1")
            for h in range(8):
                dve("tensor_tensor", [("scr", mi), ("biasm",)], [("biasm", h)] + ([("biasm",)] if h == 7 else []),
                    out=biasm[:, h, :], in0=biasm[:, h, :], in1=SCR[:, mi, 0:384], op=ALU.add)
            PTB = PT[:].rearrange("p a b -> p (a b)")[:, 0:8 * 384].rearrange("p (r q) -> p r q", r=8)
            allpt = [("PT", i) for i in range(6)]
            allptb = [("PTB", i) for i in range(8)]
            P.add("act", None, [], allpt, nostate=True)
            normq = []
            bstep = 0

            def emit_norm(ab, tg, h, c, orow, drow):
                tsl = slice(tg * TG, (tg + 1) * TG)
                r1 = scr()
                act(SCR[drow, r1, :], banks[ab][drow, :], AF.Ln, [("colsx",)], [("scr", r1), ("ps", ab)],
                    bias=colsx[drow, l * 9 + 1 + h:l * 9 + 2 + h], scale=1.0)
                act(SCR[drow, r1, :], SCR[drow, r1, :], AF.Exp, [("scr", r1)], [("scr", r1)], scale=-1.0)
                r2 = scr()
                dve("tensor_copy", [("scr", r1)], [("scr", r2)], out=SCR[orow, r2, :], in_=SCR[drow, r1, :])
                dve("tensor_tensor", [("scr", r2)], [("ps", ab), ("OT", c, tg)], out=OT[orow, c, tsl],
                    in0=banks[ab][orow, :], in1=SCR[orow, r2, :], op=ALU.mult)

            gstep = 0
            for h in range(8):
                c = 4 + h // 2
                hh = h % 2
                kv = h // 4
                rows = slice(hh * 64, hh * 64 + 64)
                orow = rows
                drow = slice(64 - hh * 64, 128 - hh * 64)
                vsl = slice(0, 128) if hh == 0 else slice(64, 192)
                for j in range(20):
                    if j < 16:
                        lo = max(j - 1, 0)
                        hi = min(j + 1, 15)
                        w = (hi - lo + 1) * 128
                        off = (lo - (j - 1)) * 128
                        lt = bank("bl", [0, 1, 2])
                        tgs = sorted(set((b * 128) // TG for b in range(lo, hi + 1)))
                        mm(banks[lt][:, 0:w], KTd[rows, kv, j * 128:(j + 1) * 128], ATT[rows, c - 4, lo * 128:(hi + 1) * 128],
                           True, True, [("ATT", c - 4, t_) for t_ in tgs] + [("KTd", kv, hh)], [("ps", lt)], tp=(hh * 64, 0))
                        tm = scr()
                        dve("tensor_tensor", [("biasm", h)], [("scr", tm), ("ps", lt)], out=SCR[:, tm, 0:w],
                            in0=banks[lt][:, 0:w], in1=biasm[:, h, off:off + w], op=ALU.add)
                        act(PTB[:, j % 8, off:off + w], SCR[:, tm, 0:w], AF.Exp, [("scr", tm)], [("PTB", j % 8)])
                    i = j - 4
                    if i >= 0:
                        ab = 4 + (gstep // 4) % 4
                        gstep += 1
                        contrib = [jj for jj in (i - 1, i, i + 1) if 0 <= jj < 16]
                        for n_, jj in enumerate(contrib):
                            b = i - jj + 1
                            mm(banks[ab][:, (i % 4) * 128:(i % 4 + 1) * 128], Vaug[:, jj, kv, vsl],
                               PTB[:, jj % 8, b * 128:(b + 1) * 128], n_ == 0, n_ == len(contrib) - 1,
                               [("PTB", jj % 8), ("V", jj)], [("ps", ab)])
                        if i % 4 == 3:
                            normq.append((bstep + 2, ab, i // 4, h, c, orow, drow))
                    bstep += 1
                    while normq and normq[0][0] <= bstep:
                        emit_norm(*normq.pop(0)[1:])
            while normq:
                emit_norm(*normq.pop(0)[1:])
            P.add("act", None, [], allptb, nostate=True)


        def phase3(l):
            for cg in range(2):
                sga = w_use()
                sgb = w_use()
                swo = w_use()
                wga = W[:, sga, :].rearrange("p (k c) -> p k c", k=8)
                wgb = W[:, sgb, :].rearrange("p (k c) -> p k c", k=8)
                woa = W[:, swo, 0:2048].rearrange("p (k c) -> p k c", k=4)
                wob = W[:, swo, 2048:4096].rearrange("p (k c) -> p k c", k=4)
                if cg == 0:
                    da0, dt0 = ht_dst(0)
                    norm_tg(l, 0, 0, da0, dt0)
                for tg in range(NTG):
                    hb = tg % 2
                    late = None
                    if not (cg == 1 and tg == NTG - 1):
                        ntg = (tg + 1) % NTG
                        da, dt_ = ht_dst(ntg % 2)
                        late = norm_tg(l, ntg, 0, da, dt_, pool="norm6", split=6)
                    tsl = slice(tg * TG, (tg + 1) * TG)
                    for cc in range(4):
                        if cc == 2 and late is not None:
                            late()
                        c = cg * 4 + cc
                        bga = bank("p3a", [0, 1])
                        bgb = bank("p3b", [2, 3])
                        bya = bank("p3c", [4])
                        byb = bank("p3d", [5])
                        for k in range(8):
                            mm(banks[bga][:], wga[:, k, cc * 128:(cc + 1) * 128], HTs[:, hb, k, :], k == 0, k == 7,
                               [("W", sga), ("HT", hb, k)], [("ps", bga)])
                        for k in range(8):
                            mm(banks[bgb][:], wgb[:, k, cc * 128:(cc + 1) * 128], HTs[:, hb, k, :], k == 0, k == 7,
                               [("W", sgb), ("HT", hb, k)], [("ps", bgb)])
                        for k in range(4):
                            mm(banks[bya][:], woa[:, k, cc * 128:(cc + 1) * 128], OT[:, k, tsl], k == 0, k == 3,
                               [("W", swo), ("OT", k, tg)], [("ps", bya)])
                        for k in range(4):
                            mm(banks[byb][:], wob[:, k, cc * 128:(cc + 1) * 128], OT[:, 4 + k, tsl], k == 0, k == 3,
                               [("W", swo), ("OT", 4 + k, tg)], [("ps", byb)])
                        ra = scr()
                        act(SCR[:, ra, :], banks[bga][:], AF.Sigmoid, [("colsr",)], [("ps", bga), ("scr", ra)],
                            bias=col(l, 16 + c))
                        rb = scr()
                        act(SCR[:, rb, :], banks[bgb][:], AF.Sigmoid, [("colsr",)], [("ps", bgb), ("scr", rb)],
                            bias=col(l, 24 + c))
                        dve("tensor_tensor", [("scr", ra)], [("scr", ra), ("ps", bya)], out=SCR[:, ra, :],
                            in0=banks[bya][:], in1=SCR[:, ra, :], op=ALU.mult)
                        dve("tensor_tensor", [("scr", rb)], [("scr", rb), ("ps", byb)], out=SCR[:, rb, :],
                            in0=banks[byb][:], in1=SCR[:, rb, :], op=ALU.mult)
                        dve("tensor_tensor", [("scr", ra), ("scr", rb)], [("ATT", c, tg)], out=ATT[:, c, tsl],
                            in0=SCR[:, ra, :], in1=SCR[:, rb, :], op=ALU.add)
                w_release(3)
            for cg in range(2):
                so = w_use()
                wo = W[:, so, :].rearrange("p (k c) -> p k c", k=8)
                for cc in range(4):
                    c = cg * 4 + cc
                    for tg in range(NTG):
                        tsl = slice(tg * TG, (tg + 1) * TG)
                        bk = bank("p3o", [0, 1, 2, 3])
                        for k in range(8):
                            mm(banks[bk][:], wo[:, k, cc * 128:(cc + 1) * 128], ATT[:, k, tsl], k == 0, k == 7,
                               [("W", so), ("ATT", k, tg)], [("ps", bk)])
                        dve("tensor_tensor", [("XT", c, tg)], [("ps", bk), ("XT", c, tg)], out=XT[:, c, tsl],
                            in0=banks[bk][:], in1=XT[:, c, tsl], op=ALU.add)
                w_release(1)

        def phase4(l):
            for tg in range(NTG):
                tsl = slice(tg * TG, (tg + 1) * TG)
                norm_tg(l, tg, 8, (lambda c, tsl=tsl: ATT[:, c, tsl]), (lambda c, tg=tg: ("ATT", c, tg)))
            for fg in range(4):
                s1 = [w_use(), w_use()]
                s2 = [w_use(), w_use()]
                for fc in range(8):
                    w1 = W[:, s1[fc // 4], :].rearrange("p (k c) -> p k c", k=8)
                    f0 = (fc % 4) * 128
                    for tg in range(NTG):
                        tsl = slice(tg * TG, (tg + 1) * TG)
                        bk = bank("p4u", [0, 1, 2, 3])
                        for k in range(8):
                            mm(banks[bk][:], w1[:, k, f0:f0 + 128], ATT[:, k, tsl], k == 0, k == 7,
                               [("W", s1[fc // 4]), ("ATT", k, tg)], [("ps", bk)])
                        r = scr()
                        act(SCR[:, r, :], banks[bk][:], AF.Square, [], [("ps", bk), ("scr", r)])
                        dve("scalar_tensor_tensor", [("scr", r)], [("ps", bk), ("OT", fc, tg)], out=OT[:, fc, tsl],
                            in0=banks[bk][:], scalar=0.0, in1=SCR[:, r, :], op0=ALU.is_gt, op1=ALU.mult)
                w_release(2)
                for c in range(8):
                    for tg in range(NTG):
                        tsl = slice(tg * TG, (tg + 1) * TG)
                        bk = bank("p4d", [4, 5, 6, 7])
                        for fc in range(8):
                            w2 = W[:, s2[fc // 4], :].rearrange("p (k c) -> p k c", k=4)
                            mm(banks[bk][:], w2[:, fc % 4, c * 128:(c + 1) * 128], OT[:, fc, tsl], fc == 0, fc == 7,
                               [("W", s2[fc // 4]), ("OT", fc, tg)], [("ps", bk)])
                        dve("tensor_tensor", [("XT", c, tg)], [("ps", bk), ("XT", c, tg)], out=XT[:, c, tsl],
                            in0=banks[bk][:], in1=XT[:, c, tsl], op=ALU.add)
                w_release(2)

        allxt = [("XT", c, tg) for c in range(8) for tg in range(NTG)]
        tap("XT0", XT[:], F32, [128, 8, S], allxt)
        for l in range(nlayers):
            phase1(l, 0)
            if l == 0:
                tap("QA", ATT[:, 0:4, :], BF16, [128, 4, S], [t_ for c in range(4) for t_ in ATTt(c)])
                tap("KA", ATT[:, 4:7, :], BF16, [128, 3, S], [t_ for c in (4, 5, 6) for t_ in ATTt(c)] + [("KTd", kv, h) for kv in range(2) for h in range(2)])
                tap("VA", ATT[:, 7:10, :], BF16, [128, 3, S], [("V", t) for t in range(16)])
            attn_global(l)
            if l == 0:
                tap("OA", OT[:, 0:4, :], BF16, [128, 4, S], [("OT", c, tg) for c in range(4) for tg in range(NTG)])
            phase1(l, 1)
            if l == 0:
                tap("QB", ATT[:, 0:4, :], BF16, [128, 4, S], [t_ for c in range(4) for t_ in ATTt(c)])
                tap("KB", ATT[:, 4:7, :], BF16, [128, 3, S], [t_ for c in (4, 5, 6) for t_ in ATTt(c)] + [("KTd", kv, h) for kv in range(2) for h in range(2)])
            attn_window(l)
            if l == 0:
                tap("OB", OT[:, 4:8, :], BF16, [128, 4, S], [("OT", c, tg) for c in range(4, 8) for tg in range(NTG)])
            phase3(l)
            if l == 0:
                tap("MIX", ATT[:, 0:8, :], BF16, [128, 8, S], [t_ for c in range(8) for t_ in ATTt(c)])
                tap("XT1", XT[:], F32, [128, 8, S], allxt)
            phase4(l)
            if l == 0:
                tap("XT2", XT[:], F32, [128, 8, S], allxt)

        outs = []
        for t in range(16):
            s = t % 8
            for half in range(2):
                bk = bank("po", [0, 1, 2, 3])
                for cc in range(4):
                    c = half * 4 + cc
                    tr(banks[bk][:, cc * 128:(cc + 1) * 128], XT[:, c, t * 128:(t + 1) * 128],
                       [("XT", c, t // 4), ("ident",)], [("ps", bk)])
                dst = stage(s)[:, half * 512:(half + 1) * 512]
                if half == 0:
                    act(dst, banks[bk][:], AF.Copy, [], [("ps", bk), ("ostg", s, half)] + ATTt(s))
                else:
                    dve("tensor_copy", [], [("ps", bk), ("ostg", s, half)] + ATTt(s), out=dst, in_=banks[bk][:])
            dma("sp", y_d[t * 128:(t + 1) * 128, :], stage(s), [("ostg", s, 0), ("ostg", s, 1)] + ATTt(s),
                [("y", t)], key=f"y{s}")
        P.add("sp", None, [("y", t) for t in range(16)] + [("tap", n_) for n_ in tapouts], [])

        sem_names = P.finalize()
        sems = {k: stack.enter_context(nc.semaphore(k)) for k in sem_names}
        with nc.Block() as block:
            @block.tensor
            def _(e):
                P.emit("pe", e, sems)

            @block.scalar
            def _(e):
                P.emit("act", e, sems)

            @block.vector
            def _(e):
                P.emit("dve", e, sems)

            @block.gpsimd
            def _(e):
                P.emit("pool", e, sems)

            @block.sync
            def _(e):
                P.emit("sp", e, sems)
    return nc


def _t5_bucket(rel):
    nb = 16
    max_exact = 8
    n = np.abs(rel)
    large = max_exact + (np.log(np.maximum(n, 1).astype(np.float32) / max_exact)
                         / math.log(128 / max_exact) * (nb - max_exact)).astype(np.int32)
    large = np.minimum(large, nb - 1)
    return np.where(rel > 0, nb, 0) + np.where(n < max_exact, n, large)


def _host_tables(inputs):
    f32 = np.float32
    cols = np.zeros((128, NL * NCL), f32)
    p = np.arange(128)
    for l in range(NL):
        b = l * NCL
        cols[:, b + 0:b + 8] = np.asarray(inputs["norm_mix"][l], f32).reshape(8, 128).T
        cols[:, b + 8:b + 16] = np.asarray(inputs["norm_mlp"][l], f32).reshape(8, 128).T
        cols[:, b + 16:b + 32] = np.asarray(inputs["b_gate"][l], f32).reshape(16, 128).T
        cols[:, b + 32] = np.asarray(inputs["qn_a"][l], f32)[p % 64]
        cols[:, b + 33] = np.asarray(inputs["kn_a"][l], f32)[p % 64]
        cols[:, b + 34] = np.asarray(inputs["qn_b"][l], f32)[p % 64]
        cols[:, b + 35] = np.asarray(inputs["kn_b"][l], f32)[p % 64]
        cols[:, b + 36:b + 44] = np.asarray(inputs["sink_b"][l], f32)[None, :]
    t = np.arange(S)
    row = (t // 64).astype(f32)
    colp = (t % 64).astype(f32)
    inv_freq = (10000.0 ** (-np.arange(16, dtype=f32) / 16)).astype(f32)
    d = p % 64
    half = d // 32
    j = d % 32
    fidx = j % 16
    pos = np.where(half[:, None] == 0, row[None, :], colp[None, :]).astype(f32)
    ang = (pos * inv_freq[fidx][:, None]).astype(f32)
    C = np.cos(ang).astype(f32)
    Sg = np.sin(ang).astype(f32)
    Sp = np.where((j < 16)[:, None], -Sg, Sg).astype(f32)
    rope = np.stack([C, Sp]).astype(f32)
    s_ = np.arange(128)[:, None]
    q_ = np.arange(384)[None, :]
    rel = s_ + 128 - q_
    bucket = _t5_bucket(rel)
    rb = np.asarray(inputs["rel_bias"], f32)
    biasT = np.ascontiguousarray(np.transpose(rb[bucket], (0, 2, 1))).astype(f32)
    maskT = np.where(np.abs(rel) <= 128, 0.0, MASKVAL).astype(f32)
    ident = np.eye(128, dtype=f32)
    ones = np.ones((128, 128), f32)
    blk = np.zeros((128, 128), f32)
    blk[:64, :64] = 1.0
    blk[64:, 64:] = 1.0
    perm = np.zeros((128, 128), f32)
    m = np.arange(128)
    pm = np.where((m % 32) < 16, m + 16, m - 16)
    perm[pm, m] = 1.0
    cst = np.stack([ident, ones, blk, perm]).astype(f32)
    return cols, rope, biasT, maskT, cst


_NC_CACHE = {}


def kernel(**inputs):
    f32 = np.float32
    x = np.asarray(inputs["x"], f32)
    cols, rope, biasT, maskT, cst = _host_tables(inputs)
    shared = {
        "w_in": np.ascontiguousarray(np.asarray(inputs["w_in"], f32)),
        "w_o_a": np.ascontiguousarray(np.asarray(inputs["w_o_a"], f32)),
        "w_o_b": np.ascontiguousarray(np.asarray(inputs["w_o_b"], f32)),
        "w_out": np.ascontiguousarray(np.asarray(inputs["w_out"], f32)),
        "w_mlp1": np.ascontiguousarray(np.asarray(inputs["w_mlp1"], f32)),
        "w_mlp2": np.ascontiguousarray(np.asarray(inputs["w_mlp2"], f32)),
        "cols": cols, "rope": rope, "biasT": biasT, "maskT": maskT, "cst": cst,
    }
    if "nc" not in _NC_CACHE:
        _NC_CACHE["nc"] = build(NL)
    nc = _NC_CACHE["nc"]
    in_maps = []
    for b in range(8):
        m = dict(shared)
        m["x"] = np.ascontiguousarray(x[b])
        in_maps.append(m)
    res = run_bass_kernel_spmd(nc, in_maps, core_ids=list(range(8)))
    out = np.stack([np.asarray(r["y"], f32) for r in res.results], axis=0)
    return out
```

```python
import math
from contextlib import ExitStack

import numpy as np
import concourse.bass as bass
import concourse.mybir as mybir
from concourse.bass_utils import run_bass_kernel_spmd

F32 = mybir.dt.float32
BF16 = mybir.dt.bfloat16
ALU = mybir.AluOpType
AF = mybir.ActivationFunctionType

S = 2048
D = 1024
NL = 2
NTG = 4
TG = 512
EPS = 1e-6
NSLOT = 4
NCL = 48
MASKVAL = -30000.0


class Prog:
    def __init__(self):
        self.ops = []
        self.last_w = {}
        self.readers = {}

    def add(self, eng, fn, reads=(), writes=(), dma=None, nostate=False):
        i = len(self.ops)
        raw = set()
        other = set()
        for t in reads:
            w = self.last_w.get(t)
            if w is not None:
                raw.add(w)
        for t in writes:
            w = self.last_w.get(t)
            if w is not None:
                other.add(w)
            for r in self.readers.get(t, ()):
                other.add(r)
        deps = set()
        for j in raw | other:
            oj = self.ops[j]
            if oj["dma"] is None and dma is None and oj["eng"] == eng:
                if eng == "pe":
                    continue
            deps.add(j)
        self.ops.append(dict(eng=eng, fn=fn, deps=sorted(deps), dma=dma, signal=False))
        if nostate:
            return i
        for t in writes:
            self.last_w[t] = i
            self.readers[t] = []
        for t in reads:
            self.readers.setdefault(t, []).append(i)
        return i

    def finalize(self):
        for op in self.ops:
            for j in op["deps"]:
                self.ops[j]["signal"] = True
        cnt = {}
        for op in self.ops:
            if op["dma"] is not None:
                k = "d_" + op["dma"]
                cnt[k] = cnt.get(k, 0) + 16
                op["sem"] = k
                op["val"] = cnt[k]
            elif op["signal"]:
                k = "e_" + op["eng"]
                cnt[k] = cnt.get(k, 0) + 1
                op["sem"] = k
                op["val"] = cnt[k]
        return sorted(cnt.keys())

    def emit(self, eng_name, eng, sems):
        waited = {}
        for op in self.ops:
            if op["eng"] != eng_name:
                continue
            need = {}
            for j in op["deps"]:
                oj = self.ops[j]
                need[oj["sem"]] = max(need.get(oj["sem"], 0), oj["val"])
            for k, v in need.items():
                if waited.get(k, 0) < v:
                    eng.wait_ge(sems[k], v)
                    waited[k] = v
            if op["fn"] is None:
                continue
            ins = op["fn"](eng)
            if op["dma"] is not None:
                ins.then_inc(sems[op["sem"]], 16)
            elif op["signal"]:
                ins.then_inc(sems[op["sem"]], 1)


def build(nlayers=NL, taps=()):
    nc = bass.Bass("TRN2", target_bir_lowering=False)
    P = Prog()

    def dram(name, shape, kind="ExternalInput"):
        return nc.dram_tensor(name, list(shape), F32, kind=kind).ap()

    x_d = dram("x", [S, D])
    w_in_d = dram("w_in", [NL, D, 3584])
    w_oa_d = dram("w_o_a", [NL, 512, D])
    w_ob_d = dram("w_o_b", [NL, 512, D])
    w_out_d = dram("w_out", [NL, D, D])
    w1_d = dram("w_mlp1", [NL, D, 4096])
    w2_d = dram("w_mlp2", [NL, 4096, D])
    cols_d = dram("cols", [128, NL * NCL])
    rope_d = dram("rope", [2, 128, S])
    bias_d = dram("biasT", [128, 8, 384])
    mask_d = dram("maskT", [128, 384])
    cst_d = dram("cst", [4, 128, 128])
    y_d = dram("y", [S, D], kind="ExternalOutput")

    stack = ExitStack()
    with stack:
        def sb(name, shape, dt):
            return stack.enter_context(nc.sbuf_tensor(name, list(shape), dt))

        XT = sb("XT", [128, 8, S], F32)
        HTs = sb("HTs", [128, 2, 8, TG], BF16)
        ATT = sb("ATT", [128, 10, S], BF16)
        OT = sb("OT", [128, 8, S], BF16)
        PT = sb("PT", [128, 6, TG], BF16)
        W = sb("W", [128, NSLOT, 4096], BF16)
        SCR = sb("SCR", [128, 7, TG], F32)
        ident = sb("ident", [128, 128], F32)
        cb = sb("cb", [128, 3, 128], BF16)
        colsr = sb("colsr", [128, NL * NCL], F32)
        colsx = sb("colsx", [128, NL * 9], F32)
        PS = stack.enter_context(nc.psum_tensor("ps", [128, 8, TG], F32))
        banks = [PS[:, i, :] for i in range(8)]

        ones_bf = cb[:, 0, :]
        blk_bf = cb[:, 1, :]
        perm_bf = cb[:, 2, :]

        KT = ATT[:, 4, :]
        KTd = ATT[:, 5:7, :]
        Vaug = ATT[:, 7:10, :].rearrange("p a b -> p (a b)").rearrange("p (t k d) -> p t k d", t=16, k=2, d=192)
        OTf = OT[:, 4:8, :].rearrange("p a b -> p (a b)").bitcast(F32)
        ropeC = OTf[:, 0:S]
        ropeS = OTf[:, S:2 * S]
        HTf = HTs[:].rearrange("p a k t -> p (a k t)").bitcast(F32)
        biasm = HTf[:, 0:8 * 384].rearrange("p (h q) -> p h q", h=8)

        def stage(s):
            return ATT[:, s, :].bitcast(F32)

        def ATTt(c):
            return [("ATT", c, tg) for tg in range(NTG)]

        def mm(out, lhsT, rhs, start, stop, reads, writes, tp=None):
            def fn(e):
                if tp is None:
                    return e.matmul(out, lhsT, rhs, start=start, stop=stop)
                return e.matmul(out, lhsT, rhs, start=start, stop=stop, tile_position=tp)
            P.add("pe", fn, reads, writes)

        def tr(out, in_, reads, writes):
            P.add("pe", lambda e: e.transpose(out, in_, ident[:]), reads, writes)

        def act(out, in_, func, reads, writes, bias=None, scale=None):
            def fn(e):
                kw = {}
                if bias is not None:
                    kw["bias"] = bias
                if scale is not None:
                    kw["scale"] = scale
                return e.activation(out=out, in_=in_, func=func, **kw)
            P.add("act", fn, reads, writes)

        def dve(method, reads, writes, **kw):
            P.add("dve", lambda e: getattr(e, method)(**kw), reads, writes)

        def dma(eng, out, in_, reads, writes, key):
            P.add(eng, lambda e: e.dma_start(out=out, in_=in_), reads, writes, dma=key)

        tapouts = {}

        def tap(name, ap, dt, shape, reads):
            if name not in taps:
                return
            d = nc.dram_tensor("tap_" + name, list(shape), dt, kind="ExternalOutput").ap()
            dma("sp", d, ap, reads, [("tap", name)], key="tap_" + name)
            tapouts[name] = True

        ringctr = {}
        SCR_POOLS = {"gen": [0, 1, 2, 3, 4, 5, 6], "rs": [0, 1], "q": [2, 3, 4], "t1": [5], "t2": [6]}
        PT_POOLS = {"norm": [0, 1, 2], "sq": [3, 4], "qb": [5], "norm6": [0, 1, 2, 3, 4, 5]}

        def scr(pool="gen"):
            k = "scr_" + pool
            i = ringctr.get(k, 0)
            ringctr[k] = i + 1
            lst = SCR_POOLS[pool]
            return lst[i % len(lst)]

        def ptn(pool):
            k = "pt_" + pool
            i = ringctr.get(k, 0)
            ringctr[k] = i + 1
            lst = PT_POOLS[pool]
            return lst[i % len(lst)]

        bankctr = {}

        def bank(pool, lst):
            i = bankctr.get(pool, 0)
            bankctr[pool] = i + 1
            return lst[i % len(lst)]

        wplan = []

        def k8(cols):
            return lambda sl: sl.rearrange("p (k c) -> p k c", k=8)[:, :, 0:cols]

        def src_k8(mat, c0, cols):
            return mat[:, c0:c0 + cols].rearrange("(k p) c -> p k c", p=128)

        for l in range(nlayers):
            wplan.append([(k8(512), src_k8(w_in_d[l], 0, 512))])
            wplan.append([(k8(256), src_k8(w_in_d[l], 512, 256))])
            wplan.append([(k8(512), src_k8(w_in_d[l], 768, 512))])
            wplan.append([(k8(256), src_k8(w_in_d[l], 1280, 256))])
            for cg in range(2):
                wplan.append([(k8(512), src_k8(w_in_d[l], 1536 + cg * 512, 512))])
                wplan.append([(k8(512), src_k8(w_in_d[l], 2560 + cg * 512, 512))])
                wplan.append([
                    (lambda sl: sl[:, 0:2048].rearrange("p (k c) -> p k c", k=4),
                     w_oa_d[l][:, cg * 512:(cg + 1) * 512].rearrange("(k p) c -> p k c", p=128)),
                    (lambda sl: sl[:, 2048:4096].rearrange("p (k c) -> p k c", k=4),
                     w_ob_d[l][:, cg * 512:(cg + 1) * 512].rearrange("(k p) c -> p k c", p=128)),
                ])
            for cg in range(2):
                wplan.append([(k8(512), src_k8(w_out_d[l], cg * 512, 512))])
            for fg in range(4):
                wplan.append([(k8(512), src_k8(w1_d[l], fg * 1024, 512))])
                wplan.append([(k8(512), src_k8(w1_d[l], fg * 1024 + 512, 512))])
                for hf in range(2):
                    f0 = fg * 1024 + hf * 512
                    wplan.append([(lambda sl: sl.rearrange("p (k c) -> p k c", k=4),
                                   w2_d[l][f0:f0 + 512, :].rearrange("(k p) c -> p k c", p=128))])

        wst = {"next_load": 0, "released": 0, "next_use": 0}

        def w_pump():
            while wst["next_load"] < len(wplan) and wst["next_load"] < wst["released"] + NSLOT:
                n = wst["next_load"]
                s = n % NSLOT
                for (dstf, src) in wplan[n]:
                    dma("pool", dstf(W[:, s, :]), src, [], [("W", s)], key=f"w{s}")
                wst["next_load"] += 1

        def w_use():
            n = wst["next_use"]
            wst["next_use"] += 1
            assert n < wst["next_load"], "weight not loaded (ring too small for this group)"
            return n % NSLOT

        def w_release(k=1):
            wst["released"] += k
            w_pump()

        dma("sp", ident[:], cst_d[0], [], [("ident",)], key="ci")
        for i in range(3):
            dma("pool", cb[:, i, :], cst_d[1 + i], [], [("cb", i)], key=f"cb{i}")
        dma("sp", colsr[:], cols_d, [], [("colsr",)], key="cc")
        w_pump()

        def col(l, j):
            return colsr[:, l * NCL + j:l * NCL + j + 1]
        colsx2 = sb("colsx2", [128, NL], F32)
        epsc = sb("epsc", [128, 2], F32)
        dve("memset", [], [("epsc",)], ap=epsc[:, 0:1], constant=EPS)
        dve("memset", [], [("epsc",)], ap=epsc[:, 1:2], constant=64.0 * EPS)
        for l in range(nlayers):
            dve("tensor_scalar", [("colsr",)], [("colsx",)], out=colsx[:, l * 9:l * 9 + 1], in0=col(l, 33),
                scalar1=8.0, scalar2=None, op0=ALU.mult)
            dve("tensor_scalar", [("colsr",)], [("colsx",)], out=colsx2[:, l:l + 1], in0=col(l, 35),
                scalar1=8.0, scalar2=None, op0=ALU.mult)
            act(colsx[:, l * 9 + 1:l * 9 + 9], colsr[:, l * NCL + 36:l * NCL + 44], AF.Exp, [("colsr",)], [("colsx",)])

        for tg in range(NTG):
            for t in range(4):
                tile = tg * 4 + t
                s = tile % 8
                dma("sp", stage(s), x_d[tile * 128:(tile + 1) * 128, :], [], ATTt(s), key=f"x{s}")
            for c in range(8):
                bk = bank("p0", [0, 1, 2, 3])
                for t in range(4):
                    s = (tg * 4 + t) % 8
                    tr(banks[bk][:, t * 128:(t + 1) * 128], stage(s)[:, c * 128:(c + 1) * 128],
                       ATTt(s) + [("ident",)], [("ps", bk)])
                dst = XT[:, c, tg * TG:(tg + 1) * TG]
                if c % 2 == 0:
                    act(dst, banks[bk][:], AF.Copy, [], [("ps", bk), ("XT", c, tg)])
                else:
                    dve("tensor_copy", [], [("ps", bk), ("XT", c, tg)], out=dst, in_=banks[bk][:])

        def norm_tg(l, tg, gbase, dst_ap, dst_tile, pool="norm", split=None):
            tsl = slice(tg * TG, (tg + 1) * TG)
            bk = bank("nss", [6, 7])
            sq_idx = {}

            def square(c):
                i = ptn(pool)
                sq_idx[c] = i
                act(PT[:, i, :], XT[:, c, tsl], AF.Square, [("XT", c, tg)], [("PT", i)])

            def rest():
                for c in range(8):
                    if c not in sq_idx:
                        square(c)
                    i = sq_idx[c]
                    mm(banks[bk][:], ones_bf, PT[:, i, :], c == 0, c == 7, [("PT", i), ("cb", 0)], [("ps", bk)])
                r = scr("rs")
                act(SCR[:, r, :], banks[bk][:], AF.Ln, [("epsc",)], [("scr", r), ("ps", bk)], bias=epsc[:, 0:1], scale=1.0 / D)
                act(SCR[:, r, :], SCR[:, r, :], AF.Exp, [("scr", r)], [("scr", r)], scale=-0.5)
                for c in range(8):
                    dve("scalar_tensor_tensor", [("XT", c, tg), ("scr", r), ("colsr",)], [dst_tile(c)],
                        out=dst_ap(c), in0=XT[:, c, tsl], scalar=col(l, gbase + c), in1=SCR[:, r, :],
                        op0=ALU.mult, op1=ALU.mult)

            if split is None:
                rest()
                return None
            for c in range(split):
                square(c)
            return rest

        def ht_dst(hb):
            return (lambda c: HTs[:, hb, c, :]), (lambda c: ("HT", hb, c))

        def qk_unit(l, tg, hb, slot, wc0, gain_ap, rope, dst_ap, dst_tiles, pend):
            tsl = slice(tg * TG, (tg + 1) * TG)
            wv = W[:, slot, :].rearrange("p (k c) -> p k c", k=8)
            bk = bank("main", [0, 1, 2])
            for k in range(8):
                mm(banks[bk][:], wv[:, k, wc0:wc0 + 128], HTs[:, hb, k, :], k == 0, k == 7,
                   [("W", slot), ("HT", hb, k)], [("ps", bk)])
            i = ptn("sq")
            act(PT[:, i, :], banks[bk][:], AF.Square, [], [("ps", bk), ("PT", i)])

            def stage2():
                sb_ = bank("hss", [3, 4])
                mm(banks[sb_][:], blk_bf, PT[:, i, :], True, True, [("PT", i), ("cb", 1)], [("ps", sb_)])
                r = scr("rs")
                act(SCR[:, r, :], banks[sb_][:], AF.Ln, [("epsc",)], [("scr", r), ("ps", sb_)], bias=epsc[:, 1:2], scale=1.0)
                act(SCR[:, r, :], SCR[:, r, :], AF.Exp, [("scr", r)], [("scr", r)], scale=-0.5)
                if not rope:
                    dve("scalar_tensor_tensor", [("scr", r), ("colsr",), ("colsx",)], [("ps", bk)] + dst_tiles,
                        out=dst_ap, in0=banks[bk][:], scalar=gain_ap, in1=SCR[:, r, :], op0=ALU.mult, op1=ALU.mult)
                    return None
                q = scr("q")
                dve("scalar_tensor_tensor", [("scr", r), ("colsr",), ("colsx",)], [("ps", bk), ("scr", q)],
                    out=SCR[:, q, :], in0=banks[bk][:], scalar=gain_ap, in1=SCR[:, r, :], op0=ALU.mult, op1=ALU.mult)
                j = ptn("qb")
                act(PT[:, j, :], SCR[:, q, :], AF.Copy, [("scr", q)], [("PT", j)])

                def stage3():
                    pb = bank("perm", [5])
                    mm(banks[pb][:], perm_bf, PT[:, j, :], True, True, [("PT", j), ("cb", 2)], [("ps", pb)])
                    t1 = scr("t1")
                    dve("tensor_tensor", [("scr", q), ("rope",)], [("scr", t1)], out=SCR[:, t1, :], in0=SCR[:, q, :],
                        in1=ropeC[:, tsl], op=ALU.mult)
                    t2 = scr("t2")
                    dve("tensor_tensor", [("rope",)], [("scr", t2), ("ps", pb)], out=SCR[:, t2, :], in0=banks[pb][:],
                        in1=ropeS[:, tsl], op=ALU.mult)
                    dve("tensor_tensor", [("scr", t1), ("scr", t2)], dst_tiles, out=dst_ap, in0=SCR[:, t1, :],
                        in1=SCR[:, t2, :], op=ALU.add)
                    return None
                return stage3
            pend.append(stage2)

        def run_pend(pend, keep):
            n = len(pend) - keep
            if n <= 0:
                return
            todo = pend[:n]
            del pend[:n]
            newp = []
            for f in todo:
                nxt = f()
                if nxt is not None:
                    newp.append(nxt)
            pend[:0] = newp

        def phase1(l, mixer):
            rope = (mixer == 0)
            if rope:
                dma("sp", ropeC, rope_d[0], [], [("rope",)] + [("OT", c, tg) for c in (4, 5) for tg in range(NTG)], key="rp0")
                dma("sp", ropeS, rope_d[1], [], [("rope",)] + [("OT", c, tg) for c in (6, 7) for tg in range(NTG)], key="rp1")
            sq_ = w_use()
            skv = w_use()
            qg = col(l, 32 if mixer == 0 else 34)
            kg = colsx[:, l * 9:l * 9 + 1] if mixer == 0 else colsx2[:, l:l + 1]
            dve("memset", [], [("V", t) for t in range(16)] + [x_ for c in (7, 8, 9) for x_ in ATTt(c)],
                ap=Vaug[:, :, :, 64:128], constant=1.0)
            pend = []
            da0, dt0 = ht_dst(0)
            norm_tg(l, 0, 0, da0, dt0)
            for tg in range(NTG):
                hb = tg % 2
                if tg + 1 < NTG:
                    da, dt_ = ht_dst((tg + 1) % 2)
                    norm_tg(l, tg + 1, 0, da, dt_)
                tsl = slice(tg * TG, (tg + 1) * TG)
                for c in range(4):
                    qk_unit(l, tg, hb, sq_, c * 128, qg, rope, ATT[:, c, tsl], [("ATT", c, tg)], pend)
                    run_pend(pend, 1)
                qk_unit(l, tg, hb, skv, 0, kg, rope, KT[:, tsl], [("ATT", 4, tg)], pend)
                run_pend(pend, 1)
                wv = W[:, skv, :].rearrange("p (k c) -> p k c", k=8)
                vb = bank("vb", [5])
                for t in range(4):
                    for k in range(8):
                        mm(banks[vb][:, t * 128:(t + 1) * 128], HTs[:, hb, k, t * 128:(t + 1) * 128], wv[:, k, 128:256],
                           k == 0, k == 7, [("W", skv), ("HT", hb, k)], [("ps", vb)])
                src = banks[vb][:].rearrange("p (t k d) -> p t k d", t=4, k=2, d=64)
                vt = [("V", tg * 4 + t) for t in range(4)]
                act(Vaug[:, tg * 4:tg * 4 + 4, :, 0:64], src, AF.Copy, [], [("ps", vb)] + vt)
                act(Vaug[:, tg * 4:tg * 4 + 4, :, 128:192], src, AF.Copy, [], [("ps", vb)] + vt)
            while pend:
                run_pend(pend, 0)
            w_release(2)
            allk = ATTt(4)
            for kv in range(2):
                for half in range(2):
                    dma("sp", KTd[half * 64:(half + 1) * 64, kv, :], KT[kv * 64:(kv + 1) * 64, :],
                        allk, [("KTd", kv, half)] + ATTt(5 + kv), key=f"kd{kv}{half}")

        def attn_global(l):
            steps = []
            u = 0
            for kv in range(2):
                for tg in range(NTG):
                    for pr in range(2):
                        for sbk in range(16):
                            steps.append((u, kv, tg, pr, sbk))
                        u += 1
            pend = []

            def do_pv(st, s):
                (u, kv, tg, pr, sbk) = st
                c = kv * 2 + pr
                tsl = slice(tg * TG, (tg + 1) * TG)
                for hh in range(2):
                    ab = 4 + 2 * (u % 2) + hh
                    pt = (s % 3) * 2 + hh
                    lhsT = Vaug[:, sbk, kv, 0:128] if hh == 0 else Vaug[:, sbk, kv, 64:192]
                    mm(banks[ab][:], lhsT, PT[:, pt, :], sbk == 0, sbk == 15, [("PT", pt), ("V", sbk)], [("ps", ab)])
                    if sbk == 15:
                        orow = slice(hh * 64, hh * 64 + 64)
                        drow = slice(64 - hh * 64, 128 - hh * 64)
                        r = scr()
                        dve("reciprocal", [], [("scr", r), ("ps", ab)], out=SCR[orow, r, :], in_=banks[ab][drow, :])
                        dve("tensor_tensor", [("scr", r)], [("ps", ab), ("OT", c, tg)], out=OT[orow, c, tsl],
                            in0=banks[ab][orow, :], in1=SCR[orow, r, :], op=ALU.mult)

            for s, st in enumerate(steps):
                (u, kv, tg, pr, sbk) = st
                c = kv * 2 + pr
                for hh in range(2):
                    lt = (s % 2) * 2 + hh
                    rows = slice(hh * 64, hh * 64 + 64)
                    mm(banks[lt][:], KTd[rows, kv, sbk * 128:(sbk + 1) * 128], ATT[rows, c, tg * TG:(tg + 1) * TG],
                       True, True, [("ATT", c, tg), ("KTd", kv, hh)], [("ps", lt)], tp=(hh * 64, 0))
                for hh in range(2):
                    lt = (s % 2) * 2 + hh
                    pt = (s % 3) * 2 + hh
                    act(PT[:, pt, :], banks[lt][:], AF.Exp, [], [("ps", lt), ("PT", pt)])
                pend.append((st, s))
                if len(pend) > 1:
                    do_pv(*pend.pop(0))
            while pend:
                do_pv(*pend.pop(0))

        def attn_window(l):
            allht = [("HT", hb, c) for hb in range(2) for c in range(8)]
            dma("sp", biasm, bias_d, [], allht + [("biasm",)], key="bm0")
            mi = scr()
            dma("sp", SCR[:, mi, 0:384], mask_d, [], [("scr", mi)], key="bm1")
            for h in range(8):
                dve("tensor_tensor", [("scr", mi), ("biasm",)], [("biasm", h)] + ([("biasm",)] if h == 7 else []),
                    out=biasm[:, h, :], in0=biasm[:, h, :], in1=SCR[:, mi, 0:384], op=ALU.add)
            PTB = PT[:].rearrange("p a b -> p (a b)")[:, 0:8 * 384].rearrange("p (r q) -> p r q", r=8)
            allpt = [("PT", i) for i in range(6)]
            allptb = [("PTB", i) for i in range(8)]
            P.add("act", None, [], allpt, nostate=True)
            normq = []
            bstep = 0

            def emit_norm(ab, tg, h, c, orow, drow):
                tsl = slice(tg * TG, (tg + 1) * TG)
                r1 = scr()
                act(SCR[drow, r1, :], banks[ab][drow, :], AF.Ln, [("colsx",)], [("scr", r1), ("ps", ab)],
                    bias=colsx[drow, l * 9 + 1 + h:l * 9 + 2 + h], scale=1.0)
                act(SCR[drow, r1, :], SCR[drow, r1, :], AF.Exp, [("scr", r1)], [("scr", r1)], scale=-1.0)
                r2 = scr()
                dve("tensor_copy", [("scr", r1)], [("scr", r2)], out=SCR[orow, r2, :], in_=SCR[drow, r1, :])
                dve("tensor_tensor", [("scr", r2)], [("ps", ab), ("OT", c, tg)], out=OT[orow, c, tsl],
                    in0=banks[ab][orow, :], in1=SCR[orow, r2, :], op=ALU.mult)

            R = 4
            DEF = 3
            for hp in range(4):
                c = 4 + hp
                kv = hp // 2
                for j in range(16 + DEF):
                    i = j - DEF
                    if i >= 0:
                        contrib = [jj for jj in (i - 1, i, i + 1) if 0 <= jj < 16]
                        for hh in range(2):
                            h = 2 * hp + hh
                            ab = 4 + ((i // 4) % 2) * 2 + hh
                            vsl = slice(0, 128) if hh == 0 else slice(64, 192)
                            for n_, jj in enumerate(contrib):
                                b = i - jj + 1
                                sl = hh * R + jj % R
                                mm(banks[ab][:, (i % 4) * 128:(i % 4 + 1) * 128], Vaug[:, jj, kv, vsl],
                                   PTB[:, sl, b * 128:(b + 1) * 128], n_ == 0, n_ == len(contrib) - 1,
                                   [("PTB", sl), ("V", jj)], [("ps", ab)])
                            if i % 4 == 3:
                                orow = slice(hh * 64, hh * 64 + 64)
                                drow = slice(64 - hh * 64, 128 - hh * 64)
                                normq.append((bstep + 2, ab, i // 4, h, c, orow, drow))
                    if j < 16:
                        lo = max(j - 1, 0)
                        hi = min(j + 1, 15)
                        w = (hi - lo + 1) * 128
                        off = (lo - (j - 1)) * 128
                        tgs = sorted(set((b * 128) // TG for b in range(lo, hi + 1)))
                        lts = []
                        for hh in range(2):
                            rows = slice(hh * 64, hh * 64 + 64)
                            lt = (bstep % 2) * 2 + hh
                            lts.append(lt)
                            mm(banks[lt][:, 0:w], KTd[rows, kv, j * 128:(j + 1) * 128], ATT[rows, hp, lo * 128:(hi + 1) * 128],
                               True, True, [("ATT", hp, t_) for t_ in tgs] + [("KTd", kv, hh)], [("ps", lt)], tp=(hh * 64, 0))
                        for hh in range(2):
                            h = 2 * hp + hh
                            lt = lts[hh]
                            tm = scr()
                            dve("tensor_tensor", [("biasm", h)], [("scr", tm), ("ps", lt)], out=SCR[:, tm, 0:w],
                                in0=banks[lt][:, 0:w], in1=biasm[:, h, off:off + w], op=ALU.add)
                            sl = hh * R + j % R
                            act(PTB[:, sl, off:off + w], SCR[:, tm, 0:w], AF.Exp, [("scr", tm)], [("PTB", sl)])
                    bstep += 1
                    while normq and normq[0][0] <= bstep:
                        emit_norm(*normq.pop(0)[1:])
            while normq:
                emit_norm(*normq.pop(0)[1:])
            P.add("act", None, [], allptb, nostate=True)


        def phase3(l):
            for cg in range(2):
                sga = w_use()
                sgb = w_use()
                swo = w_use()
                wga = W[:, sga, :].rearrange("p (k c) -> p k c", k=8)
                wgb = W[:, sgb, :].rearrange("p (k c) -> p k c", k=8)
                woa = W[:, swo, 0:2048].rearrange("p (k c) -> p k c", k=4)
                wob = W[:, swo, 2048:4096].rearrange("p (k c) -> p k c", k=4)
                if cg == 0:
                    da0, dt0 = ht_dst(0)
                    norm_tg(l, 0, 0, da0, dt0)
                for tg in range(NTG):
                    hb = tg % 2
                    late = None
                    if not (cg == 1 and tg == NTG - 1):
                        ntg = (tg + 1) % NTG
                        da, dt_ = ht_dst(ntg % 2)
                        late = norm_tg(l, ntg, 0, da, dt_, pool="norm6", split=6)
                    tsl = slice(tg * TG, (tg + 1) * TG)
                    for cc in range(4):
                        if cc == 2 and late is not None:
                            late()
                        c = cg * 4 + cc
                        bga = bank("p3a", [0, 1])
                        bgb = bank("p3b", [2, 3])
                        bya = bank("p3c", [4])
                        byb = bank("p3d", [5])
                        for k in range(8):
                            mm(banks[bga][:], wga[:, k, cc * 128:(cc + 1) * 128], HTs[:, hb, k, :], k == 0, k == 7,
                               [("W", sga), ("HT", hb, k)], [("ps", bga)])
                        for k in range(8):
                            mm(banks[bgb][:], wgb[:, k, cc * 128:(cc + 1) * 128], HTs[:, hb, k, :], k == 0, k == 7,
                               [("W", sgb), ("HT", hb, k)], [("ps", bgb)])
                        for k in range(4):
                            mm(banks[bya][:], woa[:, k, cc * 128:(cc + 1) * 128], OT[:, k, tsl], k == 0, k == 3,
                               [("W", swo), ("OT", k, tg)], [("ps", bya)])
                        for k in range(4):
                            mm(banks[byb][:], wob[:, k, cc * 128:(cc + 1) * 128], OT[:, 4 + k, tsl], k == 0, k == 3,
                               [("W", swo), ("OT", 4 + k, tg)], [("ps", byb)])
                        ra = scr()
                        act(SCR[:, ra, :], banks[bga][:], AF.Sigmoid, [("colsr",)], [("ps", bga), ("scr", ra)],
                            bias=col(l, 16 + c))
                        rb = scr()
                        act(SCR[:, rb, :], banks[bgb][:], AF.Sigmoid, [("colsr",)], [("ps", bgb), ("scr", rb)],
                            bias=col(l, 24 + c))
                        dve("tensor_tensor", [("scr", ra)], [("scr", ra), ("ps", bya)], out=SCR[:, ra, :],
                            in0=banks[bya][:], in1=SCR[:, ra, :], op=ALU.mult)
                        dve("tensor_tensor", [("scr", rb)], [("scr", rb), ("ps", byb)], out=SCR[:, rb, :],
                            in0=banks[byb][:], in1=SCR[:, rb, :], op=ALU.mult)
                        dve("tensor_tensor", [("scr", ra), ("scr", rb)], [("ATT", c, tg)], out=ATT[:, c, tsl],
                            in0=SCR[:, ra, :], in1=SCR[:, rb, :], op=ALU.add)
                w_release(3)
            for cg in range(2):
                so = w_use()
                wo = W[:, so, :].rearrange("p (k c) -> p k c", k=8)
                for cc in range(4):
                    c = cg * 4 + cc
                    for tg in range(NTG):
                        tsl = slice(tg * TG, (tg + 1) * TG)
                        bk = bank("p3o", [0, 1, 2, 3])
                        for k in range(8):
                            mm(banks[bk][:], wo[:, k, cc * 128:(cc + 1) * 128], ATT[:, k, tsl], k == 0, k == 7,
                               [("W", so), ("ATT", k, tg)], [("ps", bk)])
                        dve("tensor_tensor", [("XT", c, tg)], [("ps", bk), ("XT", c, tg)], out=XT[:, c, tsl],
                            in0=banks[bk][:], in1=XT[:, c, tsl], op=ALU.add)
                w_release(1)

        def phase4(l):
            for tg in range(NTG):
                tsl = slice(tg * TG, (tg + 1) * TG)
                norm_tg(l, tg, 8, (lambda c, tsl=tsl: ATT[:, c, tsl]), (lambda c, tg=tg: ("ATT", c, tg)))
            for fg in range(4):
                s1 = [w_use(), w_use()]
                s2 = [w_use(), w_use()]
                for fc in range(8):
                    w1 = W[:, s1[fc // 4], :].rearrange("p (k c) -> p k c", k=8)
                    f0 = (fc % 4) * 128
                    for tg in range(NTG):
                        tsl = slice(tg * TG, (tg + 1) * TG)
                        bk = bank("p4u", [0, 1, 2, 3])
                        for k in range(8):
                            mm(banks[bk][:], w1[:, k, f0:f0 + 128], ATT[:, k, tsl], k == 0, k == 7,
                               [("W", s1[fc // 4]), ("ATT", k, tg)], [("ps", bk)])
                        r = scr()
                        act(SCR[:, r, :], banks[bk][:], AF.Square, [], [("ps", bk), ("scr", r)])
                        dve("scalar_tensor_tensor", [("scr", r)], [("ps", bk), ("OT", fc, tg)], out=OT[:, fc, tsl],
                            in0=banks[bk][:], scalar=0.0, in1=SCR[:, r, :], op0=ALU.is_gt, op1=ALU.mult)
                w_release(2)
                last = (fg == 3 and l == nlayers - 1)
                order = [(c, tg) for tg in range(NTG) for c in range(8)] if last else [(c, tg) for c in range(8) for tg in range(NTG)]
                for (c, tg) in order:
                    tsl = slice(tg * TG, (tg + 1) * TG)
                    bk = bank("p4d", [4, 5, 6, 7])
                    for fc in range(8):
                        w2 = W[:, s2[fc // 4], :].rearrange("p (k c) -> p k c", k=4)
                        mm(banks[bk][:], w2[:, fc % 4, c * 128:(c + 1) * 128], OT[:, fc, tsl], fc == 0, fc == 7,
                           [("W", s2[fc // 4]), ("OT", fc, tg)], [("ps", bk)])
                    dve("tensor_tensor", [("XT", c, tg)], [("ps", bk), ("XT", c, tg)], out=XT[:, c, tsl],
                        in0=banks[bk][:], in1=XT[:, c, tsl], op=ALU.add)
                    if last and c == 7 and tg >= 1:
                        emit_out(tg - 1)
                if last:
                    emit_out(NTG - 1)
                w_release(2)

        def emit_out(tg):
            for t in range(tg * 4, tg * 4 + 4):
                s = t % 8
                for half in range(2):
                    bk = bank("po", [0, 1, 2, 3])
                    for cc in range(4):
                        c = half * 4 + cc
                        tr(banks[bk][:, cc * 128:(cc + 1) * 128], XT[:, c, t * 128:(t + 1) * 128],
                           [("XT", c, t // 4), ("ident",)], [("ps", bk)])
                    dst = stage(s)[:, half * 512:(half + 1) * 512]
                    if half == 0:
                        act(dst, banks[bk][:], AF.Copy, [], [("ps", bk), ("ostg", s, half)] + ATTt(s))
                    else:
                        dve("tensor_copy", [], [("ps", bk), ("ostg", s, half)] + ATTt(s), out=dst, in_=banks[bk][:])
                dma("sp", y_d[t * 128:(t + 1) * 128, :], stage(s), [("ostg", s, 0), ("ostg", s, 1)] + ATTt(s),
                    [("y", t)], key=f"y{s}")

        allxt = [("XT", c, tg) for c in range(8) for tg in range(NTG)]
        tap("XT0", XT[:], F32, [128, 8, S], allxt)
        for l in range(nlayers):
            phase1(l, 0)
            if l == 0:
                tap("QA", ATT[:, 0:4, :], BF16, [128, 4, S], [t_ for c in range(4) for t_ in ATTt(c)])
                tap("KA", ATT[:, 4:7, :], BF16, [128, 3, S], [t_ for c in (4, 5, 6) for t_ in ATTt(c)] + [("KTd", kv, h) for kv in range(2) for h in range(2)])
                tap("VA", ATT[:, 7:10, :], BF16, [128, 3, S], [("V", t) for t in range(16)])
            attn_global(l)
            if l == 0:
                tap("OA", OT[:, 0:4, :], BF16, [128, 4, S], [("OT", c, tg) for c in range(4) for tg in range(NTG)])
            phase1(l, 1)
            if l == 0:
                tap("QB", ATT[:, 0:4, :], BF16, [128, 4, S], [t_ for c in range(4) for t_ in ATTt(c)])
                tap("KB", ATT[:, 4:7, :], BF16, [128, 3, S], [t_ for c in (4, 5, 6) for t_ in ATTt(c)] + [("KTd", kv, h) for kv in range(2) for h in range(2)])
            attn_window(l)
            if l == 0:
                tap("OB", OT[:, 4:8, :], BF16, [128, 4, S], [("OT", c, tg) for c in range(4, 8) for tg in range(NTG)])
            phase3(l)
            if l == 0:
                tap("MIX", ATT[:, 0:8, :], BF16, [128, 8, S], [t_ for c in range(8) for t_ in ATTt(c)])
                tap("XT1", XT[:], F32, [128, 8, S], allxt)
            phase4(l)
            if l == 0:
                tap("XT2", XT[:], F32, [128, 8, S], allxt)

        P.add("sp", None, [("y", t) for t in range(16)] + [("tap", n_) for n_ in tapouts], [])

        sem_names = P.finalize()
        sems = {k: stack.enter_context(nc.semaphore(k)) for k in sem_names}
        with nc.Block() as block:
            @block.tensor
            def _(e):
                P.emit("pe", e, sems)

            @block.scalar
            def _(e):
                P.emit("act", e, sems)

            @block.vector
            def _(e):
                P.emit("dve", e, sems)

            @block.gpsimd
            def _(e):
                P.emit("pool", e, sems)

            @block.sync
            def _(e):
                P.emit("sp", e, sems)
    return nc


def _t5_bucket(rel):
    nb = 16
    max_exact = 8
    n = np.abs(rel)
    large = max_exact + (np.log(np.maximum(n, 1).astype(np.float32) / max_exact)
                         / math.log(128 / max_exact) * (nb - max_exact)).astype(np.int32)
    large = np.minimum(large, nb - 1)
    return np.where(rel > 0, nb, 0) + np.where(n < max_exact, n, large)


def _host_tables(inputs):
    f32 = np.float32
    cols = np.zeros((128, NL * NCL), f32)
    p = np.arange(128)
    for l in range(NL):
        b = l * NCL
        cols[:, b + 0:b + 8] = np.asarray(inputs["norm_mix"][l], f32).reshape(8, 128).T
        cols[:, b + 8:b + 16] = np.asarray(inputs["norm_mlp"][l], f32).reshape(8, 128).T
        cols[:, b + 16:b + 32] = np.asarray(inputs["b_gate"][l], f32).reshape(16, 128).T
        cols[:, b + 32] = np.asarray(inputs["qn_a"][l], f32)[p % 64]
        cols[:, b + 33] = np.asarray(inputs["kn_a"][l], f32)[p % 64]
        cols[:, b + 34] = np.asarray(inputs["qn_b"][l], f32)[p % 64]
        cols[:, b + 35] = np.asarray(inputs["kn_b"][l], f32)[p % 64]
        cols[:, b + 36:b + 44] = np.asarray(inputs["sink_b"][l], f32)[None, :]
    t = np.arange(S)
    row = (t // 64).astype(f32)
    colp = (t % 64).astype(f32)
    inv_freq = (10000.0 ** (-np.arange(16, dtype=f32) / 16)).astype(f32)
    d = p % 64
    half = d // 32
    j = d % 32
    fidx = j % 16
    pos = np.where(half[:, None] == 0, row[None, :], colp[None, :]).astype(f32)
    ang = (pos * inv_freq[fidx][:, None]).astype(f32)
    C = np.cos(ang).astype(f32)
    Sg = np.sin(ang).astype(f32)
    Sp = np.where((j < 16)[:, None], -Sg, Sg).astype(f32)
    rope = np.stack([C, Sp]).astype(f32)
    s_ = np.arange(128)[:, None]
    q_ = np.arange(384)[None, :]
    rel = s_ + 128 - q_
    bucket = _t5_bucket(rel)
    rb = np.asarray(inputs["rel_bias"], f32)
    biasT = np.ascontiguousarray(np.transpose(rb[bucket], (0, 2, 1))).astype(f32)
    maskT = np.where(np.abs(rel) <= 128, 0.0, MASKVAL).astype(f32)
    ident = np.eye(128, dtype=f32)
    ones = np.ones((128, 128), f32)
    blk = np.zeros((128, 128), f32)
    blk[:64, :64] = 1.0
    blk[64:, 64:] = 1.0
    perm = np.zeros((128, 128), f32)
    m = np.arange(128)
    pm = np.where((m % 32) < 16, m + 16, m - 16)
    perm[pm, m] = 1.0
    cst = np.stack([ident, ones, blk, perm]).astype(f32)
    return cols, rope, biasT, maskT, cst


_NC_CACHE = {}


def kernel(**inputs):
    f32 = np.float32
    x = np.asarray(inputs["x"], f32)
    cols, rope, biasT, maskT, cst = _host_tables(inputs)
    shared = {
        "w_in": np.ascontiguousarray(np.asarray(inputs["w_in"], f32)),
        "w_o_a": np.ascontiguousarray(np.asarray(inputs["w_o_a"], f32)),
        "w_o_b": np.ascontiguousarray(np.asarray(inputs["w_o_b"], f32)),
        "w_out": np.ascontiguousarray(np.asarray(inputs["w_out"], f32)),
        "w_mlp1": np.ascontiguousarray(np.asarray(inputs["w_mlp1"], f32)),
        "w_mlp2": np.ascontiguousarray(np.asarray(inputs["w_mlp2"], f32)),
        "cols": cols, "rope": rope, "biasT": biasT, "maskT": maskT, "cst": cst,
    }
    if "nc" not in _NC_CACHE:
        _NC_CACHE["nc"] = build(NL)
    nc = _NC_CACHE["nc"]
    in_maps = []
    for b in range(8):
        m = dict(shared)
        m["x"] = np.ascontiguousarray(x[b])
        in_maps.append(m)
    res = run_bass_kernel_spmd(nc, in_maps, core_ids=list(range(8)))
    out = np.stack([np.asarray(r["y"], f32) for r in res.results], axis=0)
    return out
```

```python
import math
from contextlib import ExitStack

import numpy as np
import concourse.bass as bass
import concourse.mybir as mybir
from concourse.bass_utils import run_bass_kernel_spmd

F32 = mybir.dt.float32
BF16 = mybir.dt.bfloat16
ALU = mybir.AluOpType
AF = mybir.ActivationFunctionType

S = 2048
D = 1024
NL = 2
NTG = 4
TG = 512
EPS = 1e-6
NSLOT = 4
NCL = 48
MASKVAL = -30000.0


class Prog:
    def __init__(self):
        self.ops = []
        self.last_w = {}
        self.readers = {}

    def add(self, eng, fn, reads=(), writes=(), dma=None, nostate=False):
        i = len(self.ops)
        raw = set()
        other = set()
        for t in reads:
            w = self.last_w.get(t)
            if w is not None:
                raw.add(w)
        for t in writes:
            w = self.last_w.get(t)
            if w is not None:
                other.add(w)
            for r in self.readers.get(t, ()):
                other.add(r)
        deps = set()
        for j in raw | other:
            oj = self.ops[j]
            if oj["dma"] is None and dma is None and oj["eng"] == eng:
                if eng == "pe":
                    continue
            deps.add(j)
        self.ops.append(dict(eng=eng, fn=fn, deps=sorted(deps), dma=dma, signal=False))
        if nostate:
            return i
        for t in writes:
            self.last_w[t] = i
            self.readers[t] = []
        for t in reads:
            self.readers.setdefault(t, []).append(i)
        return i

    def finalize(self):
        for op in self.ops:
            for j in op["deps"]:
                self.ops[j]["signal"] = True
        cnt = {}
        for op in self.ops:
            if op["dma"] is not None:
                k = "d_" + op["dma"]
                cnt[k] = cnt.get(k, 0) + 16
                op["sem"] = k
                op["val"] = cnt[k]
            elif op["signal"]:
                k = "e_" + op["eng"]
                cnt[k] = cnt.get(k, 0) + 1
                op["sem"] = k
                op["val"] = cnt[k]
        return sorted(cnt.keys())

    def emit(self, eng_name, eng, sems):
        waited = {}
        for op in self.ops:
            if op["eng"] != eng_name:
                continue
            need = {}
            for j in op["deps"]:
                oj = self.ops[j]
                need[oj["sem"]] = max(need.get(oj["sem"], 0), oj["val"])
            for k, v in need.items():
                if waited.get(k, 0) < v:
                    eng.wait_ge(sems[k], v)
                    waited[k] = v
            if op["fn"] is None:
                continue
            ins = op["fn"](eng)
            if op["dma"] is not None:
                ins.then_inc(sems[op["sem"]], 16)
            elif op["signal"]:
                ins.then_inc(sems[op["sem"]], 1)


def build(nlayers=NL, taps=()):
    nc = bass.Bass("TRN2", target_bir_lowering=False)
    P = Prog()

    def dram(name, shape, kind="ExternalInput"):
        return nc.dram_tensor(name, list(shape), F32, kind=kind).ap()

    x_d = dram("x", [S, D])
    w_in_d = dram("w_in", [NL, D, 3584])
    w_oa_d = dram("w_o_a", [NL, 512, D])
    w_ob_d = dram("w_o_b", [NL, 512, D])
    w_out_d = dram("w_out", [NL, D, D])
    w1_d = dram("w_mlp1", [NL, D, 4096])
    w2_d = dram("w_mlp2", [NL, 4096, D])
    cols_d = dram("cols", [128, NL * NCL])
    rope_d = dram("rope", [2, 128, S])
    bias_d = dram("biasT", [128, 8, 384])
    mask_d = dram("maskT", [128, 384])
    cst_d = dram("cst", [4, 128, 128])
    y_d = dram("y", [S, D], kind="ExternalOutput")

    stack = ExitStack()
    with stack:
        def sb(name, shape, dt):
            return stack.enter_context(nc.sbuf_tensor(name, list(shape), dt))

        XT = sb("XT", [128, 8, S], F32)
        HTs = sb("HTs", [128, 2, 8, TG], BF16)
        ATT = sb("ATT", [128, 10, S], BF16)
        OT = sb("OT", [128, 8, S], BF16)
        PT = sb("PT", [128, 6, TG], BF16)
        W = sb("W", [128, NSLOT, 4096], BF16)
        SCR = sb("SCR", [128, 7, TG], F32)
        ident = sb("ident", [128, 128], F32)
        cb = sb("cb", [128, 3, 128], BF16)
        colsr = sb("colsr", [128, NL * NCL], F32)
        colsx = sb("colsx", [128, NL * 9], F32)
        PS = stack.enter_context(nc.psum_tensor("ps", [128, 8, TG], F32))
        banks = [PS[:, i, :] for i in range(8)]

        ones_bf = cb[:, 0, :]
        blk_bf = cb[:, 1, :]
        perm_bf = cb[:, 2, :]

        KT = ATT[:, 4, :]
        KTd = ATT[:, 5:7, :]
        Vaug = ATT[:, 7:10, :].rearrange("p a b -> p (a b)").rearrange("p (t k d) -> p t k d", t=16, k=2, d=192)
        OTf = OT[:, 4:8, :].rearrange("p a b -> p (a b)").bitcast(F32)
        ropeC = OTf[:, 0:S]
        ropeS = OTf[:, S:2 * S]
        HTf = HTs[:].rearrange("p a k t -> p (a k t)").bitcast(F32)
        biasm = HTf[:, 0:8 * 384].rearrange("p (h q) -> p h q", h=8)

        def stage(s):
            return ATT[:, s, :].bitcast(F32)

        def ATTt(c):
            return [("ATT", c, tg) for tg in range(NTG)]

        def mm(out, lhsT, rhs, start, stop, reads, writes, tp=None):
            def fn(e):
                if tp is None:
                    return e.matmul(out, lhsT, rhs, start=start, stop=stop)
                return e.matmul(out, lhsT, rhs, start=start, stop=stop, tile_position=tp)
            P.add("pe", fn, reads, writes)

        def tr(out, in_, reads, writes):
            P.add("pe", lambda e: e.transpose(out, in_, ident[:]), reads, writes)

        def act(out, in_, func, reads, writes, bias=None, scale=None):
            def fn(e):
                kw = {}
                if bias is not None:
                    kw["bias"] = bias
                if scale is not None:
                    kw["scale"] = scale
                return e.activation(out=out, in_=in_, func=func, **kw)
            P.add("act", fn, reads, writes)

        def dve(method, reads, writes, **kw):
            P.add("dve", lambda e: getattr(e, method)(**kw), reads, writes)

        def dma(eng, out, in_, reads, writes, key):
            P.add(eng, lambda e: e.dma_start(out=out, in_=in_), reads, writes, dma=key)

        tapouts = {}

        def tap(name, ap, dt, shape, reads):
            if name not in taps:
                return
            d = nc.dram_tensor("tap_" + name, list(shape), dt, kind="ExternalOutput").ap()
            dma("sp", d, ap, reads, [("tap", name)], key="tap_" + name)
            tapouts[name] = True

        ringctr = {}
        SCR_POOLS = {"gen": [0, 1, 2, 3, 4, 5, 6], "rs": [0, 1], "q": [2, 3, 4], "t2": [5, 6]}
        PT_POOLS = {"norm": [0, 1], "sq": [2, 3], "qb": [4, 5], "norm6": [0, 1, 2, 3, 4, 5]}

        def scr(pool="gen"):
            k = "scr_" + pool
            i = ringctr.get(k, 0)
            ringctr[k] = i + 1
            lst = SCR_POOLS[pool]
            return lst[i % len(lst)]

        def ptn(pool):
            k = "pt_" + pool
            i = ringctr.get(k, 0)
            ringctr[k] = i + 1
            lst = PT_POOLS[pool]
            return lst[i % len(lst)]

        bankctr = {}

        def bank(pool, lst):
            i = bankctr.get(pool, 0)
            bankctr[pool] = i + 1
            return lst[i % len(lst)]

        wplan = []

        def k8(cols):
            return lambda sl: sl.rearrange("p (k c) -> p k c", k=8)[:, :, 0:cols]

        def src_k8(mat, c0, cols):
            return mat[:, c0:c0 + cols].rearrange("(k p) c -> p k c", p=128)

        for l in range(nlayers):
            wplan.append([(k8(512), src_k8(w_in_d[l], 0, 512))])
            wplan.append([(k8(256), src_k8(w_in_d[l], 512, 256))])
            wplan.append([(k8(512), src_k8(w_in_d[l], 768, 512))])
            wplan.append([(k8(256), src_k8(w_in_d[l], 1280, 256))])
            for cg in range(2):
                wplan.append([(k8(512), src_k8(w_in_d[l], 1536 + cg * 512, 512))])
                wplan.append([(k8(512), src_k8(w_in_d[l], 2560 + cg * 512, 512))])
                wplan.append([
                    (lambda sl: sl[:, 0:2048].rearrange("p (k c) -> p k c", k=4),
                     w_oa_d[l][:, cg * 512:(cg + 1) * 512].rearrange("(k p) c -> p k c", p=128)),
                    (lambda sl: sl[:, 2048:4096].rearrange("p (k c) -> p k c", k=4),
                     w_ob_d[l][:, cg * 512:(cg + 1) * 512].rearrange("(k p) c -> p k c", p=128)),
                ])
            for cg in range(2):
                wplan.append([(k8(512), src_k8(w_out_d[l], cg * 512, 512))])
            for fg in range(4):
                wplan.append([(k8(512), src_k8(w1_d[l], fg * 1024, 512))])
                wplan.append([(k8(512), src_k8(w1_d[l], fg * 1024 + 512, 512))])
                for hf in range(2):
                    f0 = fg * 1024 + hf * 512
                    wplan.append([(lambda sl: sl.rearrange("p (k c) -> p k c", k=4),
                                   w2_d[l][f0:f0 + 512, :].rearrange("(k p) c -> p k c", p=128))])

        wst = {"next_load": 0, "released": 0, "next_use": 0}

        def w_pump():
            while wst["next_load"] < len(wplan) and wst["next_load"] < wst["released"] + NSLOT:
                n = wst["next_load"]
                s = n % NSLOT
                for (dstf, src) in wplan[n]:
                    dma("pool", dstf(W[:, s, :]), src, [], [("W", s)], key=f"w{s}")
                wst["next_load"] += 1

        def w_use():
            n = wst["next_use"]
            wst["next_use"] += 1
            assert n < wst["next_load"], "weight not loaded (ring too small for this group)"
            return n % NSLOT

        def w_release(k=1):
            wst["released"] += k
            w_pump()

        dma("sp", ident[:], cst_d[0], [], [("ident",)], key="ci")
        for i in range(3):
            dma("pool", cb[:, i, :], cst_d[1 + i], [], [("cb", i)], key=f"cb{i}")
        dma("sp", colsr[:], cols_d, [], [("colsr",)], key="cc")
        w_pump()

        def col(l, j):
            return colsr[:, l * NCL + j:l * NCL + j + 1]
        colsx2 = sb("colsx2", [128, NL], F32)
        colsx3 = sb("colsx3", [128, NL], F32)
        epsc = sb("epsc", [128, 2], F32)
        dve("memset", [], [("epsc",)], ap=epsc[:, 0:1], constant=EPS)
        dve("memset", [], [("epsc",)], ap=epsc[:, 1:2], constant=64.0 * EPS)
        for l in range(nlayers):
            dve("tensor_scalar", [("colsr",)], [("colsx",)], out=colsx[:, l * 9:l * 9 + 1], in0=col(l, 33),
                scalar1=8.0, scalar2=None, op0=ALU.mult)
            dve("tensor_scalar", [("colsr",)], [("colsx",)], out=colsx2[:, l:l + 1], in0=col(l, 35),
                scalar1=8.0, scalar2=None, op0=ALU.mult)
            dve("tensor_scalar", [("colsr",)], [("colsx",)], out=colsx3[:, l:l + 1], in0=col(l, 45),
                scalar1=8.0, scalar2=None, op0=ALU.mult)
            act(colsx[:, l * 9 + 1:l * 9 + 9], colsr[:, l * NCL + 36:l * NCL + 44], AF.Exp, [("colsr",)], [("colsx",)])

        for tg in range(NTG):
            for t in range(4):
                tile = tg * 4 + t
                s = tile % 8
                dma("sp", stage(s), x_d[tile * 128:(tile + 1) * 128, :], [], ATTt(s), key=f"x{s}")
            for c in range(8):
                bk = bank("p0", [0, 1, 2, 3])
                for t in range(4):
                    s = (tg * 4 + t) % 8
                    tr(banks[bk][:, t * 128:(t + 1) * 128], stage(s)[:, c * 128:(c + 1) * 128],
                       ATTt(s) + [("ident",)], [("ps", bk)])
                dst = XT[:, c, tg * TG:(tg + 1) * TG]
                if c % 2 == 0:
                    act(dst, banks[bk][:], AF.Copy, [], [("ps", bk), ("XT", c, tg)])
                else:
                    dve("tensor_copy", [], [("ps", bk), ("XT", c, tg)], out=dst, in_=banks[bk][:])

        def norm_tg(l, tg, gbase, dst_ap, dst_tile, pool="norm", split=None):
            tsl = slice(tg * TG, (tg + 1) * TG)
            bk = bank("nss", [6, 7])
            sq_idx = {}

            def square(c):
                i = ptn(pool)
                sq_idx[c] = i
                act(PT[:, i, :], XT[:, c, tsl], AF.Square, [("XT", c, tg)], [("PT", i)])

            def rest():
                for c in range(8):
                    if c not in sq_idx:
                        square(c)
                    i = sq_idx[c]
                    mm(banks[bk][:], ones_bf, PT[:, i, :], c == 0, c == 7, [("PT", i), ("cb", 0)], [("ps", bk)])
                r = scr("rs")
                act(SCR[:, r, :], banks[bk][:], AF.Ln, [("epsc",)], [("scr", r), ("ps", bk)], bias=epsc[:, 0:1], scale=1.0 / D)
                act(SCR[:, r, :], SCR[:, r, :], AF.Exp, [("scr", r)], [("scr", r)], scale=-0.5)
                for c in range(8):
                    dve("scalar_tensor_tensor", [("XT", c, tg), ("scr", r), ("colsr",)], [dst_tile(c)],
                        out=dst_ap(c), in0=XT[:, c, tsl], scalar=col(l, gbase + c), in1=SCR[:, r, :],
                        op0=ALU.mult, op1=ALU.mult)

            if split is None:
                rest()
                return None
            for c in range(split):
                square(c)
            return rest

        def ht_dst(hb):
            return (lambda c: HTs[:, hb, c, :]), (lambda c: ("HT", hb, c))

        def qk_unit(l, tg, hb, slot, wc0, gain_ap, rope, dst_ap, dst_tiles, pend, gperm_ap=None):
            tsl = slice(tg * TG, (tg + 1) * TG)
            wv = W[:, slot, :].rearrange("p (k c) -> p k c", k=8)
            bk = bank("main", [0, 1, 2])
            for k in range(8):
                mm(banks[bk][:], wv[:, k, wc0:wc0 + 128], HTs[:, hb, k, :], k == 0, k == 7,
                   [("W", slot), ("HT", hb, k)], [("ps", bk)])
            i = ptn("sq")
            act(PT[:, i, :], banks[bk][:], AF.Square, [], [("ps", bk), ("PT", i)])
            if rope:
                j = ptn("qb")
                act(PT[:, j, :], banks[bk][:], AF.Copy, [], [("ps", bk), ("PT", j)])
                t1 = scr("q")
                dve("scalar_tensor_tensor", [("rope",), ("colsr",), ("colsx",)], [("ps", bk), ("scr", t1)],
                    out=SCR[:, t1, :], in0=banks[bk][:], scalar=gain_ap, in1=ropeC[:, tsl], op0=ALU.mult, op1=ALU.mult)

            def stage2():
                sb_ = bank("hss", [3])
                mm(banks[sb_][:], blk_bf, PT[:, i, :], True, True, [("PT", i), ("cb", 1)], [("ps", sb_)])
                if rope:
                    pb = bank("perm", [4])
                    mm(banks[pb][:], perm_bf, PT[:, j, :], True, True, [("PT", j), ("cb", 2)], [("ps", pb)])
                r = scr("rs")
                act(SCR[:, r, :], banks[sb_][:], AF.Ln, [("epsc",)], [("scr", r), ("ps", sb_)], bias=epsc[:, 1:2], scale=1.0)
                act(SCR[:, r, :], SCR[:, r, :], AF.Exp, [("scr", r)], [("scr", r)], scale=-0.5)
                if not rope:
                    dve("scalar_tensor_tensor", [("scr", r), ("colsr",), ("colsx",)], [("ps", bk)] + dst_tiles,
                        out=dst_ap, in0=banks[bk][:], scalar=gain_ap, in1=SCR[:, r, :], op0=ALU.mult, op1=ALU.mult)
                    return None
                t2 = scr("t2")
                dve("scalar_tensor_tensor", [("rope",), ("colsr",), ("colsx",)], [("ps", pb), ("scr", t2)],
                    out=SCR[:, t2, :], in0=banks[pb][:], scalar=gperm_ap, in1=ropeS[:, tsl], op0=ALU.mult, op1=ALU.mult)
                dve("tensor_tensor", [("scr", t1), ("scr", t2)], [("scr", t1)], out=SCR[:, t1, :], in0=SCR[:, t1, :],
                    in1=SCR[:, t2, :], op=ALU.add)
                dve("tensor_tensor", [("scr", t1), ("scr", r)], dst_tiles, out=dst_ap, in0=SCR[:, t1, :],
                    in1=SCR[:, r, :], op=ALU.mult)
                return None
            pend.append(stage2)

        def run_pend(pend, keep):
            n = len(pend) - keep
            if n <= 0:
                return
            todo = pend[:n]
            del pend[:n]
            newp = []
            for f in todo:
                nxt = f()
                if nxt is not None:
                    newp.append(nxt)
            pend[:0] = newp

        def phase1(l, mixer):
            rope = (mixer == 0)
            if rope:
                dma("sp", ropeC, rope_d[0], [], [("rope",)] + [("OT", c, tg) for c in (4, 5) for tg in range(NTG)], key="rp0")
                dma("sp", ropeS, rope_d[1], [], [("rope",)] + [("OT", c, tg) for c in (6, 7) for tg in range(NTG)], key="rp1")
            sq_ = w_use()
            skv = w_use()
            qg = col(l, 32 if mixer == 0 else 34)
            kg = colsx[:, l * 9:l * 9 + 1] if mixer == 0 else colsx2[:, l:l + 1]
            qgp = col(l, 44)
            kgp = colsx3[:, l:l + 1]
            dve("memset", [], [("V", t) for t in range(16)] + [x_ for c in (7, 8, 9) for x_ in ATTt(c)],
                ap=Vaug[:, :, :, 64:128], constant=1.0)
            pend = []
            da0, dt0 = ht_dst(0)
            norm_tg(l, 0, 0, da0, dt0)
            for tg in range(NTG):
                hb = tg % 2
                if tg + 1 < NTG:
                    da, dt_ = ht_dst((tg + 1) % 2)
                    norm_tg(l, tg + 1, 0, da, dt_)
                tsl = slice(tg * TG, (tg + 1) * TG)
                for c in range(4):
                    qk_unit(l, tg, hb, sq_, c * 128, qg, rope, ATT[:, c, tsl], [("ATT", c, tg)], pend, gperm_ap=qgp)
                    run_pend(pend, 1)
                qk_unit(l, tg, hb, skv, 0, kg, rope, KT[:, tsl], [("ATT", 4, tg)], pend, gperm_ap=kgp)
                run_pend(pend, 1)
                wv = W[:, skv, :].rearrange("p (k c) -> p k c", k=8)
                vb = bank("vb", [5])
                for t in range(4):
                    for k in range(8):
                        mm(banks[vb][:, t * 128:(t + 1) * 128], HTs[:, hb, k, t * 128:(t + 1) * 128], wv[:, k, 128:256],
                           k == 0, k == 7, [("W", skv), ("HT", hb, k)], [("ps", vb)])
                src = banks[vb][:].rearrange("p (t k d) -> p t k d", t=4, k=2, d=64)
                vt = [("V", tg * 4 + t) for t in range(4)]
                act(Vaug[:, tg * 4:tg * 4 + 4, :, 0:64], src, AF.Copy, [], [("ps", vb)] + vt)
                act(Vaug[:, tg * 4:tg * 4 + 4, :, 128:192], src, AF.Copy, [], [("ps", vb)] + vt)
            while pend:
                run_pend(pend, 0)
            w_release(2)
            allk = ATTt(4)
            for kv in range(2):
                for half in range(2):
                    dma("sp", KTd[half * 64:(half + 1) * 64, kv, :], KT[kv * 64:(kv + 1) * 64, :],
                        allk, [("KTd", kv, half)] + ATTt(5 + kv), key=f"kd{kv}{half}")

        def attn_global(l):
            steps = []
            u = 0
            for kv in range(2):
                for tg in range(NTG):
                    for pr in range(2):
                        for sbk in range(16):
                            steps.append((u, kv, tg, pr, sbk))
                        u += 1
            pend = []

            def do_pv(st, s):
                (u, kv, tg, pr, sbk) = st
                c = kv * 2 + pr
                tsl = slice(tg * TG, (tg + 1) * TG)
                for hh in range(2):
                    ab = 4 + 2 * (u % 2) + hh
                    pt = (s % 3) * 2 + hh
                    lhsT = Vaug[:, sbk, kv, 0:128] if hh == 0 else Vaug[:, sbk, kv, 64:192]
                    mm(banks[ab][:], lhsT, PT[:, pt, :], sbk == 0, sbk == 15, [("PT", pt), ("V", sbk)], [("ps", ab)])
                    if sbk == 15:
                        orow = slice(hh * 64, hh * 64 + 64)
                        drow = slice(64 - hh * 64, 128 - hh * 64)
                        r = scr()
                        dve("reciprocal", [], [("scr", r), ("ps", ab)], out=SCR[orow, r, :], in_=banks[ab][drow, :])
                        dve("tensor_tensor", [("scr", r)], [("ps", ab), ("OT", c, tg)], out=OT[orow, c, tsl],
                            in0=banks[ab][orow, :], in1=SCR[orow, r, :], op=ALU.mult)

            for s, st in enumerate(steps):
                (u, kv, tg, pr, sbk) = st
                c = kv * 2 + pr
                for hh in range(2):
                    lt = (s % 2) * 2 + hh
                    rows = slice(hh * 64, hh * 64 + 64)
                    mm(banks[lt][:], KTd[rows, kv, sbk * 128:(sbk + 1) * 128], ATT[rows, c, tg * TG:(tg + 1) * TG],
                       True, True, [("ATT", c, tg), ("KTd", kv, hh)], [("ps", lt)], tp=(hh * 64, 0))
                for hh in range(2):
                    lt = (s % 2) * 2 + hh
                    pt = (s % 3) * 2 + hh
                    act(PT[:, pt, :], banks[lt][:], AF.Exp, [], [("ps", lt), ("PT", pt)])
                pend.append((st, s))
                if len(pend) > 1:
                    do_pv(*pend.pop(0))
            while pend:
                do_pv(*pend.pop(0))

        def attn_window(l):
            allht = [("HT", hb, c) for hb in range(2) for c in range(8)]
            dma("sp", biasm, bias_d, [], allht + [("biasm",)], key="bm0")
            mi = scr()
            dma("sp", SCR[:, mi, 0:384], mask_d, [], [("scr", mi)], key="bm1")
            for h in range(8):
                dve("tensor_tensor", [("scr", mi), ("biasm",)], [("biasm", h)] + ([("biasm",)] if h == 7 else []),
                    out=biasm[:, h, :], in0=biasm[:, h, :], in1=SCR[:, mi, 0:384], op=ALU.add)
            PTB = PT[:].rearrange("p a b -> p (a b)")[:, 0:8 * 384].rearrange("p (r q) -> p r q", r=8)
            allpt = [("PT", i) for i in range(6)]
            allptb = [("PTB", i) for i in range(8)]
            P.add("act", None, [], allpt, nostate=True)
            normq = []
            bstep = 0

            def emit_norm(ab, tg, h, c, orow, drow):
                tsl = slice(tg * TG, (tg + 1) * TG)
                r1 = scr()
                act(SCR[drow, r1, :], banks[ab][drow, :], AF.Ln, [("colsx",)], [("scr", r1), ("ps", ab)],
                    bias=colsx[drow, l * 9 + 1 + h:l * 9 + 2 + h], scale=1.0)
                act(SCR[drow, r1, :], SCR[drow, r1, :], AF.Exp, [("scr", r1)], [("scr", r1)], scale=-1.0)
                r2 = scr()
                dve("tensor_copy", [("scr", r1)], [("scr", r2)], out=SCR[orow, r2, :], in_=SCR[drow, r1, :])
                dve("tensor_tensor", [("scr", r2)], [("ps", ab), ("OT", c, tg)], out=OT[orow, c, tsl],
                    in0=banks[ab][orow, :], in1=SCR[orow, r2, :], op=ALU.mult)

            R = 4
            DEF = 3
            for hp in range(4):
                c = 4 + hp
                kv = hp // 2
                for j in range(16 + DEF):
                    i = j - DEF
                    if i >= 0:
                        contrib = [jj for jj in (i - 1, i, i + 1) if 0 <= jj < 16]
                        for hh in range(2):
                            h = 2 * hp + hh
                            ab = 4 + ((i // 4) % 2) * 2 + hh
                            vsl = slice(0, 128) if hh == 0 else slice(64, 192)
                            for n_, jj in enumerate(contrib):
                                b = i - jj + 1
                                sl = hh * R + jj % R
                                mm(banks[ab][:, (i % 4) * 128:(i % 4 + 1) * 128], Vaug[:, jj, kv, vsl],
                                   PTB[:, sl, b * 128:(b + 1) * 128], n_ == 0, n_ == len(contrib) - 1,
                                   [("PTB", sl), ("V", jj)], [("ps", ab)])
                            if i % 4 == 3:
                                orow = slice(hh * 64, hh * 64 + 64)
                                drow = slice(64 - hh * 64, 128 - hh * 64)
                                normq.append((bstep + 2, ab, i // 4, h, c, orow, drow))
                    if j < 16:
                        lo = max(j - 1, 0)
                        hi = min(j + 1, 15)
                        w = (hi - lo + 1) * 128
                        off = (lo - (j - 1)) * 128
                        tgs = sorted(set((b * 128) // TG for b in range(lo, hi + 1)))
                        lts = []
                        for hh in range(2):
                            rows = slice(hh * 64, hh * 64 + 64)
                            lt = (bstep % 2) * 2 + hh
                            lts.append(lt)
                            mm(banks[lt][:, 0:w], KTd[rows, kv, j * 128:(j + 1) * 128], ATT[rows, hp, lo * 128:(hi + 1) * 128],
                               True, True, [("ATT", hp, t_) for t_ in tgs] + [("KTd", kv, hh)], [("ps", lt)], tp=(hh * 64, 0))
                        for hh in range(2):
                            h = 2 * hp + hh
                            lt = lts[hh]
                            tm = scr()
                            dve("tensor_tensor", [("biasm", h)], [("scr", tm), ("ps", lt)], out=SCR[:, tm, 0:w],
                                in0=banks[lt][:, 0:w], in1=biasm[:, h, off:off + w], op=ALU.add)
                            sl = hh * R + j % R
                            act(PTB[:, sl, off:off + w], SCR[:, tm, 0:w], AF.Exp, [("scr", tm)], [("PTB", sl)])
                    bstep += 1
                    while normq and normq[0][0] <= bstep:
                        emit_norm(*normq.pop(0)[1:])
            while normq:
                emit_norm(*normq.pop(0)[1:])
            P.add("act", None, [], allptb, nostate=True)


        def phase3(l):
            for cg in range(2):
                sga = w_use()
                sgb = w_use()
                swo = w_use()
                wga = W[:, sga, :].rearrange("p (k c) -> p k c", k=8)
                wgb = W[:, sgb, :].rearrange("p (k c) -> p k c", k=8)
                woa = W[:, swo, 0:2048].rearrange("p (k c) -> p k c", k=4)
                wob = W[:, swo, 2048:4096].rearrange("p (k c) -> p k c", k=4)
                if cg == 0:
                    da0, dt0 = ht_dst(0)
                    norm_tg(l, 0, 0, da0, dt0)
                for tg in range(NTG):
                    hb = tg % 2
                    late = None
                    if not (cg == 1 and tg == NTG - 1):
                        ntg = (tg + 1) % NTG
                        da, dt_ = ht_dst(ntg % 2)
                        late = norm_tg(l, ntg, 0, da, dt_, pool="norm6", split=6)
                    tsl = slice(tg * TG, (tg + 1) * TG)
                    for cc in range(4):
                        if cc == 2 and late is not None:
                            late()
                        c = cg * 4 + cc
                        bga = bank("p3a", [0, 1])
                        bgb = bank("p3b", [2, 3])
                        bya = bank("p3c", [4])
                        byb = bank("p3d", [5])
                        for k in range(8):
                            mm(banks[bga][:], wga[:, k, cc * 128:(cc + 1) * 128], HTs[:, hb, k, :], k == 0, k == 7,
                               [("W", sga), ("HT", hb, k)], [("ps", bga)])
                        for k in range(8):
                            mm(banks[bgb][:], wgb[:, k, cc * 128:(cc + 1) * 128], HTs[:, hb, k, :], k == 0, k == 7,
                               [("W", sgb), ("HT", hb, k)], [("ps", bgb)])
                        for k in range(4):
                            mm(banks[bya][:], woa[:, k, cc * 128:(cc + 1) * 128], OT[:, k, tsl], k == 0, k == 3,
                               [("W", swo), ("OT", k, tg)], [("ps", bya)])
                        for k in range(4):
                            mm(banks[byb][:], wob[:, k, cc * 128:(cc + 1) * 128], OT[:, 4 + k, tsl], k == 0, k == 3,
                               [("W", swo), ("OT", 4 + k, tg)], [("ps", byb)])
                        ra = scr()
                        act(SCR[:, ra, :], banks[bga][:], AF.Sigmoid, [("colsr",)], [("ps", bga), ("scr", ra)],
                            bias=col(l, 16 + c))
                        rb = scr()
                        act(SCR[:, rb, :], banks[bgb][:], AF.Sigmoid, [("colsr",)], [("ps", bgb), ("scr", rb)],
                            bias=col(l, 24 + c))
                        dve("tensor_tensor", [("scr", ra)], [("scr", ra), ("ps", bya)], out=SCR[:, ra, :],
                            in0=banks[bya][:], in1=SCR[:, ra, :], op=ALU.mult)
                        dve("tensor_tensor", [("scr", rb)], [("scr", rb), ("ps", byb)], out=SCR[:, rb, :],
                            in0=banks[byb][:], in1=SCR[:, rb, :], op=ALU.mult)
                        dve("tensor_tensor", [("scr", ra), ("scr", rb)], [("ATT", c, tg)], out=ATT[:, c, tsl],
                            in0=SCR[:, ra, :], in1=SCR[:, rb, :], op=ALU.add)
                w_release(3)
            for cg in range(2):
                so = w_use()
                wo = W[:, so, :].rearrange("p (k c) -> p k c", k=8)
                for cc in range(4):
                    c = cg * 4 + cc
                    for tg in range(NTG):
                        tsl = slice(tg * TG, (tg + 1) * TG)
                        bk = bank("p3o", [0, 1, 2, 3])
                        for k in range(8):
                            mm(banks[bk][:], wo[:, k, cc * 128:(cc + 1) * 128], ATT[:, k, tsl], k == 0, k == 7,
                               [("W", so), ("ATT", k, tg)], [("ps", bk)])
                        dve("tensor_tensor", [("XT", c, tg)], [("ps", bk), ("XT", c, tg)], out=XT[:, c, tsl],
                            in0=banks[bk][:], in1=XT[:, c, tsl], op=ALU.add)
                w_release(1)

        def phase4(l):
            for tg in range(NTG):
                tsl = slice(tg * TG, (tg + 1) * TG)
                norm_tg(l, tg, 8, (lambda c, tsl=tsl: ATT[:, c, tsl]), (lambda c, tg=tg: ("ATT", c, tg)))
            for fg in range(4):
                s1 = [w_use(), w_use()]
                s2 = [w_use(), w_use()]
                for fc in range(8):
                    w1 = W[:, s1[fc // 4], :].rearrange("p (k c) -> p k c", k=8)
                    f0 = (fc % 4) * 128
                    for tg in range(NTG):
                        tsl = slice(tg * TG, (tg + 1) * TG)
                        bk = bank("p4u", [0, 1, 2, 3])
                        for k in range(8):
                            mm(banks[bk][:], w1[:, k, f0:f0 + 128], ATT[:, k, tsl], k == 0, k == 7,
                               [("W", s1[fc // 4]), ("ATT", k, tg)], [("ps", bk)])
                        r = scr()
                        act(SCR[:, r, :], banks[bk][:], AF.Square, [], [("ps", bk), ("scr", r)])
                        dve("scalar_tensor_tensor", [("scr", r)], [("ps", bk), ("OT", fc, tg)], out=OT[:, fc, tsl],
                            in0=banks[bk][:], scalar=0.0, in1=SCR[:, r, :], op0=ALU.is_gt, op1=ALU.mult)
                w_release(2)
                last = (fg == 3 and l == nlayers - 1)
                order = [(c, tg) for tg in range(NTG) for c in range(8)] if last else [(c, tg) for c in range(8) for tg in range(NTG)]
                for (c, tg) in order:
                    tsl = slice(tg * TG, (tg + 1) * TG)
                    bk = bank("p4d", [4, 5, 6, 7])
                    for fc in range(8):
                        w2 = W[:, s2[fc // 4], :].rearrange("p (k c) -> p k c", k=4)
                        mm(banks[bk][:], w2[:, fc % 4, c * 128:(c + 1) * 128], OT[:, fc, tsl], fc == 0, fc == 7,
                           [("W", s2[fc // 4]), ("OT", fc, tg)], [("ps", bk)])
                    dve("tensor_tensor", [("XT", c, tg)], [("ps", bk), ("XT", c, tg)], out=XT[:, c, tsl],
                        in0=banks[bk][:], in1=XT[:, c, tsl], op=ALU.add)
                    if last and c == 7 and tg >= 1:
                        emit_out(tg - 1)
                if last:
                    emit_out(NTG - 1)
                w_release(2)

        def emit_out(tg):
            for t in range(tg * 4, tg * 4 + 4):
                s = t % 8
                for half in range(2):
                    bk = bank("po", [0, 1, 2, 3])
                    for cc in range(4):
                        c = half * 4 + cc
                        tr(banks[bk][:, cc * 128:(cc + 1) * 128], XT[:, c, t * 128:(t + 1) * 128],
                           [("XT", c, t // 4), ("ident",)], [("ps", bk)])
                    dst = stage(s)[:, half * 512:(half + 1) * 512]
                    if half == 0:
                        act(dst, banks[bk][:], AF.Copy, [], [("ps", bk), ("ostg", s, half)] + ATTt(s))
                    else:
                        dve("tensor_copy", [], [("ps", bk), ("ostg", s, half)] + ATTt(s), out=dst, in_=banks[bk][:])
                dma("sp", y_d[t * 128:(t + 1) * 128, :], stage(s), [("ostg", s, 0), ("ostg", s, 1)] + ATTt(s),
                    [("y", t)], key=f"y{s}")

        allxt = [("XT", c, tg) for c in range(8) for tg in range(NTG)]
        tap("XT0", XT[:], F32, [128, 8, S], allxt)
        for l in range(nlayers):
            phase1(l, 0)
            if l == 0:
                tap("QA", ATT[:, 0:4, :], BF16, [128, 4, S], [t_ for c in range(4) for t_ in ATTt(c)])
                tap("KA", ATT[:, 4:7, :], BF16, [128, 3, S], [t_ for c in (4, 5, 6) for t_ in ATTt(c)] + [("KTd", kv, h) for kv in range(2) for h in range(2)])
                tap("VA", ATT[:, 7:10, :], BF16, [128, 3, S], [("V", t) for t in range(16)])
            attn_global(l)
            if l == 0:
                tap("OA", OT[:, 0:4, :], BF16, [128, 4, S], [("OT", c, tg) for c in range(4) for tg in range(NTG)])
            phase1(l, 1)
            if l == 0:
                tap("QB", ATT[:, 0:4, :], BF16, [128, 4, S], [t_ for c in range(4) for t_ in ATTt(c)])
                tap("KB", ATT[:, 4:7, :], BF16, [128, 3, S], [t_ for c in (4, 5, 6) for t_ in ATTt(c)] + [("KTd", kv, h) for kv in range(2) for h in range(2)])
            attn_window(l)
            if l == 0:
                tap("OB", OT[:, 4:8, :], BF16, [128, 4, S], [("OT", c, tg) for c in range(4, 8) for tg in range(NTG)])
            phase3(l)
            if l == 0:
                tap("MIX", ATT[:, 0:8, :], BF16, [128, 8, S], [t_ for c in range(8) for t_ in ATTt(c)])
                tap("XT1", XT[:], F32, [128, 8, S], allxt)
            phase4(l)
            if l == 0:
                tap("XT2", XT[:], F32, [128, 8, S], allxt)

        P.add("sp", None, [("y", t) for t in range(16)] + [("tap", n_) for n_ in tapouts], [])

        sem_names = P.finalize()
        sems = {k: stack.enter_context(nc.semaphore(k)) for k in sem_names}
        with nc.Block() as block:
            @block.tensor
            def _(e):
                P.emit("pe", e, sems)

            @block.scalar
            def _(e):
                P.emit("act", e, sems)

            @block.vector
            def _(e):
                P.emit("dve", e, sems)

            @block.gpsimd
            def _(e):
                P.emit("pool", e, sems)

            @block.sync
            def _(e):
                P.emit("sp", e, sems)
    return nc


def _t5_bucket(rel):
    nb = 16
    max_exact = 8
    n = np.abs(rel)
    large = max_exact + (np.log(np.maximum(n, 1).astype(np.float32) / max_exact)
                         / math.log(128 / max_exact) * (nb - max_exact)).astype(np.int32)
    large = np.minimum(large, nb - 1)
    return np.where(rel > 0, nb, 0) + np.where(n < max_exact, n, large)


def _host_tables(inputs):
    f32 = np.float32
    cols = np.zeros((128, NL * NCL), f32)
    p = np.arange(128)
    for l in range(NL):
        b = l * NCL
        cols[:, b + 0:b + 8] = np.asarray(inputs["norm_mix"][l], f32).reshape(8, 128).T
        cols[:, b + 8:b + 16] = np.asarray(inputs["norm_mlp"][l], f32).reshape(8, 128).T
        cols[:, b + 16:b + 32] = np.asarray(inputs["b_gate"][l], f32).reshape(16, 128).T
        cols[:, b + 32] = np.asarray(inputs["qn_a"][l], f32)[p % 64]
        cols[:, b + 33] = np.asarray(inputs["kn_a"][l], f32)[p % 64]
        cols[:, b + 34] = np.asarray(inputs["qn_b"][l], f32)[p % 64]
        cols[:, b + 35] = np.asarray(inputs["kn_b"][l], f32)[p % 64]
        cols[:, b + 36:b + 44] = np.asarray(inputs["sink_b"][l], f32)[None, :]
        pm_ = np.where((p % 32) < 16, p + 16, p - 16) % 64
        cols[:, b + 44] = np.asarray(inputs["qn_a"][l], f32)[pm_]
        cols[:, b + 45] = np.asarray(inputs["kn_a"][l], f32)[pm_]
    t = np.arange(S)
    row = (t // 64).astype(f32)
    colp = (t % 64).astype(f32)
    inv_freq = (10000.0 ** (-np.arange(16, dtype=f32) / 16)).astype(f32)
    d = p % 64
    half = d // 32
    j = d % 32
    fidx = j % 16
    pos = np.where(half[:, None] == 0, row[None, :], colp[None, :]).astype(f32)
    ang = (pos * inv_freq[fidx][:, None]).astype(f32)
    C = np.cos(ang).astype(f32)
    Sg = np.sin(ang).astype(f32)
    Sp = np.where((j < 16)[:, None], -Sg, Sg).astype(f32)
    rope = np.stack([C, Sp]).astype(f32)
    s_ = np.arange(128)[:, None]
    q_ = np.arange(384)[None, :]
    rel = s_ + 128 - q_
    bucket = _t5_bucket(rel)
    rb = np.asarray(inputs["rel_bias"], f32)
    biasT = np.ascontiguousarray(np.transpose(rb[bucket], (0, 2, 1))).astype(f32)
    maskT = np.where(np.abs(rel) <= 128, 0.0, MASKVAL).astype(f32)
    ident = np.eye(128, dtype=f32)
    ones = np.ones((128, 128), f32)
    blk = np.zeros((128, 128), f32)
    blk[:64, :64] = 1.0
    blk[64:, 64:] = 1.0
    perm = np.zeros((128, 128), f32)
    m = np.arange(128)
    pm = np.where((m % 32) < 16, m + 16, m - 16)
    perm[pm, m] = 1.0
    cst = np.stack([ident, ones, blk, perm]).astype(f32)
    return cols, rope, biasT, maskT, cst


_NC_CACHE = {}


def kernel(**inputs):
    f32 = np.float32
    x = np.asarray(inputs["x"], f32)
    cols, rope, biasT, maskT, cst = _host_tables(inputs)
    shared = {
        "w_in": np.ascontiguousarray(np.asarray(inputs["w_in"], f32)),
        "w_o_a": np.ascontiguousarray(np.asarray(inputs["w_o_a"], f32)),
        "w_o_b": np.ascontiguousarray(np.asarray(inputs["w_o_b"], f32)),
        "w_out": np.ascontiguousarray(np.asarray(inputs["w_out"], f32)),
        "w_mlp1": np.ascontiguousarray(np.asarray(inputs["w_mlp1"], f32)),
        "w_mlp2": np.ascontiguousarray(np.asarray(inputs["w_mlp2"], f32)),
        "cols": cols, "rope": rope, "biasT": biasT, "maskT": maskT, "cst": cst,
    }
    if "nc" not in _NC_CACHE:
        _NC_CACHE["nc"] = build(NL)
    nc = _NC_CACHE["nc"]
    in_maps = []
    for b in range(8):
        m = dict(shared)
        m["x"] = np.ascontiguousarray(x[b])
        in_maps.append(m)
    res = run_bass_kernel_spmd(nc, in_maps, core_ids=list(range(8)))
    out = np.stack([np.asarray(r["y"], f32) for r in res.results], axis=0)
    return out
```

```python
import math
from contextlib import ExitStack

import numpy as np
import concourse.bass as bass
import concourse.mybir as mybir
from concourse.bass_utils import run_bass_kernel_spmd

F32 = mybir.dt.float32
BF16 = mybir.dt.bfloat16
ALU = mybir.AluOpType
AF = mybir.ActivationFunctionType

S = 2048
D = 1024
NL = 2
NTG = 4
TG = 512
EPS = 1e-6
NSLOT = 4
NCL = 48
MASKVAL = -30000.0


class Prog:
    def __init__(self):
        self.ops = []
        self.last_w = {}
        self.readers = {}

    def add(self, eng, fn, reads=(), writes=(), dma=None, nostate=False):
        i = len(self.ops)
        raw = set()
        other = set()
        for t in reads:
            w = self.last_w.get(t)
            if w is not None:
                raw.add(w)
        for t in writes:
            w = self.last_w.get(t)
            if w is not None:
                other.add(w)
            for r in self.readers.get(t, ()):
                other.add(r)
        deps = set()
        for j in raw | other:
            oj = self.ops[j]
            if oj["dma"] is None and dma is None and oj["eng"] == eng:
                if eng == "pe":
                    continue
            deps.add(j)
        self.ops.append(dict(eng=eng, fn=fn, deps=sorted(deps), dma=dma, signal=False))
        if nostate:
            return i
        for t in writes:
            self.last_w[t] = i
            self.readers[t] = []
        for t in reads:
            self.readers.setdefault(t, []).append(i)
        return i

    def finalize(self):
        for op in self.ops:
            for j in op["deps"]:
                self.ops[j]["signal"] = True
        cnt = {}
        for op in self.ops:
            if op["dma"] is not None:
                k = "d_" + op["dma"]
                cnt[k] = cnt.get(k, 0) + 16
                op["sem"] = k
                op["val"] = cnt[k]
            elif op["signal"]:
                k = "e_" + op["eng"]
                cnt[k] = cnt.get(k, 0) + 1
                op["sem"] = k
                op["val"] = cnt[k]
        return sorted(cnt.keys())

    def emit(self, eng_name, eng, sems):
        waited = {}
        for op in self.ops:
            if op["eng"] != eng_name:
                continue
            need = {}
            for j in op["deps"]:
                oj = self.ops[j]
                need[oj["sem"]] = max(need.get(oj["sem"], 0), oj["val"])
            for k, v in need.items():
                if waited.get(k, 0) < v:
                    eng.wait_ge(sems[k], v)
                    waited[k] = v
            if op["fn"] is None:
                continue
            ins = op["fn"](eng)
            if op["dma"] is not None:
                ins.then_inc(sems[op["sem"]], 16)
            elif op["signal"]:
                ins.then_inc(sems[op["sem"]], 1)


def build(nlayers=NL, taps=()):
    nc = bass.Bass("TRN2", target_bir_lowering=False)
    P = Prog()

    def dram(name, shape, kind="ExternalInput"):
        return nc.dram_tensor(name, list(shape), F32, kind=kind).ap()

    x_d = dram("x", [S, D])
    w_in_d = dram("w_in", [NL, D, 3584])
    w_oa_d = dram("w_o_a", [NL, 512, D])
    w_ob_d = dram("w_o_b", [NL, 512, D])
    w_out_d = dram("w_out", [NL, D, D])
    w1_d = dram("w_mlp1", [NL, D, 4096])
    w2_d = dram("w_mlp2", [NL, 4096, D])
    cols_d = dram("cols", [128, NL * NCL])
    rope_d = dram("rope", [2, 128, S])
    bias_d = dram("biasT", [128, 8, 384])
    mask_d = dram("maskT", [128, 384])
    cst_d = dram("cst", [4, 128, 128])
    y_d = dram("y", [S, D], kind="ExternalOutput")

    stack = ExitStack()
    with stack:
        def sb(name, shape, dt):
            return stack.enter_context(nc.sbuf_tensor(name, list(shape), dt))

        XT = sb("XT", [128, 8, S], F32)
        HTs = sb("HTs", [128, 2, 8, TG], BF16)
        ATT = sb("ATT", [128, 10, S], BF16)
        OT = sb("OT", [128, 8, S], BF16)
        PT = sb("PT", [128, 6, TG], BF16)
        W = sb("W", [128, NSLOT, 4096], BF16)
        SCR = sb("SCR", [128, 7, TG], F32)
        ident = sb("ident", [128, 128], F32)
        cb = sb("cb", [128, 3, 128], BF16)
        colsr = sb("colsr", [128, NL * NCL], F32)
        colsx = sb("colsx", [128, NL * 9], F32)
        PS = stack.enter_context(nc.psum_tensor("ps", [128, 8, TG], F32))
        banks = [PS[:, i, :] for i in range(8)]

        ones_bf = cb[:, 0, :]
        blk_bf = cb[:, 1, :]
        perm_bf = cb[:, 2, :]

        KT = ATT[:, 4, :]
        KTd = ATT[:, 5:7, :]
        Vaug = ATT[:, 7:10, :].rearrange("p a b -> p (a b)").rearrange("p (t k d) -> p t k d", t=16, k=2, d=192)
        OTf = OT[:, 4:8, :].rearrange("p a b -> p (a b)").bitcast(F32)
        ropeC = OTf[:, 0:S]
        ropeS = OTf[:, S:2 * S]
        HTf = HTs[:].rearrange("p a k t -> p (a k t)").bitcast(F32)
        biasm = HTf[:, 0:8 * 384].rearrange("p (h q) -> p h q", h=8)

        def stage(s):
            return ATT[:, s, :].bitcast(F32)

        def ATTt(c):
            return [("ATT", c, tg) for tg in range(NTG)]

        def mm(out, lhsT, rhs, start, stop, reads, writes, tp=None):
            def fn(e):
                if tp is None:
                    return e.matmul(out, lhsT, rhs, start=start, stop=stop)
                return e.matmul(out, lhsT, rhs, start=start, stop=stop, tile_position=tp)
            P.add("pe", fn, reads, writes)

        def tr(out, in_, reads, writes):
            P.add("pe", lambda e: e.transpose(out, in_, ident[:]), reads, writes)

        def act(out, in_, func, reads, writes, bias=None, scale=None):
            def fn(e):
                kw = {}
                if bias is not None:
                    kw["bias"] = bias
                if scale is not None:
                    kw["scale"] = scale
                return e.activation(out=out, in_=in_, func=func, **kw)
            P.add("act", fn, reads, writes)

        def dve(method, reads, writes, **kw):
            P.add("dve", lambda e: getattr(e, method)(**kw), reads, writes)

        def dma(eng, out, in_, reads, writes, key):
            P.add(eng, lambda e: e.dma_start(out=out, in_=in_), reads, writes, dma=key)

        tapouts = {}

        def tap(name, ap, dt, shape, reads):
            if name not in taps:
                return
            d = nc.dram_tensor("tap_" + name, list(shape), dt, kind="ExternalOutput").ap()
            dma("sp", d, ap, reads, [("tap", name)], key="tap_" + name)
            tapouts[name] = True

        ringctr = {}
        SCR_POOLS = {"gen": [0, 1, 2, 3, 4, 5, 6], "rs": [0, 1], "q": [2, 3, 4], "t2": [5, 6]}
        PT_POOLS = {"norm": [0, 1], "sq": [2, 3], "qb": [4, 5], "norm6": [0, 1, 2, 3, 4, 5]}

        def scr(pool="gen"):
            k = "scr_" + pool
            i = ringctr.get(k, 0)
            ringctr[k] = i + 1
            lst = SCR_POOLS[pool]
            return lst[i % len(lst)]

        def ptn(pool):
            k = "pt_" + pool
            i = ringctr.get(k, 0)
            ringctr[k] = i + 1
            lst = PT_POOLS[pool]
            return lst[i % len(lst)]

        bankctr = {}

        def bank(pool, lst):
            i = bankctr.get(pool, 0)
            bankctr[pool] = i + 1
            return lst[i % len(lst)]

        wplan = []

        def k8(cols):
            return lambda sl: sl.rearrange("p (k c) -> p k c", k=8)[:, :, 0:cols]

        def src_k8(mat, c0, cols):
            return mat[:, c0:c0 + cols].rearrange("(k p) c -> p k c", p=128)

        for l in range(nlayers):
            wplan.append([(k8(512), src_k8(w_in_d[l], 0, 512))])
            wplan.append([(k8(256), src_k8(w_in_d[l], 512, 256))])
            wplan.append([(k8(512), src_k8(w_in_d[l], 768, 512))])
            wplan.append([(k8(256), src_k8(w_in_d[l], 1280, 256))])
            for cg in range(2):
                wplan.append([(k8(512), src_k8(w_in_d[l], 1536 + cg * 512, 512))])
                wplan.append([(k8(512), src_k8(w_in_d[l], 2560 + cg * 512, 512))])
                wplan.append([
                    (lambda sl: sl[:, 0:2048].rearrange("p (k c) -> p k c", k=4),
                     w_oa_d[l][:, cg * 512:(cg + 1) * 512].rearrange("(k p) c -> p k c", p=128)),
                    (lambda sl: sl[:, 2048:4096].rearrange("p (k c) -> p k c", k=4),
                     w_ob_d[l][:, cg * 512:(cg + 1) * 512].rearrange("(k p) c -> p k c", p=128)),
                ])
            for cg in range(2):
                wplan.append([(k8(512), src_k8(w_out_d[l], cg * 512, 512))])
            for fg in range(4):
                wplan.append([(k8(512), src_k8(w1_d[l], fg * 1024, 512))])
                wplan.append([(k8(512), src_k8(w1_d[l], fg * 1024 + 512, 512))])
                for hf in range(2):
                    f0 = fg * 1024 + hf * 512
                    wplan.append([(lambda sl: sl.rearrange("p (k c) -> p k c", k=4),
                                   w2_d[l][f0:f0 + 512, :].rearrange("(k p) c -> p k c", p=128))])

        wst = {"next_load": 0, "released": 0, "next_use": 0}

        def w_pump():
            while wst["next_load"] < len(wplan) and wst["next_load"] < wst["released"] + NSLOT:
                n = wst["next_load"]
                s = n % NSLOT
                for (dstf, src) in wplan[n]:
                    dma("pool", dstf(W[:, s, :]), src, [], [("W", s)], key=f"w{s}")
                wst["next_load"] += 1

        def w_use():
            n = wst["next_use"]
            wst["next_use"] += 1
            assert n < wst["next_load"], "weight not loaded (ring too small for this group)"
            return n % NSLOT

        def w_release(k=1):
            wst["released"] += k
            w_pump()

        dma("sp", ident[:], cst_d[0], [], [("ident",)], key="ci")
        for i in range(3):
            dma("pool", cb[:, i, :], cst_d[1 + i], [], [("cb", i)], key=f"cb{i}")
        dma("sp", colsr[:], cols_d, [], [("colsr",)], key="cc")
        w_pump()

        def col(l, j):
            return colsr[:, l * NCL + j:l * NCL + j + 1]
        colsx2 = sb("colsx2", [128, NL], F32)
        colsx3 = sb("colsx3", [128, NL], F32)
        epsc = sb("epsc", [128, 2], F32)
        dve("memset", [], [("epsc",)], ap=epsc[:, 0:1], constant=EPS)
        dve("memset", [], [("epsc",)], ap=epsc[:, 1:2], constant=64.0 * EPS)
        for l in range(nlayers):
            dve("tensor_scalar", [("colsr",)], [("colsx",)], out=colsx[:, l * 9:l * 9 + 1], in0=col(l, 33),
                scalar1=8.0, scalar2=None, op0=ALU.mult)
            dve("tensor_scalar", [("colsr",)], [("colsx",)], out=colsx2[:, l:l + 1], in0=col(l, 35),
                scalar1=8.0, scalar2=None, op0=ALU.mult)
            dve("tensor_scalar", [("colsr",)], [("colsx",)], out=colsx3[:, l:l + 1], in0=col(l, 45),
                scalar1=8.0, scalar2=None, op0=ALU.mult)
            act(colsx[:, l * 9 + 1:l * 9 + 9], colsr[:, l * NCL + 36:l * NCL + 44], AF.Exp, [("colsr",)], [("colsx",)])

        for tg in range(NTG):
            for t in range(4):
                tile = tg * 4 + t
                s = tile % 8
                dma("sp", stage(s), x_d[tile * 128:(tile + 1) * 128, :], [], ATTt(s), key=f"x{s}")
            for c in range(8):
                bk = bank("p0", [0, 1, 2, 3])
                for t in range(4):
                    s = (tg * 4 + t) % 8
                    tr(banks[bk][:, t * 128:(t + 1) * 128], stage(s)[:, c * 128:(c + 1) * 128],
                       ATTt(s) + [("ident",)], [("ps", bk)])
                dst = XT[:, c, tg * TG:(tg + 1) * TG]
                if c % 2 == 0:
                    act(dst, banks[bk][:], AF.Copy, [], [("ps", bk), ("XT", c, tg)])
                else:
                    dve("tensor_copy", [], [("ps", bk), ("XT", c, tg)], out=dst, in_=banks[bk][:])

        def norm_tg(l, tg, gbase, dst_ap, dst_tile, pool="norm", split=None):
            tsl = slice(tg * TG, (tg + 1) * TG)
            bk = bank("nss", [6, 7])
            sq_idx = {}

            def square(c):
                i = ptn(pool)
                sq_idx[c] = i
                act(PT[:, i, :], XT[:, c, tsl], AF.Square, [("XT", c, tg)], [("PT", i)])

            def rest():
                for c in range(8):
                    if c not in sq_idx:
                        square(c)
                    i = sq_idx[c]
                    mm(banks[bk][:], ones_bf, PT[:, i, :], c == 0, c == 7, [("PT", i), ("cb", 0)], [("ps", bk)])
                r = scr("rs")
                act(SCR[:, r, :], banks[bk][:], AF.Ln, [("epsc",)], [("scr", r), ("ps", bk)], bias=epsc[:, 0:1], scale=1.0 / D)
                act(SCR[:, r, :], SCR[:, r, :], AF.Exp, [("scr", r)], [("scr", r)], scale=-0.5)
                for c in range(8):
                    dve("scalar_tensor_tensor", [("XT", c, tg), ("scr", r), ("colsr",)], [dst_tile(c)],
                        out=dst_ap(c), in0=XT[:, c, tsl], scalar=col(l, gbase + c), in1=SCR[:, r, :],
                        op0=ALU.mult, op1=ALU.mult)

            if split is None:
                rest()
                return None
            for c in range(split):
                square(c)
            return rest

        def ht_dst(hb):
            return (lambda c: HTs[:, hb, c, :]), (lambda c: ("HT", hb, c))

        def qk_unit(l, tg, hb, slot, wc0, gain_ap, rope, dst_ap, dst_tiles, pend, gperm_ap=None):
            tsl = slice(tg * TG, (tg + 1) * TG)
            wv = W[:, slot, :].rearrange("p (k c) -> p k c", k=8)
            bk = bank("main", [0, 1, 2])
            for k in range(8):
                mm(banks[bk][:], wv[:, k, wc0:wc0 + 128], HTs[:, hb, k, :], k == 0, k == 7,
                   [("W", slot), ("HT", hb, k)], [("ps", bk)])
            i = ptn("sq")
            act(PT[:, i, :], banks[bk][:], AF.Square, [], [("ps", bk), ("PT", i)])
            if rope:
                j = ptn("qb")
                act(PT[:, j, :], banks[bk][:], AF.Copy, [], [("ps", bk), ("PT", j)])
                t1 = scr("q")
                dve("scalar_tensor_tensor", [("rope",), ("colsr",), ("colsx",)], [("ps", bk), ("scr", t1)],
                    out=SCR[:, t1, :], in0=banks[bk][:], scalar=gain_ap, in1=ropeC[:, tsl], op0=ALU.mult, op1=ALU.mult)

            def stage2():
                sb_ = bank("hss", [3])
                mm(banks[sb_][:], blk_bf, PT[:, i, :], True, True, [("PT", i), ("cb", 1)], [("ps", sb_)])
                if rope:
                    pb = bank("perm", [4])
                    mm(banks[pb][:], perm_bf, PT[:, j, :], True, True, [("PT", j), ("cb", 2)], [("ps", pb)])
                r = scr("rs")
                act(SCR[:, r, :], banks[sb_][:], AF.Ln, [("epsc",)], [("scr", r), ("ps", sb_)], bias=epsc[:, 1:2], scale=1.0)
                act(SCR[:, r, :], SCR[:, r, :], AF.Exp, [("scr", r)], [("scr", r)], scale=-0.5)
                if not rope:
                    dve("scalar_tensor_tensor", [("scr", r), ("colsr",), ("colsx",)], [("ps", bk)] + dst_tiles,
                        out=dst_ap, in0=banks[bk][:], scalar=gain_ap, in1=SCR[:, r, :], op0=ALU.mult, op1=ALU.mult)
                    return None
                t2 = scr("t2")
                dve("scalar_tensor_tensor", [("rope",), ("colsr",), ("colsx",)], [("ps", pb), ("scr", t2)],
                    out=SCR[:, t2, :], in0=banks[pb][:], scalar=gperm_ap, in1=ropeS[:, tsl], op0=ALU.mult, op1=ALU.mult)
                dve("tensor_tensor", [("scr", t1), ("scr", t2)], [("scr", t1)], out=SCR[:, t1, :], in0=SCR[:, t1, :],
                    in1=SCR[:, t2, :], op=ALU.add)
                dve("tensor_tensor", [("scr", t1), ("scr", r)], dst_tiles, out=dst_ap, in0=SCR[:, t1, :],
                    in1=SCR[:, r, :], op=ALU.mult)
                return None
            pend.append(stage2)

        def run_pend(pend, keep):
            n = len(pend) - keep
            if n <= 0:
                return
            todo = pend[:n]
            del pend[:n]
            newp = []
            for f in todo:
                nxt = f()
                if nxt is not None:
                    newp.append(nxt)
            pend[:0] = newp

        def phase1(l, mixer):
            rope = (mixer == 0)
            if rope:
                dma("sp", ropeC, rope_d[0], [], [("rope",)] + [("OT", c, tg) for c in (4, 5) for tg in range(NTG)], key="rp0")
                dma("sp", ropeS, rope_d[1], [], [("rope",)] + [("OT", c, tg) for c in (6, 7) for tg in range(NTG)], key="rp1")
            sq_ = w_use()
            skv = w_use()
            qg = col(l, 32 if mixer == 0 else 34)
            kg = colsx[:, l * 9:l * 9 + 1] if mixer == 0 else colsx2[:, l:l + 1]
            qgp = col(l, 44)
            kgp = colsx3[:, l:l + 1]
            dve("memset", [], [("V", t) for t in range(16)] + [x_ for c in (7, 8, 9) for x_ in ATTt(c)],
                ap=Vaug[:, :, :, 64:128], constant=1.0)
            pend = []
            da0, dt0 = ht_dst(0)
            norm_tg(l, 0, 0, da0, dt0)
            for tg in range(NTG):
                hb = tg % 2
                if tg + 1 < NTG:
                    da, dt_ = ht_dst((tg + 1) % 2)
                    norm_tg(l, tg + 1, 0, da, dt_)
                tsl = slice(tg * TG, (tg + 1) * TG)
                for c in range(4):
                    qk_unit(l, tg, hb, sq_, c * 128, qg, rope, ATT[:, c, tsl], [("ATT", c, tg)], pend, gperm_ap=qgp)
                    run_pend(pend, 1)
                qk_unit(l, tg, hb, skv, 0, kg, rope, KT[:, tsl], [("ATT", 4, tg)], pend, gperm_ap=kgp)
                run_pend(pend, 1)
                wv = W[:, skv, :].rearrange("p (k c) -> p k c", k=8)
                vb = bank("vb", [5])
                for t in range(4):
                    for k in range(8):
                        mm(banks[vb][:, t * 128:(t + 1) * 128], HTs[:, hb, k, t * 128:(t + 1) * 128], wv[:, k, 128:256],
                           k == 0, k == 7, [("W", skv), ("HT", hb, k)], [("ps", vb)])
                src = banks[vb][:].rearrange("p (t k d) -> p t k d", t=4, k=2, d=64)
                vt = [("V", tg * 4 + t) for t in range(4)]
                act(Vaug[:, tg * 4:tg * 4 + 4, :, 0:64], src, AF.Copy, [], [("ps", vb)] + vt)
                act(Vaug[:, tg * 4:tg * 4 + 4, :, 128:192], src, AF.Copy, [], [("ps", vb)] + vt)
            while pend:
                run_pend(pend, 0)
            w_release(2)
            allk = ATTt(4)
            for kv in range(2):
                for half in range(2):
                    dma("sp", KTd[half * 64:(half + 1) * 64, kv, :], KT[kv * 64:(kv + 1) * 64, :],
                        allk, [("KTd", kv, half)] + ATTt(5 + kv), key=f"kd{kv}{half}")

        def attn_global(l):
            steps = []
            u = 0
            for kv in range(2):
                for tg in range(NTG):
                    for pr in range(2):
                        for sbk in range(16):
                            steps.append((u, kv, tg, pr, sbk))
                        u += 1
            pend = []

            def do_pv(st, s):
                (u, kv, tg, pr, sbk) = st
                c = kv * 2 + pr
                tsl = slice(tg * TG, (tg + 1) * TG)
                for hh in range(2):
                    ab = 4 + 2 * (u % 2) + hh
                    pt = (s % 3) * 2 + hh
                    lhsT = Vaug[:, sbk, kv, 0:128] if hh == 0 else Vaug[:, sbk, kv, 64:192]
                    mm(banks[ab][:], lhsT, PT[:, pt, :], sbk == 0, sbk == 15, [("PT", pt), ("V", sbk)], [("ps", ab)])
                    if sbk == 15:
                        orow = slice(hh * 64, hh * 64 + 64)
                        drow = slice(64 - hh * 64, 128 - hh * 64)
                        r = scr()
                        dve("reciprocal", [], [("scr", r), ("ps", ab)], out=SCR[orow, r, :], in_=banks[ab][drow, :])
                        dve("tensor_tensor", [("scr", r)], [("ps", ab), ("OT", c, tg)], out=OT[orow, c, tsl],
                            in0=banks[ab][orow, :], in1=SCR[orow, r, :], op=ALU.mult)

            for s, st in enumerate(steps):
                (u, kv, tg, pr, sbk) = st
                c = kv * 2 + pr
                for hh in range(2):
                    lt = (s % 2) * 2 + hh
                    rows = slice(hh * 64, hh * 64 + 64)
                    mm(banks[lt][:], KTd[rows, kv, sbk * 128:(sbk + 1) * 128], ATT[rows, c, tg * TG:(tg + 1) * TG],
                       True, True, [("ATT", c, tg), ("KTd", kv, hh)], [("ps", lt)], tp=(hh * 64, 0))
                for hh in range(2):
                    lt = (s % 2) * 2 + hh
                    pt = (s % 3) * 2 + hh
                    act(PT[:, pt, :], banks[lt][:], AF.Exp, [], [("ps", lt), ("PT", pt)])
                pend.append((st, s))
                if len(pend) > 1:
                    do_pv(*pend.pop(0))
            while pend:
                do_pv(*pend.pop(0))

        def attn_window(l):
            allht = [("HT", hb, c) for hb in range(2) for c in range(8)]
            dma("sp", biasm, bias_d, [], allht + [("biasm",)], key="bm0")
            mi = scr()
            dma("sp", SCR[:, mi, 0:384], mask_d, [], [("scr", mi)], key="bm1")
            for h in range(8):
                dve("tensor_tensor", [("scr", mi), ("biasm",)], [("biasm", h)] + ([("biasm",)] if h == 7 else []),
                    out=biasm[:, h, :], in0=biasm[:, h, :], in1=SCR[:, mi, 0:384], op=ALU.add)
            PTB = PT[:].rearrange("p a b -> p (a b)")[:, 0:8 * 384].rearrange("p (r q) -> p r q", r=8)
            allpt = [("PT", i) for i in range(6)]
            allptb = [("PTB", i) for i in range(8)]
            P.add("act", None, [], allpt, nostate=True)
            normq = []
            bstep = 0

            normq2 = []

            def emit_norm(ab, tg, h, c, orow, drow):
                r1 = scr()
                act(SCR[drow, r1, :], banks[ab][drow, :], AF.Ln, [("colsx",)], [("scr", r1), ("ps", ab)],
                    bias=colsx[drow, l * 9 + 1 + h:l * 9 + 2 + h], scale=1.0)
                r2 = scr()
                act(SCR[orow, r2, :], SCR[drow, r1, :], AF.Exp, [("scr", r1)], [("scr", r2)], scale=-1.0)
                normq2.append((bstep + 2, ab, tg, c, orow, r2))

            def emit_norm_dve(ab, tg, c, orow, r2):
                tsl = slice(tg * TG, (tg + 1) * TG)
                dve("tensor_tensor", [("scr", r2)], [("ps", ab), ("OT", c, tg)], out=OT[orow, c, tsl],
                    in0=banks[ab][orow, :], in1=SCR[orow, r2, :], op=ALU.mult)

            R = 4
            DEF = 3
            for hp in range(4):
                c = 4 + hp
                kv = hp // 2
                for j in range(16 + DEF):
                    i = j - DEF
                    if i >= 0:
                        contrib = [jj for jj in (i - 1, i, i + 1) if 0 <= jj < 16]
                        for hh in range(2):
                            h = 2 * hp + hh
                            ab = 4 + ((i // 4) % 2) * 2 + hh
                            vsl = slice(0, 128) if hh == 0 else slice(64, 192)
                            for n_, jj in enumerate(contrib):
                                b = i - jj + 1
                                sl = hh * R + jj % R
                                mm(banks[ab][:, (i % 4) * 128:(i % 4 + 1) * 128], Vaug[:, jj, kv, vsl],
                                   PTB[:, sl, b * 128:(b + 1) * 128], n_ == 0, n_ == len(contrib) - 1,
                                   [("PTB", sl), ("V", jj)], [("ps", ab)])
                            if i % 4 == 3:
                                orow = slice(hh * 64, hh * 64 + 64)
                                drow = slice(64 - hh * 64, 128 - hh * 64)
                                normq.append((bstep + 2, ab, i // 4, h, c, orow, drow))
                    if j < 16:
                        lo = max(j - 1, 0)
                        hi = min(j + 1, 15)
                        w = (hi - lo + 1) * 128
                        off = (lo - (j - 1)) * 128
                        tgs = sorted(set((b * 128) // TG for b in range(lo, hi + 1)))
                        lts = []
                        for hh in range(2):
                            rows = slice(hh * 64, hh * 64 + 64)
                            lt = (bstep % 2) * 2 + hh
                            lts.append(lt)
                            mm(banks[lt][:, 0:w], KTd[rows, kv, j * 128:(j + 1) * 128], ATT[rows, hp, lo * 128:(hi + 1) * 128],
                               True, True, [("ATT", hp, t_) for t_ in tgs] + [("KTd", kv, hh)], [("ps", lt)], tp=(hh * 64, 0))
                        for hh in range(2):
                            h = 2 * hp + hh
                            lt = lts[hh]
                            tm = scr()
                            dve("tensor_tensor", [("biasm", h)], [("scr", tm), ("ps", lt)], out=SCR[:, tm, 0:w],
                                in0=banks[lt][:, 0:w], in1=biasm[:, h, off:off + w], op=ALU.add)
                            sl = hh * R + j % R
                            act(PTB[:, sl, off:off + w], SCR[:, tm, 0:w], AF.Exp, [("scr", tm)], [("PTB", sl)])
                    bstep += 1
                    while normq and normq[0][0] <= bstep:
                        emit_norm(*normq.pop(0)[1:])
                    while normq2 and normq2[0][0] <= bstep:
                        emit_norm_dve(*normq2.pop(0)[1:])
            while normq:
                emit_norm(*normq.pop(0)[1:])
            while normq2:
                emit_norm_dve(*normq2.pop(0)[1:])
            P.add("act", None, [], allptb, nostate=True)


        def phase3(l):
            for cg in range(2):
                sga = w_use()
                sgb = w_use()
                swo = w_use()
                wga = W[:, sga, :].rearrange("p (k c) -> p k c", k=8)
                wgb = W[:, sgb, :].rearrange("p (k c) -> p k c", k=8)
                woa = W[:, swo, 0:2048].rearrange("p (k c) -> p k c", k=4)
                wob = W[:, swo, 2048:4096].rearrange("p (k c) -> p k c", k=4)
                if cg == 0:
                    da0, dt0 = ht_dst(0)
                    norm_tg(l, 0, 0, da0, dt0)
                for tg in range(NTG):
                    hb = tg % 2
                    late = None
                    if not (cg == 1 and tg == NTG - 1):
                        ntg = (tg + 1) % NTG
                        da, dt_ = ht_dst(ntg % 2)
                        late = norm_tg(l, ntg, 0, da, dt_, pool="norm6", split=6)
                    tsl = slice(tg * TG, (tg + 1) * TG)
                    for cc in range(4):
                        if cc == 2 and late is not None:
                            late()
                        c = cg * 4 + cc
                        bga = bank("p3a", [0, 1])
                        bgb = bank("p3b", [2, 3])
                        bya = bank("p3c", [4])
                        byb = bank("p3d", [5])
                        for k in range(8):
                            mm(banks[bga][:], wga[:, k, cc * 128:(cc + 1) * 128], HTs[:, hb, k, :], k == 0, k == 7,
                               [("W", sga), ("HT", hb, k)], [("ps", bga)])
                        for k in range(8):
                            mm(banks[bgb][:], wgb[:, k, cc * 128:(cc + 1) * 128], HTs[:, hb, k, :], k == 0, k == 7,
                               [("W", sgb), ("HT", hb, k)], [("ps", bgb)])
                        for k in range(4):
                            mm(banks[bya][:], woa[:, k, cc * 128:(cc + 1) * 128], OT[:, k, tsl], k == 0, k == 3,
                               [("W", swo), ("OT", k, tg)], [("ps", bya)])
                        for k in range(4):
                            mm(banks[byb][:], wob[:, k, cc * 128:(cc + 1) * 128], OT[:, 4 + k, tsl], k == 0, k == 3,
                               [("W", swo), ("OT", 4 + k, tg)], [("ps", byb)])
                        ra = scr()
                        act(SCR[:, ra, :], banks[bga][:], AF.Sigmoid, [("colsr",)], [("ps", bga), ("scr", ra)],
                            bias=col(l, 16 + c))
                        rb = scr()
                        act(SCR[:, rb, :], banks[bgb][:], AF.Sigmoid, [("colsr",)], [("ps", bgb), ("scr", rb)],
                            bias=col(l, 24 + c))
                        dve("tensor_tensor", [("scr", ra)], [("scr", ra), ("ps", bya)], out=SCR[:, ra, :],
                            in0=banks[bya][:], in1=SCR[:, ra, :], op=ALU.mult)
                        dve("tensor_tensor", [("scr", rb)], [("scr", rb), ("ps", byb)], out=SCR[:, rb, :],
                            in0=banks[byb][:], in1=SCR[:, rb, :], op=ALU.mult)
                        dve("tensor_tensor", [("scr", ra), ("scr", rb)], [("ATT", c, tg)], out=ATT[:, c, tsl],
                            in0=SCR[:, ra, :], in1=SCR[:, rb, :], op=ALU.add)
                w_release(3)
            for cg in range(2):
                so = w_use()
                wo = W[:, so, :].rearrange("p (k c) -> p k c", k=8)
                for cc in range(4):
                    c = cg * 4 + cc
                    for tg in range(NTG):
                        tsl = slice(tg * TG, (tg + 1) * TG)
                        bk = bank("p3o", [0, 1, 2, 3])
                        for k in range(8):
                            mm(banks[bk][:], wo[:, k, cc * 128:(cc + 1) * 128], ATT[:, k, tsl], k == 0, k == 7,
                               [("W", so), ("ATT", k, tg)], [("ps", bk)])
                        dve("tensor_tensor", [("XT", c, tg)], [("ps", bk), ("XT", c, tg)], out=XT[:, c, tsl],
                            in0=banks[bk][:], in1=XT[:, c, tsl], op=ALU.add)
                w_release(1)

        def phase4(l):
            for tg in range(NTG):
                tsl = slice(tg * TG, (tg + 1) * TG)
                norm_tg(l, tg, 8, (lambda c, tsl=tsl: ATT[:, c, tsl]), (lambda c, tg=tg: ("ATT", c, tg)))
            for fg in range(4):
                s1 = [w_use(), w_use()]
                s2 = [w_use(), w_use()]
                for fc in range(8):
                    w1 = W[:, s1[fc // 4], :].rearrange("p (k c) -> p k c", k=8)
                    f0 = (fc % 4) * 128
                    for tg in range(NTG):
                        tsl = slice(tg * TG, (tg + 1) * TG)
                        bk = bank("p4u", [0, 1, 2, 3])
                        for k in range(8):
                            mm(banks[bk][:], w1[:, k, f0:f0 + 128], ATT[:, k, tsl], k == 0, k == 7,
                               [("W", s1[fc // 4]), ("ATT", k, tg)], [("ps", bk)])
                        r = scr()
                        act(SCR[:, r, :], banks[bk][:], AF.Square, [], [("ps", bk), ("scr", r)])
                        dve("scalar_tensor_tensor", [("scr", r)], [("ps", bk), ("OT", fc, tg)], out=OT[:, fc, tsl],
                            in0=banks[bk][:], scalar=0.0, in1=SCR[:, r, :], op0=ALU.is_gt, op1=ALU.mult)
                w_release(2)
                last = (fg == 3 and l == nlayers - 1)
                order = [(c, tg) for tg in range(NTG) for c in range(8)] if last else [(c, tg) for c in range(8) for tg in range(NTG)]
                for (c, tg) in order:
                    tsl = slice(tg * TG, (tg + 1) * TG)
                    bk = bank("p4d", [4, 5, 6, 7])
                    for fc in range(8):
                        w2 = W[:, s2[fc // 4], :].rearrange("p (k c) -> p k c", k=4)
                        mm(banks[bk][:], w2[:, fc % 4, c * 128:(c + 1) * 128], OT[:, fc, tsl], fc == 0, fc == 7,
                           [("W", s2[fc // 4]), ("OT", fc, tg)], [("ps", bk)])
                    dve("tensor_tensor", [("XT", c, tg)], [("ps", bk), ("XT", c, tg)], out=XT[:, c, tsl],
                        in0=banks[bk][:], in1=XT[:, c, tsl], op=ALU.add)
                    if last and c == 7 and tg >= 1:
                        emit_out(tg - 1)
                if last:
                    emit_out(NTG - 1)
                w_release(2)

        def emit_out(tg):
            for t in range(tg * 4, tg * 4 + 4):
                s = t % 8
                for half in range(2):
                    bk = bank("po", [0, 1, 2, 3])
                    for cc in range(4):
                        c = half * 4 + cc
                        tr(banks[bk][:, cc * 128:(cc + 1) * 128], XT[:, c, t * 128:(t + 1) * 128],
                           [("XT", c, t // 4), ("ident",)], [("ps", bk)])
                    dst = stage(s)[:, half * 512:(half + 1) * 512]
                    if half == 0:
                        act(dst, banks[bk][:], AF.Copy, [], [("ps", bk), ("ostg", s, half)] + ATTt(s))
                    else:
                        dve("tensor_copy", [], [("ps", bk), ("ostg", s, half)] + ATTt(s), out=dst, in_=banks[bk][:])
                dma("sp", y_d[t * 128:(t + 1) * 128, :], stage(s), [("ostg", s, 0), ("ostg", s, 1)] + ATTt(s),
                    [("y", t)], key=f"y{s}")

        allxt = [("XT", c, tg) for c in range(8) for tg in range(NTG)]
        tap("XT0", XT[:], F32, [128, 8, S], allxt)
        for l in range(nlayers):
            phase1(l, 0)
            if l == 0:
                tap("QA", ATT[:, 0:4, :], BF16, [128, 4, S], [t_ for c in range(4) for t_ in ATTt(c)])
                tap("KA", ATT[:, 4:7, :], BF16, [128, 3, S], [t_ for c in (4, 5, 6) for t_ in ATTt(c)] + [("KTd", kv, h) for kv in range(2) for h in range(2)])
                tap("VA", ATT[:, 7:10, :], BF16, [128, 3, S], [("V", t) for t in range(16)])
            attn_global(l)
            if l == 0:
                tap("OA", OT[:, 0:4, :], BF16, [128, 4, S], [("OT", c, tg) for c in range(4) for tg in range(NTG)])
            phase1(l, 1)
            if l == 0:
                tap("QB", ATT[:, 0:4, :], BF16, [128, 4, S], [t_ for c in range(4) for t_ in ATTt(c)])
                tap("KB", ATT[:, 4:7, :], BF16, [128, 3, S], [t_ for c in (4, 5, 6) for t_ in ATTt(c)] + [("KTd", kv, h) for kv in range(2) for h in range(2)])
            attn_window(l)
            if l == 0:
                tap("OB", OT[:, 4:8, :], BF16, [128, 4, S], [("OT", c, tg) for c in range(4, 8) for tg in range(NTG)])
            phase3(l)
            if l == 0:
                tap("MIX", ATT[:, 0:8, :], BF16, [128, 8, S], [t_ for c in range(8) for t_ in ATTt(c)])
                tap("XT1", XT[:], F32, [128, 8, S], allxt)
            phase4(l)
            if l == 0:
                tap("XT2", XT[:], F32, [128, 8, S], allxt)

        P.add("sp", None, [("y", t) for t in range(16)] + [("tap", n_) for n_ in tapouts], [])

        sem_names = P.finalize()
        sems = {k: stack.enter_context(nc.semaphore(k)) for k in sem_names}
        with nc.Block() as block:
            @block.tensor
            def _(e):
                P.emit("pe", e, sems)

            @block.scalar
            def _(e):
                P.emit("act", e, sems)

            @block.vector
            def _(e):
                P.emit("dve", e, sems)

            @block.gpsimd
            def _(e):
                P.emit("pool", e, sems)

            @block.sync
            def _(e):
                P.emit("sp", e, sems)
    return nc


def _t5_bucket(rel):
    nb = 16
    max_exact = 8
    n = np.abs(rel)
    large = max_exact + (np.log(np.maximum(n, 1).astype(np.float32) / max_exact)
                         / math.log(128 / max_exact) * (nb - max_exact)).astype(np.int32)
    large = np.minimum(large, nb - 1)
    return np.where(rel > 0, nb, 0) + np.where(n < max_exact, n, large)


def _host_tables(inputs):
    f32 = np.float32
    cols = np.zeros((128, NL * NCL), f32)
    p = np.arange(128)
    for l in range(NL):
        b = l * NCL
        cols[:, b + 0:b + 8] = np.asarray(inputs["norm_mix"][l], f32).reshape(8, 128).T
        cols[:, b + 8:b + 16] = np.asarray(inputs["norm_mlp"][l], f32).reshape(8, 128).T
        cols[:, b + 16:b + 32] = np.asarray(inputs["b_gate"][l], f32).reshape(16, 128).T
        cols[:, b + 32] = np.asarray(inputs["qn_a"][l], f32)[p % 64]
        cols[:, b + 33] = np.asarray(inputs["kn_a"][l], f32)[p % 64]
        cols[:, b + 34] = np.asarray(inputs["qn_b"][l], f32)[p % 64]
        cols[:, b + 35] = np.asarray(inputs["kn_b"][l], f32)[p % 64]
        cols[:, b + 36:b + 44] = np.asarray(inputs["sink_b"][l], f32)[None, :]
        pm_ = np.where((p % 32) < 16, p + 16, p - 16) % 64
        cols[:, b + 44] = np.asarray(inputs["qn_a"][l], f32)[pm_]
        cols[:, b + 45] = np.asarray(inputs["kn_a"][l], f32)[pm_]
    t = np.arange(S)
    row = (t // 64).astype(np.float64)
    colp = (t % 64).astype(np.float64)
    inv_freq = 10000.0 ** (-np.arange(16, dtype=np.float64) / 16)
    d = p % 64
    half = d // 32
    j = d % 32
    fidx = j % 16
    pos = np.where(half[:, None] == 0, row[None, :], colp[None, :])
    ang = pos * inv_freq[fidx][:, None]
    C = np.cos(ang)
    Sg = np.sin(ang)
    Sp = np.where((j < 16)[:, None], -Sg, Sg)
    rope = np.stack([C, Sp]).astype(f32)
    s_ = np.arange(128)[:, None]
    q_ = np.arange(384)[None, :]
    rel = s_ + 128 - q_
    bucket = _t5_bucket(rel)
    rb = np.asarray(inputs["rel_bias"], f32)
    biasT = np.ascontiguousarray(np.transpose(rb[bucket], (0, 2, 1))).astype(f32)
    maskT = np.where(np.abs(rel) <= 128, 0.0, MASKVAL).astype(f32)
    ident = np.eye(128, dtype=f32)
    ones = np.ones((128, 128), f32)
    blk = np.zeros((128, 128), f32)
    blk[:64, :64] = 1.0
    blk[64:, 64:] = 1.0
    perm = np.zeros((128, 128), f32)
    m = np.arange(128)
    pm = np.where((m % 32) < 16, m + 16, m - 16)
    perm[pm, m] = 1.0
    cst = np.stack([ident, ones, blk, perm]).astype(f32)
    return cols, rope, biasT, maskT, cst


_NC_CACHE = {}


def kernel(**inputs):
    f32 = np.float32
    x = np.asarray(inputs["x"], f32)
    cols, rope, biasT, maskT, cst = _host_tables(inputs)
    shared = {
        "w_in": np.ascontiguousarray(np.asarray(inputs["w_in"], f32)),
        "w_o_a": np.ascontiguousarray(np.asarray(inputs["w_o_a"], f32)),
        "w_o_b": np.ascontiguousarray(np.asarray(inputs["w_o_b"], f32)),
        "w_out": np.ascontiguousarray(np.asarray(inputs["w_out"], f32)),
        "w_mlp1": np.ascontiguousarray(np.asarray(inputs["w_mlp1"], f32)),
        "w_mlp2": np.ascontiguousarray(np.asarray(inputs["w_mlp2"], f32)),
        "cols": cols, "rope": rope, "biasT": biasT, "maskT": maskT, "cst": cst,
    }
    if "nc" not in _NC_CACHE:
        _NC_CACHE["nc"] = build(NL)
    nc = _NC_CACHE["nc"]
    in_maps = []
    for b in range(8):
        m = dict(shared)
        m["x"] = np.ascontiguousarray(x[b])
        in_maps.append(m)
    res = run_bass_kernel_spmd(nc, in_maps, core_ids=list(range(8)))
    out = np.stack([np.asarray(r["y"], f32) for r in res.results], axis=0)
    return out
```
